# Optimizing a Trainium2 kernel written in Bass

```python
import math
import jax
import jax.numpy as jnp
from jax import lax
import numpy as np

D_MODEL = 1024
BATCH = 2
SEQ = 8192
DEPTH = 1

N_META = 16
Q_BLOCK = 128
ROPE_THETA = 500000.0
NORM_EPS = 1e-6
DA_HEADS = 4
DA_QK_DIM = 64
DA_V_DIM = 2 * DA_QK_DIM
DA_WIDTH = DA_HEADS * DA_V_DIM
DA_QK_WIDTH = DA_HEADS * 2 * DA_QK_DIM
ROPE_DIM = DA_QK_DIM // 4
RW_HEAD = 64
RW_WIDTH = D_MODEL - DA_WIDTH
RW_HEADS = RW_WIDTH // RW_HEAD
D_DECAY_LORA = 64
D_AAA_LORA = 64
D_GATE_LORA = 160
GN_EPS = 64e-5
MIX_WIDTH = DA_WIDTH + RW_WIDTH
DA_IN = 2 * DA_QK_WIDTH + DA_WIDTH
RW_IN = 3 * RW_WIDTH + D_DECAY_LORA + D_AAA_LORA + D_GATE_LORA
N_IN = DA_IN + RW_IN
D_FF = 2816

kernel_name = "hymba_diffattn_rwkv7_macaron"


def rms_norm(x, g, eps=NORM_EPS):
    xf = x.astype(jnp.float32)
    y = xf * lax.rsqrt(jnp.mean(xf * xf, axis=-1, keepdims=True) + eps)
    return (y * g.astype(jnp.float32)).astype(x.dtype)


def swiglu(x, w_gate, w_up, w_down):
    return (jax.nn.silu(x @ w_gate) * (x @ w_up)) @ w_down


def rope_tables(length):
    pos = jnp.arange(length, dtype=jnp.float32)
    inv_freq = ROPE_THETA ** (-jnp.arange(0, ROPE_DIM, 2, dtype=jnp.float32) / ROPE_DIM)
    ang = pos[:, None] * inv_freq[None, :]
    return jnp.cos(ang), jnp.sin(ang)


def apply_partial_rope(x, cos, sin):
    half = ROPE_DIM // 2
    c = cos[None, :, None, None, :].astype(x.dtype)
    s = sin[None, :, None, None, :].astype(x.dtype)
    x1, x2, rest = x[..., :half], x[..., half:ROPE_DIM], x[..., ROPE_DIM:]
    return jnp.concatenate([x1 * c - x2 * s, x2 * c + x1 * s, rest], axis=-1)


def token_shift(u, mix):
    prev = jnp.pad(u, ((0, 0), (1, 0), (0, 0)))[:, :-1]
    return u + (prev - u) * mix


def diff_attention(q, k, v, q_norm, k_norm, lq1, lk1, lq2, lk2, subln, cos, sin, lam_init):
    B, L, _ = q.shape
    q = q.reshape(B, L, DA_HEADS, 2, DA_QK_DIM)
    k = k.reshape(B, L, DA_HEADS, 2, DA_QK_DIM)
    v = v.reshape(B, L, DA_HEADS, DA_V_DIM)
    q = apply_partial_rope(rms_norm(q, q_norm), cos, sin)
    k = apply_partial_rope(rms_norm(k, k_norm), cos, sin)
    f32 = jnp.float32
    lam = (jnp.exp(jnp.sum(lq1.astype(f32) * lk1.astype(f32)))
           - jnp.exp(jnp.sum(lq2.astype(f32) * lk2.astype(f32))) + lam_init)
    kh = jnp.transpose(k, (0, 2, 3, 1, 4))
    vh = jnp.transpose(v, (0, 2, 1, 3))
    nb = L // Q_BLOCK
    qb = jnp.transpose(q, (0, 2, 3, 1, 4)).reshape(B, DA_HEADS, 2, nb, Q_BLOCK, DA_QK_DIM)
    qb = jnp.transpose(qb, (3, 0, 1, 2, 4, 5))
    kpos = jnp.arange(L)
    scale = DA_QK_DIM ** -0.5

    def one_block(args):
        q_blk, bi = args
        s = jnp.einsum('bhcqd,bhckd->bhcqk', q_blk, kh, preferred_element_type=f32) * scale
        qpos = bi * Q_BLOCK + jnp.arange(Q_BLOCK)
        s = jnp.where(kpos[None, :] <= qpos[:, None], s, -jnp.inf)
        p = jax.nn.softmax(s, axis=-1)
        attn = p[:, :, 0] - lam * p[:, :, 1]
        return jnp.einsum('bhqk,bhkd->bhqd', attn.astype(vh.dtype), vh)

    o = lax.map(one_block, (qb, jnp.arange(nb)))
    o = jnp.transpose(o, (1, 0, 3, 2, 4)).reshape(B, L, DA_HEADS, DA_V_DIM)
    o = rms_norm(o, subln) * (1.0 - lam_init)
    return o.reshape(B, L, DA_WIDTH)


def rwkv7_time_mix(r, k, v, w_lo, a_lo, g_lo, w0, w2, a0, a2, g2, k_k, k_a, r_k, ln_w, ln_b):
    B, L, C = r.shape
    f32 = jnp.float32

    def heads(t):
        return t.astype(f32).reshape(B, L, RW_HEADS, RW_HEAD)

    w_log = -jax.nn.softplus(-(w0 + jnp.tanh(w_lo) @ w2).astype(f32)) - 0.5
    decay = jnp.exp(-jnp.exp(w_log))
    a = jax.nn.sigmoid((a0 + a_lo @ a2).astype(f32))
    g = jax.nn.sigmoid(g_lo) @ g2
    kk = heads(k * k_k)
    kk = kk / jnp.maximum(jnp.linalg.norm(kk, axis=-1, keepdims=True), 1e-12)
    k = k.astype(f32) * (1.0 + (a - 1.0) * k_a.astype(f32))
    rh, kh, vh, wh, ah = heads(r), heads(k), heads(v), heads(decay), heads(a)
    a_neg = -kk
    b_vec = kk * ah

    def step(S, inp):
        r_t, w_t, k_t, v_t, an_t, b_t = inp
        sa = jnp.einsum('bhvk,bhk->bhv', S, an_t)
        S = S * w_t[:, :, None, :] + sa[..., None] * b_t[:, :, None, :] + v_t[..., None] * k_t[:, :, None, :]
        return S, jnp.einsum('bhvk,bhk->bhv', S, r_t)

    def seq_major(t):
        return jnp.swapaxes(t, 0, 1)

    S0 = jnp.zeros((B, RW_HEADS, RW_HEAD, RW_HEAD), f32)
    _, y = lax.scan(step, S0, (seq_major(rh), seq_major(wh), seq_major(kh),
                                seq_major(vh), seq_major(a_neg), seq_major(b_vec)))
    y = seq_major(y)
    mu = jnp.mean(y, axis=-1, keepdims=True)
    var = jnp.mean(jnp.square(y - mu), axis=-1, keepdims=True)
    y = ((y - mu) * lax.rsqrt(var + GN_EPS)).reshape(B, L, C) * ln_w.astype(f32) + ln_b.astype(f32)
    bonus = jnp.sum(rh * kh * r_k.astype(f32), axis=-1, keepdims=True) * vh
    out = (y + bonus.reshape(B, L, C)) * g.astype(f32)
    return out.astype(r.dtype)


def setup_inputs(seed: int = 0) -> dict:
    key = jax.random.key(seed)
    ks = jax.random.split(key, 32)
    f = jnp.float32

    def nrm(k, shape, scale):
        return jax.random.normal(k, shape, f) * scale

    def gain(k, shape):
        return 1.0 + 0.02 * jax.random.normal(k, shape, f)

    Dp = DEPTH
    ratio = jnp.arange(RW_WIDTH, dtype=f) / (RW_WIDTH - 1)
    w0_base = -7.0 + 5.0 * ratio ** 0.85 + 0.5
    return {
        "x": nrm(ks[0], (BATCH, SEQ, D_MODEL), 1.0),
        "meta_tokens": nrm(ks[1], (N_META, D_MODEL), 1.0),
        "ffn1_norm": gain(ks[2], (Dp, D_MODEL)),
        "ffn1_w_gate": nrm(ks[3], (Dp, D_MODEL, D_FF), D_MODEL ** -0.5),
        "ffn1_w_up": nrm(ks[4], (Dp, D_MODEL, D_FF), D_MODEL ** -0.5),
        "ffn1_w_down": nrm(ks[5], (Dp, D_FF, D_MODEL), D_FF ** -0.5),
        "mix_norm": gain(ks[6], (Dp, D_MODEL)),
        "w_in": nrm(ks[7], (Dp, D_MODEL, N_IN), D_MODEL ** -0.5),
        "da_q_norm": gain(ks[8], (Dp, DA_QK_DIM)),
        "da_k_norm": gain(ks[9], (Dp, DA_QK_DIM)),
        "da_lambda_q1": nrm(ks[10], (Dp, DA_QK_DIM), 0.1),
        "da_lambda_k1": nrm(ks[11], (Dp, DA_QK_DIM), 0.1),
        "da_lambda_q2": nrm(ks[12], (Dp, DA_QK_DIM), 0.1),
        "da_lambda_k2": nrm(ks[13], (Dp, DA_QK_DIM), 0.1),
        "da_subln": gain(ks[14], (Dp, DA_V_DIM)),
        "rw_shift_mix": jax.random.uniform(ks[15], (Dp, RW_IN), f),
        "rw_w0": w0_base[None, :] + nrm(ks[16], (Dp, RW_WIDTH), 0.1),
        "rw_w2": nrm(ks[17], (Dp, D_DECAY_LORA, RW_WIDTH), 0.1 * D_DECAY_LORA ** -0.5),
        "rw_a0": nrm(ks[18], (Dp, RW_WIDTH), 0.1),
        "rw_a2": nrm(ks[19], (Dp, D_AAA_LORA, RW_WIDTH), 0.1 * D_AAA_LORA ** -0.5),
        "rw_g2": nrm(ks[20], (Dp, D_GATE_LORA, RW_WIDTH), D_GATE_LORA ** -0.5),
        "rw_k_k": 0.85 + nrm(ks[21], (Dp, RW_WIDTH), 0.02),
        "rw_k_a": 1.0 + nrm(ks[22], (Dp, RW_WIDTH), 0.02),
        "rw_r_k": nrm(ks[23], (Dp, RW_HEADS, RW_HEAD), 0.1),
        "rw_ln_w": gain(ks[24], (Dp, RW_WIDTH)),
        "rw_ln_b": nrm(ks[25], (Dp, RW_WIDTH), 0.02),
        "w_out": nrm(ks[26], (Dp, MIX_WIDTH, D_MODEL), MIX_WIDTH ** -0.5),
        "ffn2_norm": gain(ks[27], (Dp, D_MODEL)),
        "ffn2_w_gate": nrm(ks[28], (Dp, D_MODEL, D_FF), D_MODEL ** -0.5),
        "ffn2_w_up": nrm(ks[29], (Dp, D_MODEL, D_FF), D_MODEL ** -0.5),
        "ffn2_w_down": nrm(ks[30], (Dp, D_FF, D_MODEL), D_FF ** -0.5),
    }


def reference(x, meta_tokens, ffn1_norm, ffn1_w_gate, ffn1_w_up, ffn1_w_down, mix_norm, w_in,
              da_q_norm, da_k_norm, da_lambda_q1, da_lambda_k1, da_lambda_q2, da_lambda_k2, da_subln,
              rw_shift_mix, rw_w0, rw_w2, rw_a0, rw_a2, rw_g2, rw_k_k, rw_k_a, rw_r_k, rw_ln_w, rw_ln_b,
              w_out, ffn2_norm, ffn2_w_gate, ffn2_w_up, ffn2_w_down):
    B, T, _ = x.shape
    L = N_META + T
    L_pad = -(-L // Q_BLOCK) * Q_BLOCK
    meta = jnp.broadcast_to(meta_tokens.astype(x.dtype)[None], (B, N_META, D_MODEL))
    h = jnp.concatenate([meta, x], axis=1)
    h = jnp.pad(h, ((0, 0), (0, L_pad - L), (0, 0)))
    cos, sin = rope_tables(L_pad)
    rw_splits = [RW_WIDTH, 2 * RW_WIDTH, 3 * RW_WIDTH,
                 3 * RW_WIDTH + D_DECAY_LORA, 3 * RW_WIDTH + D_DECAY_LORA + D_AAA_LORA]
    for l in range(DEPTH):
        lam_init = 0.8 - 0.6 * math.exp(-0.3 * l)
        h = h + 0.5 * swiglu(rms_norm(h, ffn1_norm[l]), ffn1_w_gate[l], ffn1_w_up[l], ffn1_w_down[l])
        n = rms_norm(h, mix_norm[l])
        u = n @ w_in[l]
        u_da, u_rw = u[..., :DA_IN], u[..., DA_IN:]
        q, k, v = jnp.split(u_da, [DA_QK_WIDTH, 2 * DA_QK_WIDTH], axis=-1)
        u_rw = token_shift(u_rw, rw_shift_mix[l])
        r_rw, k_rw, v_rw, w_lo, a_lo, g_lo = jnp.split(u_rw, rw_splits, axis=-1)
        o_da = diff_attention(q, k, v, da_q_norm[l], da_k_norm[l], da_lambda_q1[l], da_lambda_k1[l],
                              da_lambda_q2[l], da_lambda_k2[l], da_subln[l], cos, sin, lam_init)
        o_rw = rwkv7_time_mix(r_rw, k_rw, v_rw, w_lo, a_lo, g_lo, rw_w0[l], rw_w2[l], rw_a0[l],
                              rw_a2[l], rw_g2[l], rw_k_k[l], rw_k_a[l], rw_r_k[l], rw_ln_w[l], rw_ln_b[l])
        h = h + jnp.concatenate([o_da, o_rw], axis=-1) @ w_out[l]
        h = h + 0.5 * swiglu(rms_norm(h, ffn2_norm[l]), ffn2_w_gate[l], ffn2_w_up[l], ffn2_w_down[l])
    return h[:, N_META:L]
```

```python
import contextlib
import numpy as np
import concourse.bass as bass
import concourse.mybir as mybir
from concourse.bass_utils import run_bass_kernel_spmd

F32 = mybir.dt.float32
BF16 = mybir.dt.bfloat16
I32 = mybir.dt.int32
ALU = mybir.AluOpType
AF = mybir.ActivationFunctionType

D = 1024
FF = 2816
NFF = 22
NPP = 26
NCST = 1088
GN_EPS = 64e-5
EPS = 1e-6


class Buf:
    __slots__ = ("name", "last_w", "readers", "excl")

    def __init__(self, name="", excl=False):
        self.name = name
        self.last_w = None
        self.readers = []
        self.excl = excl


class Op:
    __slots__ = ("eng", "fn", "deps", "signal", "tick", "dkey", "dval", "dinc", "idx")


class Prog:
    ENGS = ("pe", "act", "dve", "pool", "sp")

    def __init__(self, nc):
        self.nc = nc
        self.ops = {e: [] for e in self.ENGS}
        self.n = 0
        self.dcount = {}
        self.dma_ops = []

    def add(self, eng, fn, r=(), w=(), dma=None, dinc=16):
        op = Op()
        op.eng = eng
        op.fn = fn
        op.signal = False
        op.tick = None
        op.dkey = dma
        op.dinc = dinc
        op.dval = None
        op.idx = self.n
        self.n += 1
        snap = self.dcount
        if any(b.excl for b in r):
            w = list(w) + [b for b in r if b.excl]
            r = [b for b in r if not b.excl]
        deps = {}
        for b in r:
            if b.last_w is not None:
                deps[b.last_w.idx] = b.last_w
        for b in w:
            if b.last_w is not None:
                deps[b.last_w.idx] = b.last_w
            for o in b.readers:
                deps[o.idx] = o
        dl = []
        for d in deps.values():
            if d.dkey is None and d.eng == "pe" and eng == "pe" and dma is None:
                continue
            d.signal = True
            dl.append((d, snap[d.dkey] if d.dkey is not None else None))
        op.deps = dl
        if dma is not None:
            self.dcount[dma] = self.dcount.get(dma, 0) + dinc
            op.dval = self.dcount[dma]
            self.dma_ops.append(op)
        for b in r:
            b.readers.append(op)
        for b in w:
            b.last_w = op
            b.readers = []
        self.ops[eng].append(op)
        return op

    def fence(self, bufs):
        lasts = [self.ops[e][-1] for e in self.ENGS if self.ops[e] and self.ops[e][-1].dkey is None]
        last_dma = {}
        for o in self.dma_ops:
            last_dma[o.dkey] = o
        lasts += list(last_dma.values())
        for b in bufs:
            b.readers = list(b.readers) + lasts

    def emit(self, final_wait_ops=()):
        nc = self.nc
        st = contextlib.ExitStack()
        esem = {e: st.enter_context(nc.semaphore("s_" + e)) for e in self.ENGS}
        dsem = {k: st.enter_context(nc.semaphore("d_" + str(k))) for k in self.dcount}
        for d in final_wait_ops:
            d.signal = True
        for e in self.ENGS:
            t = 0
            for op in self.ops[e]:
                if op.dkey is None and op.signal:
                    t += 1
                    op.tick = t
        block = st.enter_context(nc.Block())
        ops = self.ops

        def run(e, eng):
            known = {}
            for op in ops[e]:
                need = {}
                for (d, dv) in op.deps:
                    if d.dkey is not None:
                        s, v = dsem[d.dkey], dv
                    else:
                        s, v = esem[d.eng], d.tick
                    key = id(s)
                    if key not in need or need[key][1] < v:
                        need[key] = (s, v)
                for key, (s, v) in need.items():
                    if known.get(key, 0) >= v:
                        continue
                    eng.wait_ge(s, v)
                    known[key] = v
                inst = op.fn(eng)
                if op.dkey is not None:
                    inst.then_inc(dsem[op.dkey], op.dinc)
                elif op.signal:
                    inst.then_inc(esem[e], 1)
            if e == "sp":
                for d in final_wait_ops:
                    if d.dkey is not None:
                        eng.wait_ge(dsem[d.dkey], d.dval)
                    else:
                        eng.wait_ge(esem[d.eng], d.tick)

        @block.tensor
        def _(eng):
            run("pe", eng)

        @block.scalar
        def _(eng):
            run("act", eng)

        @block.vector
        def _(eng):
            run("dve", eng)

        @block.gpsimd
        def _(eng):
            run("pool", eng)

        @block.sync
        def _(eng):
            run("sp", eng)

        st.close()


class Rot:
    def __init__(self, items):
        self.items = items
        self.i = 0

    def next(self):
        it = self.items[self.i % len(self.items)]
        self.i += 1
        return it


def build(T, stage=9):
    TQ = T // 4
    NG1 = TQ // 256
    NG2 = T // 256
    LP = 16 + T
    NB = T // 128
    nc = bass.Bass("TRN2", target_bir_lowering=False)
    st = contextlib.ExitStack()
    P = Prog(nc)

    def din(name, shape, dt=F32):
        return nc.dram_tensor(name, shape, dt, kind="ExternalInput").ap()

    x_d = din("x", [TQ, D])
    meta_d = din("meta", [16, D])
    gains_d = din("gains", [3, 128, D])
    wg_d = [din("wg1", [D, FF]), din("wg2", [D, FF])]
    wu_d = [din("wu1", [D, FF]), din("wu2", [D, FF])]
    wd_d = [din("wd1", [FF, D]), din("wd2", [FF, D])]
    win_d = din("win", [D, 1056])
    wout_d = din("wout", [D, D])
    pp_d = din("pp", [128, NPP])
    w2_d = din("w2", [64, 128])
    a2_d = din("a2", [64, 128])
    g2_d = din("g2w", [160, 128])
    lam_d = din("lamv", [128, 256])
    subln_d = din("subln", [128, 128])
    ropec_d = din("ropec", [128, 48 + LP])
    ropes_d = din("ropes", [128, 48 + LP])
    cst_d = din("cst", [128, NCST])
    oidx_d = din("oidx", [128, 8], I32)
    out_d = nc.dram_tensor("out", [TQ, D], F32, kind="ExternalOutput").ap()
    dbg = stage < 9
    if stage == 3:
        dbg_d = nc.dram_tensor("dbgo", [128, 8, 128], F32, kind="ExternalOutput").ap()
        dbg_s = nc.dram_tensor("dbgs", [128, 80], F32, kind="ExternalOutput").ap()
    kw_ = dict(kind="ExternalOutput") if dbg else {}
    nTl = [nc.dram_tensor("nTl%d" % g, [D, 256], BF16, **(kw_ if stage == 1 else {})) for g in range(NG1)]
    nTa = [nc.dram_tensor("nTa%d" % g, [4 * D, 256], BF16) for g in range(NG1)]
    oTl = [nc.dram_tensor("oTl%d" % g, [4 * 256, 256], BF16, **(kw_ if stage in (3, 4) else {})) for g in range(NG1)]
    oTa = [nc.dram_tensor("oTa%d" % g, [16 * 256, 256], BF16) for g in range(NG1)]
    h1s = nc.dram_tensor("h1s", [TQ, D], F32, **kw_)
    B_nTl = [Buf("nTl") for _ in range(NG1)]
    B_nTa = [Buf("nTa") for _ in range(NG1)]
    B_oTl = [Buf("oTl") for _ in range(NG1)]
    B_oTa = [Buf("oTa") for _ in range(NG1)]
    RG = [[0, 1, 2, 3], [4, 5, 6, 7]]

    def allgather(src, dst, sB, dB, key):
        return P.add("pool", lambda e: e.collective_compute("AllGather", ALU.bypass, replica_groups=RG,
                                                            ins=[src.ap().bitcast(F32).opt()], outs=[dst.ap().bitcast(F32).opt()]),
                     [sB], [dB], dma=key, dinc=1)
    B_h1s = [Buf("h1s%d" % i) for i in range(TQ // 128)]

    def sb(name, shape, dt=F32):
        return st.enter_context(nc.sbuf_tensor("sb_" + name, shape, dt))

    def mm(out, lhsT, rhs, r, w, start=True, stop=True, skip=False):
        if skip:
            return P.add("pe", lambda e: e.matmul(out, lhsT=lhsT, rhs=rhs, start=start, stop=stop, skip_group_check=True), r, w)
        return P.add("pe", lambda e: e.matmul(out, lhsT=lhsT, rhs=rhs, start=start, stop=stop), r, w)

    def tr(out, in_, ident_ap, r, w):
        return P.add("pe", lambda e: e.transpose(out=out, in_=in_, identity=ident_ap), r, w)

    def act(out, in_, func, r, w, bias=None, scale=None, accum=None):
        kw = {}
        if bias is not None:
            kw["bias"] = bias
        if scale is not None:
            kw["scale"] = scale
        if accum is not None:
            kw["accum_out"] = accum
        return P.add("act", lambda e: e.activation(out=out, in_=in_, func=func, **kw), r, w)

    def cp(eng, out, in_, r, w):
        if eng == "act":
            return P.add("act", lambda e: e.copy(out=out, in_=in_), r, w)
        return P.add(eng, lambda e: e.tensor_copy(out=out, in_=in_), r, w)

    def tt(eng, out, in0, in1, op, r, w):
        return P.add(eng, lambda e: e.tensor_tensor(out=out, in0=in0, in1=in1, op=op), r, w)

    def ts(eng, out, in0, s1, op0, r, w, s2=None, op1=None):
        if s2 is None:
            return P.add(eng, lambda e: e.tensor_scalar(out=out, in0=in0, scalar1=s1, scalar2=None, op0=op0), r, w)
        return P.add(eng, lambda e: e.tensor_scalar(out=out, in0=in0, scalar1=s1, scalar2=s2, op0=op0, op1=op1), r, w)

    def stt(eng, out, in0, scalar, in1, op0, op1, r, w):
        return P.add("dve", lambda e: e.scalar_tensor_tensor(out=out, in0=in0, scalar=scalar, in1=in1, op0=op0, op1=op1), r, w)

    def recip(out, in_, r, w):
        return P.add("dve", lambda e: e.reciprocal(out=out, in_=in_), r, w)

    def memset(eng, ap, val, w):
        return P.add(eng, lambda e: e.memset(ap, val), (), w)

    def dma(q, out, in_, r, w, key):
        return P.add(q, lambda e: e.dma_start(out=out, in_=in_), r, w, dma=key)

    PS = [st.enter_context(nc.psum_tensor("ps%d" % i, [128, 512], F32)) for i in range(8)]
    B_PS = [Buf("ps%d" % i, excl=True) for i in range(8)]

    cst = sb("cst", [128, NCST]); B_cst = Buf("cst")
    ident = cst[:, 0:128]
    onesblk = cst[:, 128:256]
    ones64 = cst[0:64, 128:192]
    rblk = cst[:, 256:384]
    tri_f = cst[:, 384:512]
    suiu2 = cst[0:64, 512:768]
    slm = cst[0:64, 768:832]
    scanmask = cst[0:64, 832:1088]
    ident64 = cst[0:64, 0:64]
    tri_b = sb("tri_b", [128, 128], BF16); B_trib = Buf("trib")
    pp = sb("pp", [128, NPP]); B_pp = Buf("pp")
    ppd = sb("ppd", [128, 8]); B_ppd = Buf("ppd")
    lamv = sb("lamv", [128, 256]); B_lam = Buf("lam")
    lamt = sb("lamt", [128, 72]); B_lamt = Buf("lamt")
    subln = sb("subln", [128, 128]); B_subln = Buf("subln")
    w2s = sb("w2s", [64, 128]); a2s = sb("a2s", [64, 128]); g2a = sb("g2a", [128, 128]); g2b = sb("g2b", [32, 128]); B_lw = Buf("lw")
    gainA = sb("gainA", [128, D]); B_gA = Buf("gA")
    gainM = sb("gainM", [128, D]); B_gM = Buf("gM")
    nTm = sb("nTm", [128, 8, 64], BF16); B_nTm = Buf("nTm")
    nTmA = sb("nTmA", [128, 8, 16], BF16); B_nTmA = Buf("nTmA")
    oidx = sb("oidx", [128, 8], I32); B_oidx = Buf("oidx")
    stat = sb("stat", [128, 32]); statR = Rot([(stat[:, i:i + 1], Buf("st%d" % i)) for i in range(32)])

    dma("sp", cst[:], cst_d, (), [B_cst], "c0")
    dma("sp", pp[:], pp_d, (), [B_pp], "c0")
    dma("sp", lamv[:], lam_d, (), [B_lam], "c0")
    dma("sp", subln[:], subln_d, (), [B_subln], "c0")
    dma("sp", w2s[:], w2_d, (), [B_lw], "c0")
    dma("sp", a2s[:], a2_d, (), [B_lw], "c0")
    dma("sp", g2a[:], g2_d[0:128, :], (), [B_lw], "c0")
    dma("sp", g2b[:], g2_d[128:160, :], (), [B_lw], "c0")
    dma("sp", gainA[:], gains_d[0], (), [B_gA], "c0")
    dma("sp", gainM[:], gains_d[1], (), [B_gM], "c0")
    dma("sp", oidx[:], oidx_d, (), [B_oidx], "c0")
    cp("dve", tri_b[:], tri_f, [B_cst], [B_trib])
    memset("pool", nTm[:], 0.0, [B_nTm])
    for h in range(2):
        ts("dve", ppd[:, h:h + 1], pp[:, 5 + 10 * h:6 + 10 * h], -1.0, ALU.mult, [B_pp], [B_ppd])
        ts("dve", ppd[:, 2 + h:3 + h], pp[:, 8 + 10 * h:9 + 10 * h], -1.0, ALU.mult, [B_pp], [B_ppd], s2=1.0, op1=ALU.add)
    for i in range(2):
        tt("dve", lamt[:, 0:64], lamv[:, 128 * i:128 * i + 64], lamv[:, 128 * i + 64:128 * i + 128], ALU.mult, [B_lam], [B_lamt])
        P.add("dve", (lambda i: lambda e: e.reduce_sum(out=lamt[:, 64 + i:65 + i], in_=lamt[:, 0:64], axis=mybir.AxisListType.X))(i), [B_lamt], [B_lamt])
    act(lamt[:, 66:68], lamt[:, 64:66], AF.Exp, [B_lamt], [B_lamt])
    tt("dve", lamt[:, 68:69], lamt[:, 67:68], lamt[:, 66:67], ALU.subtract, [B_lamt], [B_lamt])
    ts("dve", lamt[:, 70:71], lamt[:, 68:69], -0.2, ALU.add, [B_lamt], [B_lamt])
    neglam = lamt[:, 70:71]
    ts("dve", subln[:], subln[:], 0.8, ALU.mult, [B_subln], [B_subln])

    AR_BYTES = 151552 + 1024
    arena = sb("arena", [128, AR_BYTES // 4])

    class Arena:
        def __init__(self):
            self.off = 0

        def alloc(self, parts, free, dt):
            esz = 4 if dt in (F32, I32) else 2
            n = int(np.prod(free))
            nb = (n * esz + 3) // 4 * 4
            assert self.off + nb <= AR_BYTES, (self.off, nb)
            a = arena[:, self.off // 4:(self.off + nb) // 4]
            self.off += nb
            if dt != F32:
                a = a.bitcast(dt)
            a = a[0:parts, 0:n]
            if len(free) == 2:
                a = a.rearrange("p (a b) -> p a b", b=free[1])
            elif len(free) == 3:
                a = a.rearrange("p (a b c) -> p a b c", b=free[1], c=free[2])
            return a

    A13 = Arena()
    WG = A13.alloc(128, [8, FF], BF16)
    WU = A13.alloc(128, [8, FF], BF16)
    WD = A13.alloc(128, [NFF, D], BF16)
    WO = A13.alloc(128, [8, D], BF16)
    B_WG, B_WU, B_WD, B_WO = Buf("WG"), Buf("WU"), Buf("WD"), Buf("WO")

    def load_ffn_weights(i):
        for k in range(8):
            P.add("pool", (lambda k: lambda e: e.dma_start(out=WG[:, k, :].rearrange("p (a b) -> p a b", b=704),
                                                            in_=wg_d[i][k * 128:(k + 1) * 128, :].rearrange("p (a b) -> p a b", b=704)))(k),
                  (), [B_WG], dma="wg")
        for k in range(8):
            P.add("pool", (lambda k: lambda e: e.dma_start(out=WU[:, k, :].rearrange("p (a b) -> p a b", b=704),
                                                            in_=wu_d[i][k * 128:(k + 1) * 128, :].rearrange("p (a b) -> p a b", b=704)))(k),
                  (), [B_WU], dma="wu")
        for f in range(NFF):
            P.add("pool", (lambda f: lambda e: e.dma_start(out=WD[:, f, :].rearrange("p (a b) -> p a b", b=512),
                                                            in_=wd_d[i][f * 128:(f + 1) * 128, :].rearrange("p (a b) -> p a b", b=512)))(f),
                  (), [B_WD], dma="wd")

    load_ffn_weights(0)

    XT = Rot([(sb("xt%d" % i, [128, D]), Buf("xt%d" % i)) for i in range(3)])
    HT = Rot([(sb("ht%d" % i, [128, D]), Buf("ht%d" % i)) for i in range(2)])
    NF = Rot([(sb("nf%d" % i, [128, D]), Buf("nf%d" % i)) for i in range(2)])
    NT = Rot([(sb("nt%d" % i, [128, 8, 256], BF16), Buf("nt%d" % i)) for i in range(1)])
    NT2 = Rot([(sb("nt2_%d" % i, [128, 8, 256], BF16), Buf("nt2_%d" % i)) for i in range(1)])
    SG = Rot([(sb("sg%d" % i, [128, 256]), Buf("sg%d" % i)) for i in range(2)])
    ACTT = Rot([(sb("actt%d" % i, [128, 256], BF16), Buf("actt%d" % i)) for i in range(3)])

    def norm_T(h, hB, npart, gain, gB, dst, dB, c0):
        nf, nfB = NF.next()
        ss, ssB = statR.next()
        sd, sdB = statR.next()
        memset("pool", ss[0:npart], 0.0, [ssB])
        act(nf[0:npart, :], h, AF.Square, [hB], [nfB, ssB], accum=ss[0:npart])
        act(sd[0:npart], ss[0:npart], AF.Sqrt, [ssB], [sdB], bias=EPS, scale=1.0 / D)
        recip(sd[0:npart], sd[0:npart], [sdB], [sdB])
        stt("dve", nf[0:npart, :], h, sd[0:npart], gain[0:npart, :], ALU.mult, ALU.mult, [hB, sdB, gB], [nfB])
        for b in range(2):
            bank = 6 + b
            for j in range(4):
                k = 4 * b + j
                tr(PS[bank][:, j * 128:j * 128 + npart], nf[0:npart, k * 128:(k + 1) * 128], ident[0:npart, 0:npart], [nfB, B_cst], [B_PS[bank]])
            src = PS[bank][:, :].rearrange("p (a b) -> p a b", b=128)[:, :, 0:npart]
            cp("act" if b == 0 else "dve", dst[:, 4 * b:4 * b + 4, c0:c0 + npart], src, [B_PS[bank]], [dB])

    def ffn_group(tiles, gain, gB, outs):
        nT, nTB = NT.next()
        offs = []
        N = 0
        for (h, hB, npart) in tiles:
            offs.append(N)
            norm_T(h, hB, npart, gain, gB, nT, nTB, N)
            N += npart

        def gu(f):
            bank = 4 + f % 2
            for k in range(8):
                mm(PS[bank][:, 0:N], WG[:, k, f * 128:(f + 1) * 128], nT[:, k, 0:N], [B_WG, nTB], [B_PS[bank]], start=(k == 0), stop=(k == 7))
            for k in range(8):
                mm(PS[bank][:, 256:256 + N], WU[:, k, f * 128:(f + 1) * 128], nT[:, k, 0:N], [B_WU, nTB], [B_PS[bank]], start=(k == 0), stop=(k == 7))

        gu(0)
        for f in range(NFF):
            if f + 1 < NFF:
                gu(f + 1)
            bank = 4 + f % 2
            sg, sgB = SG.next()
            at, atB = ACTT.next()
            act(sg[:, 0:N], PS[bank][:, 0:N], AF.Silu, [B_PS[bank]], [sgB])
            tt("dve", at[:, 0:N], sg[:, 0:N], PS[bank][:, 256:256 + N], ALU.mult, [sgB, B_PS[bank]], [atB])
            for i, (h, hB, npart) in enumerate(tiles):
                for half in range(2):
                    yb = 2 * i + half
                    mm(PS[yb][0:npart, :], at[:, offs[i]:offs[i] + npart], WD[:, f, half * 512:(half + 1) * 512], [atB, B_WD], [B_PS[yb]],
                       start=(f == 0), stop=(f == NFF - 1))
        for i, (h, hB, npart) in enumerate(tiles):
            o, oB = outs[i]
            for half in range(2):
                yb = 2 * i + half
                stt("dve", o[0:npart, half * 512:(half + 1) * 512], PS[yb][0:npart, :], 0.5, h[:, half * 512:(half + 1) * 512], ALU.mult, ALU.add,
                    [B_PS[yb], hB], [oB])

    xm, xmB = XT.next()
    dma("sp", xm[0:16, :], meta_d, (), [xmB], "x")
    hm, hmB = HT.next()
    ffn_group([(xm[0:16, :], xmB, 16)], gainA, B_gA, [(hm, hmB)])
    norm_T(hm[0:16, :], hmB, 16, gainM, B_gM, nTm, B_nTm, 48)
    cp("pool", nTmA[:, :, :], nTm[:, :, 48:64], [B_nTm], [B_nTmA])
    for g in range(NG1):
        tiles = []
        outs = []
        for i in range(2):
            xt, xB = XT.next()
            r0 = g * 256 + i * 128
            dma("sp", xt[:], x_d[r0:r0 + 128, :], (), [xB], "x")
            tiles.append((xt[:], xB, 128))
            outs.append(HT.next())
        ffn_group(tiles, gainA, B_gA, outs)
        n2, n2B = NT2.next()
        for i in range(2):
            ht, hB = outs[i]
            r0 = g * 256 + i * 128
            dma("sp", h1s.ap()[r0:r0 + 128, :], ht[:], [hB], [B_h1s[2 * g + i]], "h1w")
            norm_T(ht[:], hB, 128, gainM, B_gM, n2, n2B, i * 128)
        dma("sp", nTl[g].ap().rearrange("(k p) t -> p k t", p=128), n2[:], [n2B], [B_nTl[g]], "nTw")
        if stage != 1:
            cc1 = allgather(nTl[g], nTa[g], B_nTl[g], B_nTa[g], "cc1")
    if stage == 1:
        lastw = [o for o in P.dma_ops if o.dkey in ("nTw", "h1w")]
        P.emit(final_wait_ops=[lastw[-1]] + [o for o in lastw if o.dkey == "h1w"][-1:])
        st.close()
        return nc
    if stage == 2:
        P.emit(final_wait_ops=[cc1])
        st.close()
        return nc

    A2 = Arena()
    tenants = []

    def ten(parts, free, dt, name):
        b = Buf(name)
        tenants.append(b)
        return A2.alloc(parts, free, dt), b

    WIN, B_WIN = ten(128, [8, 1056], BF16, "win")
    KT, _ = ten(128, [LP], BF16, "KT")
    B_KT = [Buf("kt%d" % j) for j in range(NB + 1)]
    VX, _ = ten(128, [NB, 129], BF16, "VX")
    B_VX = [Buf("vx%d" % j) for j in range(NB)]
    VM, B_VM = ten(16, [129], BF16, "VM")
    tenants += B_KT + B_VX
    NTG = Rot([ten(128, [8, 256], BF16, "ntg%d" % i) for i in range(2)])
    RC = Rot([ten(128, [256], F32, "rc%d" % i) for i in range(2)])
    RS = Rot([ten(128, [256], F32, "rs%d" % i) for i in range(2)])
    QTB = Rot([ten(128, [256], BF16, "qtb%d" % i) for i in range(2)])
    PTB = Rot([ten(128, [512], BF16, "ptb%d" % i) for i in range(3)])
    F128 = Rot([ten(128, [256], F32, "f128_%d" % i) for i in range(6)])
    OTD = Rot([ten(128, [256], BF16, "otd%d" % i) for i in range(2)])
    ATT = Rot([ten(128, [128], F32, "att%d" % i) for i in range(4)])
    RECS, B_RECS = ten(128, [8], F32, "recs")
    RAW = {}
    for nm, parts in (("r0", 64), ("k0", 64), ("v0", 64), ("r1", 64), ("k1", 64), ("v1", 64), ("wlo", 64), ("alo", 64), ("ga", 128), ("gb", 32)):
        RAW[nm] = ten(parts, [257], F32, "raw_" + nm) + (parts,)
    F64 = Rot([ten(64, [256], F32, "f64_%d" % i) for i in range(40)])
    ALOS = ten(64, [256], F32, "alos")
    WLOS = ten(64, [256], F32, "wlos")
    THB = ten(64, [256], F32, "thb")
    SGA, B_SGA = ten(128, [256], F32, "sga")
    SGB, B_SGB = ten(32, [256], F32, "sgb")
    ARb = Rot([ten(64, [512], F32, "ar%d" % i) for i in range(2)])
    OTR = Rot([ten(64, [256], BF16, "otr%d" % i) for i in range(2)])
    TM = Rot([ten(64, [3, 64], F32, "tm%d" % i) for i in range(2)])
    M12 = Rot([ten(64, [256], F32, "m12_%d" % i) for i in range(2)])
    XX = Rot([ten(64, [128], F32, "xx%d" % i) for i in range(3)])
    YY = Rot([ten(64, [128], F32, "yy%d" % i) for i in range(3)])
    SM = Rot([ten(64, [64], F32, "sm%d" % i) for i in range(8)])
    HS = [[ten(64, [64], F32, "hs%d_%d" % (h, i)) for i in range(2)] for h in range(2)]

    P.fence(tenants)
    SKIP = int(os.environ.get("SKIP", "0"))
    if not SKIP & 1:
        for k in range(8):
            P.add("pool", (lambda k: lambda e: e.dma_start(out=WIN[:, k, :].rearrange("p (a b) -> p a b", b=528),
                                                            in_=win_d[k * 128:(k + 1) * 128, :].rearrange("p (a b) -> p a b", b=528)))(k),
                  (), [B_WIN], dma="win")
    if not SKIP & 2:
        for h in range(2):
            memset("pool", HS[h][0][0], 0.0, [HS[h][0][1]])
        for nm in RAW:
            memset("pool", RAW[nm][0][:, 0:1], 0.0, [RAW[nm][1]])
    if not SKIP & 4:
        memset("pool", VX[:, :, 128:129], 1.0, B_VX)
    if not SKIP & 8:
        memset("pool", VM[:, 128:129], 1.0, [B_VM])

    PJ = Rot([(PS[4][:, 0:256], B_PS[4]), (PS[5][:, 0:256], B_PS[5])])
    RW = Rot([(PS[6][:, 0:256], B_PS[6]), (PS[7][:, 0:256], B_PS[7])])
    STS = [[(PS[c][:, 0:256], B_PS[c]), (PS[c][:, 0:256], B_PS[c])] for c in range(2)]
    hcur = [0, 0]
    blk_count = [0]

    def proj(col0, M, nT, nTB, ntok):
        pj, pjB = PJ.next()
        for k in range(8):
            mm(pj[0:M, 0:ntok], WIN[:, k, col0:col0 + M], nT[:, k, 0:ntok], [B_WIN, nTB], [pjB], start=(k == 0), stop=(k == 7))
        return pj, pjB

    def tokshift(nm, psrc, psB, ntok, mixcol, dst=None):
        raw, rawB, parts = RAW[nm]
        cp("act", raw[:, 1:ntok + 1], psrc[0:parts, 0:ntok], [psB], [rawB])
        d, dB = (F128.next() if parts > 64 else F64.next())
        o, oB = dst if dst is not None else (F128.next() if parts > 64 else F64.next())
        tt("pool", d[0:parts, 0:ntok], raw[:, 0:ntok], raw[:, 1:ntok + 1], ALU.subtract, [rawB], [dB])
        stt("pool", o[0:parts, 0:ntok], d[0:parts, 0:ntok], pp[0:parts, mixcol:mixcol + 1], raw[:, 1:ntok + 1], ALU.mult, ALU.add, [dB, rawB, B_pp], [oB])
        cp("pool", raw[:, 0:1], raw[:, ntok:ntok + 1], [rawB, dB, oB], [rawB])
        return o, oB

    class StopBuild(Exception):
        pass
    ckc = [0]
    CUTN = int(os.environ.get("CUTN", "0"))

    CUTTAG = os.environ.get("CUTTAG", "")

    def ck(tag=None):
        if tag is not None:
            if tag == CUTTAG:
                raise StopBuild()
            return
        ckc[0] += 1
        if ckc[0] == CUTN:
            raise StopBuild()

    def phase2_group(gi):
        is_meta = gi < 0
        if CUT == 5:
            return
        ntok = 64 if is_meta else 256
        nch = ntok // 64
        if is_meta:
            nT, nTB = nTm, B_nTm
            tcol = 0
        else:
            nT, nTB = NTG.next()
            q, gl = gi // NG1, gi % NG1
            src = nTa[gl].ap().rearrange("(q k p) t -> q p k t", q=4, k=8, p=128)[q]
            dma("sp", nT[:], src, [B_nTa[gl]], [nTB], "ntg")
            tcol = 64 + gi * 256
        rc, rcB = RC.next()
        rs, rsB = RS.next()
        dma("sp", rc[:, 0:ntok], ropec_d[:, tcol:tcol + ntok], (), [rcB], "rope")
        dma("sp", rs[:, 0:ntok], ropes_d[:, tcol:tcol + ntok], (), [rsB], "rope")

        qtb = None
        for which in (["k"] if is_meta else ["q", "k"]):
            col0 = 0 if which == "q" else 128
            gcol = 0 if which == "q" else 1
            pj, pjB = proj(col0, 128, nT, nTB, ntok)
            ck()
            sq, sqB = F128.next()
            act(sq[:, 0:ntok], pj[:, 0:ntok], AF.Square, [pjB], [sqB])
            ck()
            p2, p2B = PJ.next()
            mm(p2[:, 0:ntok], onesblk, sq[:, 0:ntok], [sqB, B_cst], [p2B])
            ck()
            rn, rnB = F128.next()
            act(rn[:, 0:ntok], p2[:, 0:ntok], AF.Sqrt, [p2B], [rnB], bias=EPS, scale=1.0 / 64)
            recip(rn[:, 0:ntok], rn[:, 0:ntok], [rnB], [rnB])
            ck()
            qn, qnB = F128.next()
            stt("dve", qn[:, 0:ntok], pj[:, 0:ntok], pp[:, gcol:gcol + 1], rn[:, 0:ntok], ALU.mult, ALU.mult, [pjB, rnB, B_pp], [qnB])
            ck()
            p3, p3B = PJ.next()
            mm(p3[:, 0:ntok], rblk, qn[:, 0:ntok], [qnB, B_cst], [p3B])
            ck()
            t1, t1B = F128.next()
            tt("pool", t1[:, 0:ntok], qn[:, 0:ntok], rc[:, 0:ntok], ALU.mult, [qnB, rcB], [t1B])
            t2, t2B = F128.next()
            tt("dve", t2[:, 0:ntok], p3[:, 0:ntok], rs[:, 0:ntok], ALU.mult, [p3B, rsB], [t2B])
            ck()
            if which == "q":
                qtb, qtbB = QTB.next()
                tt("pool", qtb[:, 0:ntok], t1[:, 0:ntok], t2[:, 0:ntok], ALU.add, [t1B, t2B], [qtbB])
            elif is_meta:
                tt("pool", KT[:, 0:16], t1[:, 48:64], t2[:, 48:64], ALU.add, [t1B, t2B], [B_KT[0]])
            else:
                for i in range(2):
                    p0 = 16 + gi * 256 + i * 128
                    tt("pool", KT[:, p0:p0 + 128], t1[:, i * 128:(i + 1) * 128], t2[:, i * 128:(i + 1) * 128], ALU.add, [t1B, t2B], [B_KT[1 + 2 * gi + i]])
        ck()
        if is_meta:
            if os.environ.get("VARS"):
                PJ.next()
            pj, pjB = PJ.next()
            for k in range(8):
                if os.environ.get("VARM") == "rhs0":
                    mm(pj[0:64, 0:128], nT[:, k, 0:64], WIN[:, k, 0:128], [B_WIN, nTB], [pjB], start=(k == 0), stop=(k == 7))
                elif os.environ.get("VARM") == "swap":
                    mm(pj[0:128, 0:64], WIN[:, k, 256:384], nT[:, k, 0:64], [B_WIN, nTB], [pjB], start=(k == 0), stop=(k == 7))
                elif os.environ.get("VARM") == "64":
                    mm(pj[0:64, 0:128], nT[:, k, 0:64], WIN[:, k, 256:384], [B_WIN, nTB], [pjB], start=(k == 0), stop=(k == 7))
                else:
                    mm(pj[0:16, 0:128], nTmA[:, k, :], WIN[:, k, 256:384], [B_WIN, B_nTmA], [pjB], start=(k == 0), stop=(k == 7))
            ck()
            cp("act", VM[:, 0:128], pj[0:16, 0:128], [pjB], [B_VM])
            ck()
        else:
            for i in range(2):
                pj, pjB = PJ.next()
                for k in range(8):
                    mm(pj[:, 0:128], nT[:, k, i * 128:(i + 1) * 128], WIN[:, k, 256:384], [B_WIN, nTB], [pjB], start=(k == 0), stop=(k == 7))
                cp("act", VX[:, 2 * gi + i, 0:128], pj[:, 0:128], [pjB], [B_VX[2 * gi + i]])

        if not is_meta and CUT != 3:
            blocks = [(-1, 16)] + [(j, 128) for j in range(2 * gi + 2)]
            first = [True, True]
            lastj = [2 * gi, 2 * gi + 1]
            for (j, kb) in blocks:
                sl = blk_count[0] % 2
                blk_count[0] += 1
                kB = B_KT[0] if j < 0 else B_KT[1 + j]
                k0 = 0 if j < 0 else 16 + j * 128
                vap = VM[:, :] if j < 0 else VX[:, j, :]
                vB = B_VM if j < 0 else B_VX[j]
                pt, ptB = PTB.next()
                for c in range(2):
                    s_ap, sB = STS[c][sl]
                    mm(s_ap[0:kb, :], KT[c * 64:(c + 1) * 64, k0:k0 + kb], qtb[c * 64:(c + 1) * 64, :], [kB, qtbB], [sB])
                    act(pt[0:kb, c * 256:(c + 1) * 256], s_ap[0:kb, :], AF.Exp, [sB], [ptB], scale=0.125)
                for it in range(2):
                    if j == lastj[it]:
                        for c in range(2):
                            a = pt[:, c * 256 + it * 128:c * 256 + (it + 1) * 128]
                            tt("pool", a, a, tri_b[:], ALU.mult, [ptB, B_trib], [ptB])
                for c in range(2):
                    for it in range(2):
                        if j > lastj[it]:
                            continue
                        mm(PS[2 + c][:, it * 129:(it + 1) * 129], pt[0:kb, c * 256 + it * 128:c * 256 + (it + 1) * 128], vap[0:kb, :],
                           [ptB, vB], [B_PS[2 + c]], start=first[c], stop=(j == lastj[it]), skip=True)
                        first[c] = False
            otd, otdB = OTD.next()
            for it in range(2):
                for c in range(2):
                    recip(RECS[:, 2 * it + c:2 * it + c + 1], PS[2 + c][:, it * 129 + 128:it * 129 + 129], [B_PS[2 + c]], [B_RECS])
                o1, o1B = ATT.next()
                t2, t2B = ATT.next()
                ts("dve", o1[:], PS[2][:, it * 129:it * 129 + 128], RECS[:, 2 * it:2 * it + 1], ALU.mult, [B_PS[2], B_RECS], [o1B])
                ts("dve", t2[:], PS[3][:, it * 129:it * 129 + 128], RECS[:, 2 * it + 1:2 * it + 2], ALU.mult, [B_PS[3], B_RECS, B_lamt], [t2B],
                   s2=neglam, op1=ALU.mult)
                od, odB = ATT.next()
                tt("pool", od[:], o1[:], t2[:], ALU.add, [o1B, t2B], [odB])
                ss, ssB = statR.next()
                sd, sdB = statR.next()
                memset("pool", ss, 0.0, [ssB])
                act(o1[:], od[:], AF.Square, [odB], [o1B, ssB], accum=ss)
                act(sd, ss, AF.Sqrt, [ssB], [sdB], bias=EPS, scale=1.0 / 128)
                recip(sd, sd, [sdB], [sdB])
                on, onB = ATT.next()
                stt("dve", on[:], od[:], sd, subln[:], ALU.mult, ALU.mult, [odB, sdB, B_subln], [onB])
                if stage == 3 and gi == 0:
                    dma("sp", dbg_d[:, 4 * it + 0, :], o1[:], [o1B], (), "dbg")
                    dma("sp", dbg_d[:, 4 * it + 1, :], t2[:], [t2B], (), "dbg")
                    dma("sp", dbg_d[:, 4 * it + 2, :], od[:], [odB], (), "dbg")
                    dma("sp", dbg_d[:, 4 * it + 3, :], on[:], [onB], (), "dbg")
                    if it == 1:
                        dma("sp", dbg_s[:, 0:8], RECS[:, :], [B_RECS], (), "dbg")
                        dma("sp", dbg_s[:, 8:80], lamt[:, :], [B_lamt], (), "dbg")
                rw, rwB = RW.next()
                tr(rw[:, 0:128], on[:], ident, [onB, B_cst], [rwB])
                cp("act", otd[:, it * 128:(it + 1) * 128], rw[:, 0:128], [rwB], [otdB])
            q, gl = gi // NG1, gi % NG1
            r0 = q * 256
            dma("sp", oTl[gl].ap()[r0:r0 + 128, :], otd[:], [otdB], [B_oTl[gl]], "oTw")

        if CUT == 1 or (CUT == 2 and not is_meta):
            return
        pj, pjB = proj(768, 64, nT, nTB, ntok)
        wlo, wloB = tokshift("wlo", pj, pjB, ntok, 22, dst=WLOS)
        pj, pjB = proj(832, 64, nT, nTB, ntok)
        alo, aloB = tokshift("alo", pj, pjB, ntok, 23, dst=ALOS)
        pj, pjB = proj(896, 128, nT, nTB, ntok)
        ga, gaB = tokshift("ga", pj, pjB, ntok, 24)
        pj, pjB = proj(1024, 32, nT, nTB, ntok)
        gb, gbB = tokshift("gb", pj, pjB, ntok, 25)
        if not is_meta:
            ck("A")
        th, thB = THB
        act(th[:, 0:ntok], wlo[0:64, 0:ntok], AF.Tanh, [wloB], [thB])
        act(SGA[:, 0:ntok], ga[:, 0:ntok], AF.Sigmoid, [gaB], [B_SGA])
        act(SGB[:, 0:ntok], gb[0:32, 0:ntok], AF.Sigmoid, [gbB], [B_SGB])
        for h in range(2):
            pb = 2 + 10 * h
            pj, pjB = proj(384 + 64 * h, 64, nT, nTB, ntok)
            r_s, rB = tokshift("r%d" % h, pj, pjB, ntok, pb + 0)
            pj, pjB = proj(512 + 64 * h, 64, nT, nTB, ntok)
            k_s, kB_ = tokshift("k%d" % h, pj, pjB, ntok, pb + 1)
            pj, pjB = proj(640 + 64 * h, 64, nT, nTB, ntok)
            v_s, vB_ = tokshift("v%d" % h, pj, pjB, ntok, pb + 2)
            N_ = slice(0, ntok)
            pj, pjB = PJ.next()
            mm(pj[0:64, N_], w2s[:, 64 * h:64 * h + 64], th[:, N_], [B_lw, thB], [pjB])
            e1, e1B = F64.next()
            act(e1[:, N_], pj[0:64, N_], AF.Exp, [pjB, B_ppd], [e1B], bias=ppd[0:64, h:h + 1], scale=-1.0)
            act(e1[:, N_], e1[:, N_], AF.Ln, [e1B], [e1B], bias=1.0)
            e2, e2B = F64.next()
            act(e2[:, N_], e1[:, N_], AF.Exp, [e1B], [e2B], bias=-0.5, scale=-1.0)
            pj, pjB = PJ.next()
            mm(pj[0:64, N_], a2s[:, 64 * h:64 * h + 64], alo[0:64, N_], [B_lw, aloB], [pjB])
            lr, lrB = F64.next()
            act(lr[:, N_], pj[0:64, N_], AF.Sigmoid, [pjB, B_pp], [lrB], bias=pp[0:64, pb + 4:pb + 5])
            pj, pjB = PJ.next()
            mm(pj[0:64, N_], g2a[:, 64 * h:64 * h + 64], SGA[:, N_], [B_lw, B_SGA], [pjB], start=True, stop=False)
            mm(pj[0:64, N_], g2b[:, 64 * h:64 * h + 64], SGB[:, N_], [B_lw, B_SGB], [pjB], start=False, stop=True)
            gT, gTB = F64.next()
            cp("act", gT[:, N_], pj[0:64, N_], [pjB], [gTB])
            kk, kkB = F64.next()
            ts("pool", kk[:, N_], k_s[0:64, N_], pp[0:64, pb + 5:pb + 6], ALU.mult, [kB_, B_pp], [kkB])
            ksq, ksqB = F64.next()
            act(ksq[:, N_], kk[:, N_], AF.Square, [kkB], [ksqB])
            pj, pjB = PJ.next()
            mm(pj[0:64, N_], ones64, ksq[:, N_], [ksqB, B_cst], [pjB])
            rn, rnB = F64.next()
            act(rn[:, N_], pj[0:64, N_], AF.Sqrt, [pjB], [rnB])
            ts("dve", rn[:, N_], rn[:, N_], 1e-12, ALU.max, [rnB], [rnB])
            recip(rn[:, N_], rn[:, N_], [rnB], [rnB])
            kkn, kknB = F64.next()
            tt("pool", kkn[:, N_], kk[:, N_], rn[:, N_], ALU.mult, [kkB, rnB], [kknB])
            t1, t1B = F64.next()
            ts("dve", t1[:, N_], lr[:, N_], pp[0:64, pb + 6:pb + 7], ALU.mult, [lrB, B_pp, B_ppd], [t1B], s2=ppd[0:64, 2 + h:3 + h], op1=ALU.add)
            kmod, kmB = F64.next()
            tt("pool", kmod[:, N_], k_s[0:64, N_], t1[:, N_], ALU.mult, [kB_, t1B], [kmB])
            bv, bvB = F64.next()
            tt("pool", bv[:, N_], kkn[:, N_], lr[:, N_], ALU.mult, [kknB, lrB], [bvB])
            rk, rkB = F64.next()
            stt("pool", rk[:, N_], r_s[0:64, N_], pp[0:64, pb + 7:pb + 8], kmod[:, N_], ALU.mult, ALU.mult, [rB, kmB, B_pp], [rkB])
            pj, pjB = PJ.next()
            mm(pj[0:64, N_], ones64, rk[:, N_], [rkB, B_cst], [pjB])
            bon, bonB = F64.next()
            tt("dve", bon[:, N_], pj[0:64, N_], v_s[0:64, N_], ALU.mult, [pjB, vB_], [bonB])
            gneg, gnB = F64.next()
            P.add("dve", (lambda o, d0, d1: lambda e: e.tensor_tensor_scan(out=o, data0=d0, data1=d1, initial=0.0, op0=ALU.mult, op1=ALU.add))(
                gneg[:, N_], scanmask[:, N_], e2[:, N_]), [e2B, B_cst], [gnB])
            Ep, EpB = F64.next()
            Em, EmB = F64.next()
            Ea, EaB = F64.next()
            act(Ep[:, N_], gneg[:, N_], AF.Exp, [gnB], [EpB], scale=-1.0)
            act(Em[:, N_], gneg[:, N_], AF.Exp, [gnB], [EmB])
            tt("pool", Ea[:, N_], e2[:, N_], gneg[:, N_], ALU.subtract, [e2B, gnB], [EaB])
            act(Ea[:, N_], Ea[:, N_], AF.Exp, [EaB], [EaB])
            AR, ARB = ARb.next()
            ARv = AR[:, 0:2 * ntok].rearrange("p (c t) -> p c t", t=128)
            stt("pool", ARv[:, :, 0:64], kkn[:, N_].rearrange("p (c t) -> p c t", t=64), -1.0, Ea[:, N_].rearrange("p (c t) -> p c t", t=64),
                ALU.mult, ALU.mult, [kknB, EaB], [ARB])
            tt("pool", ARv[:, :, 64:128], r_s[0:64, N_].rearrange("p (c t) -> p c t", t=64), Ep[:, N_].rearrange("p (c t) -> p c t", t=64), ALU.mult,
               [rB, EpB], [ARB])
            BT, BTB = F64.next()
            KTl, KTlB = F64.next()
            tt("pool", BT[:, N_], bv[:, N_], Em[:, N_], ALU.mult, [bvB, EmB], [BTB])
            tt("pool", KTl[:, N_], kmod[:, N_], Em[:, N_], ALU.mult, [kmB, EmB], [KTlB])
            BH, BHB = F64.next()
            KH, KHB = F64.next()
            for c in range(nch):
                cc = slice(c * 64, (c + 1) * 64)
                gc = Ep[:, c * 64 + 63:c * 64 + 64]
                ts("pool", BH[:, cc], BT[:, cc], gc, ALU.mult, [BTB, EpB], [BHB])
                ts("pool", KH[:, cc], KTl[:, cc], gc, ALU.mult, [KTlB, EpB], [KHB])
            yT, yTB = F64.next()
            if not is_meta:
                ck("B%d" % h)
            for c in range(nch):
                cc = slice(c * 64, (c + 1) * 64)
                at_c = AR[:, c * 128:c * 128 + 64]
                rt_c = AR[:, c * 128 + 64:c * 128 + 128]
                ar_c = AR[:, c * 128:(c + 1) * 128]
                gc = Ep[:, c * 64 + 63:c * 64 + 64]
                rw, rwB = RW.next()
                tr(rw[0:64, 0:64], BH[:, cc], ident64, [BHB, B_cst], [rwB])
                tr(rw[0:64, 64:128], KH[:, cc], ident64, [KHB, B_cst], [rwB])
                tr(rw[0:64, 128:192], v_s[0:64, cc], ident64, [vB_, B_cst], [rwB])
                tr(rw[0:64, 192:256], at_c, ident64, [ARB, B_cst], [rwB])
                tm, tmB = TM.next()
                yy, yyB = YY.next()
                cp("act", tm[:, :, :], rw[0:64, 0:192].rearrange("p (a b) -> p a b", b=64), [rwB], [tmB])
                cp("dve", yy[:, 0:64], rw[0:64, 192:256], [rwB], [yyB])
                bh_t, kh_t, v_t = tm[:, 0, :], tm[:, 1, :], tm[:, 2, :]
                rw, rwB = RW.next()
                mm(rw[0:64, 0:128], BT[:, cc], ar_c, [BTB, ARB], [rwB])
                mm(rw[0:64, 128:256], KTl[:, cc], ar_c, [KTlB, ARB], [rwB])
                m12, m12B = M12.next()
                tt("dve", m12[:, :], rw[0:64, 0:256], suiu2, ALU.mult, [rwB, B_cst], [m12B])
                nabT, arbT, nakT, arkT = m12[:, 0:64], m12[:, 64:128], m12[:, 128:192], m12[:, 192:256]
                rw, rwB = RW.next()
                mm(rw[0:64, 0:64], at_c, BT[:, cc], [ARB, BTB], [rwB])
                mm(rw[0:64, 64:128], nakT, v_t, [m12B, tmB], [rwB])
                xx, xxB = XX.next()
                tt("dve", xx[:, 0:64], rw[0:64, 0:64], slm, ALU.mult, [rwB, B_cst], [xxB])
                cp("act", yy[:, 64:128], rw[0:64, 64:128], [rwB], [yyB])
                X, XTt, XB = xx[:, 0:64], nabT, [xxB, m12B]
                for m in range(6):
                    rw, rwB = RW.next()
                    mm(rw[0:64, 0:128], XTt, yy[:, :], XB + [yyB], [rwB])
                    yn, ynB = YY.next()
                    tt("dve", yn[:, :], rw[0:64, 0:128], yy[:, :], ALU.add, [rwB, yyB], [ynB])
                    yy, yyB = yn, ynB
                    if m < 5:
                        rw, rwB = RW.next()
                        mm(rw[0:64, 0:64], XTt, X, XB, [rwB])
                        mm(rw[0:64, 64:128], X, XTt, XB, [rwB])
                        xn, xnB = XX.next()
                        cp("act", xn[:, :], rw[0:64, 0:128], [rwB], [xnB])
                        X, XTt, XB = xn[:, 0:64], xn[:, 64:128], [xnB]
                Pm, Qm = yy[:, 0:64], yy[:, 64:128]
                rw, rwB = RW.next()
                mm(rw[0:64, 0:64], Pm, bh_t, [yyB, tmB], [rwB])
                mm(rw[0:64, 64:128], Pm, arbT, [yyB, m12B], [rwB])
                mts, mtsB = SM.next()
                rhs_, rhsB = SM.next()
                stt("dve", mts[:, :], ident64, gc, rw[0:64, 0:64], ALU.mult, ALU.add, [B_cst, EpB, rwB], [mtsB])
                tt("dve", rhs_[:, :], rw[0:64, 64:128], rt_c, ALU.add, [rwB, ARB], [rhsB])
                rw, rwB = RW.next()
                mm(rw[0:64, 0:64], bh_t, Qm, [tmB, yyB], [rwB], start=True, stop=False)
                mm(rw[0:64, 0:64], kh_t, v_t, [tmB], [rwB], start=False, stop=True)
                gs, gsB = SM.next()
                cp("act", gs[:, :], rw[0:64, 0:64], [rwB], [gsB])
                hc, hcB = HS[h][hcur[h]]
                hn, hnB = HS[h][1 - hcur[h]]
                rw, rwB = RW.next()
                mm(rw[0:64, 0:64], Qm, arbT, [yyB, m12B], [rwB], start=True, stop=False)
                mm(rw[0:64, 0:64], v_t, arkT, [tmB, m12B], [rwB], start=False, stop=False)
                mm(rw[0:64, 0:64], hc[:, :], rhs_[:, :], [hcB, rhsB], [rwB], start=False, stop=True)
                cp("act", yT[:, cc], rw[0:64, 0:64], [rwB], [yTB])
                rw, rwB = RW.next()
                mm(rw[0:64, 0:64], mts[:, :], hc[:, :], [mtsB, hcB], [rwB])
                tt("dve", hn[:, :], rw[0:64, 0:64], gs[:, :], ALU.add, [rwB, gsB], [hnB])
                hcur[h] = 1 - hcur[h]
            if not is_meta:
                ck("C%d" % h)
            if not is_meta:
                pj, pjB = PJ.next()
                mm(pj[0:64, N_], ones64, yT[:, N_], [yTB, B_cst], [pjB])
                dd, ddB = F64.next()
                stt("dve", dd[:, N_], pj[0:64, N_], -1.0 / 64, yT[:, N_], ALU.mult, ALU.add, [pjB, yTB], [ddB])
                dq, dqB = F64.next()
                act(dq[:, N_], dd[:, N_], AF.Square, [ddB], [dqB])
                pj, pjB = PJ.next()
                mm(pj[0:64, N_], ones64, dq[:, N_], [dqB, B_cst], [pjB])
                act(dq[:, N_], pj[0:64, N_], AF.Sqrt, [pjB], [dqB], bias=GN_EPS, scale=1.0 / 64)
                recip(dq[:, N_], dq[:, N_], [dqB], [dqB])
                tt("pool", dd[:, N_], dd[:, N_], dq[:, N_], ALU.mult, [ddB, dqB], [ddB])
                act(dd[:, N_], dd[:, N_], AF.Identity, [ddB, B_pp], [ddB], bias=pp[0:64, pb + 9:pb + 10], scale=pp[0:64, pb + 8:pb + 9])
                tt("pool", dd[:, N_], dd[:, N_], bon[:, N_], ALU.add, [ddB, bonB], [ddB])
                ot, otB = OTR.next()
                tt("pool", ot[:, N_], dd[:, N_], gT[:, N_], ALU.mult, [ddB, gTB], [otB])
                ck("D%d" % h)
                q, gl = gi // NG1, gi % NG1
                r0 = q * 256 + 128 + 64 * h
                dma("sp", oTl[gl].ap()[r0:r0 + 64, :], ot[:, :], [otB], [B_oTl[gl]], "oTw")

    try:
        phase2_group(-1)
        for gi in range(NG2 if stage != 3 else (0 if CUT in (1, 4, 5) else 1)):
            phase2_group(gi)
    except StopBuild:
        P.emit(final_wait_ops=[P.ops[e][-1] for e in ("pe", "act", "dve", "pool")] + [o for o in P.dma_ops if o.dkey in ("win", "rope", "ntg")][-3:])
        st.close()
        return nc
    if stage in (3, 4):
        lastw = [o for o in P.dma_ops if o.dkey == "oTw"]
        lastd = [o for o in P.dma_ops if o.dkey == "dbg"]
        P.emit(final_wait_ops=[lastw[-1]] + lastd[-1:] if lastw else [P.ops[e][-1] for e in ("pe", "act", "dve", "pool")])
        st.close()
        return nc
    for gl in range(NG1):
        allgather(oTl[gl], oTa[gl], B_oTl[gl], B_oTa[gl], "cc2")

    P.fence([B_WG, B_WU, B_WD, B_WO])
    for k in range(8):
        P.add("pool", (lambda k: lambda e: e.dma_start(out=WO[:, k, :].rearrange("p (a b) -> p a b", b=512),
                                                        in_=wout_d[k * 128:(k + 1) * 128, :].rearrange("p (a b) -> p a b", b=512)))(k),
              (), [B_WO], dma="wo")
    load_ffn_weights(1)
    dma("sp", gainA[:], gains_d[2], (), [B_gA], "c1")
    last_out = None
    for g in range(NG1):
        oT, oTB = NT2.next()
        for j in range(8):
            P.add("pool", (lambda j, g, oT: lambda e: e.indirect_dma_start(out=oT[:, j, :], out_offset=None, in_=oTa[g].ap()[:, :],
                                                                           in_offset=bass.IndirectOffsetOnAxis(ap=oidx[:, j:j + 1], axis=0)))(j, g, oT),
                  [B_oTa[g], B_oidx], [oTB], dma="og")
        tiles = []
        outs = []
        xts = []
        for i in range(2):
            xt, xB = XT.next()
            r0 = g * 256 + i * 128
            dma("sp", xt[:], h1s.ap()[r0:r0 + 128, :], [B_h1s[2 * g + i]], [xB], "x")
            xts.append((xt, xB))
        for i in range(2):
            xt, xB = xts[i]
            for half in range(2):
                yb = 2 * i + half
                for k in range(8):
                    mm(PS[yb][:, :], oT[:, k, i * 128:(i + 1) * 128], WO[:, k, half * 512:(half + 1) * 512], [oTB, B_WO], [B_PS[yb]], start=(k == 0), stop=(k == 7))
            ht, hB = HT.next()
            for half in range(2):
                yb = 2 * i + half
                tt("dve", ht[:, half * 512:(half + 1) * 512], PS[yb][:, :], xt[:, half * 512:(half + 1) * 512], ALU.add, [B_PS[yb], xB], [hB])
            tiles.append((ht[:], hB, 128))
            outs.append((xt, xB))
        ffn_group(tiles, gainA, B_gA, outs)
        for i in range(2):
            xt, xB = outs[i]
            r0 = g * 256 + i * 128
            last_out = dma("sp", out_d[r0:r0 + 128, :], xt[:], [xB], (), "out")
    P.emit(final_wait_ops=[last_out])
    st.close()
    return nc


def _consts():
    c = np.zeros((128, NCST), np.float32)
    c[:, 0:128] = np.eye(128)
    c[0:64, 128:192] = 1.0
    c[64:128, 192:256] = 1.0
    rb = np.zeros((128, 128), np.float32)
    for b in (0, 64):
        for m in range(8):
            rb[b + m + 8, b + m] = -1.0
            rb[b + m, b + m + 8] = 1.0
    c[:, 256:384] = rb
    kq = np.arange(128)
    c[:, 384:512] = (kq[:, None] <= kq[None, :]).astype(np.float32)
    j = np.arange(64)
    su = (j[:, None] < j[None, :]).astype(np.float32)
    iu = (j[:, None] <= j[None, :]).astype(np.float32)
    c[0:64, 512:576] = su
    c[0:64, 576:640] = iu
    c[0:64, 640:704] = su
    c[0:64, 704:768] = iu
    c[0:64, 768:832] = su.T
    sm = np.ones(256, np.float32)
    sm[::64] = 0.0
    c[:, 832:1088] = sm[None, :]
    return c


def _rope(LP):
    pos = np.arange(LP, dtype=np.float32)
    inv = (np.float32(500000.0) ** (-np.arange(0, 16, 2, dtype=np.float32) / np.float32(16))).astype(np.float32)
    ang = (pos[:, None] * inv[None, :]).astype(np.float32)
    cos, sin = np.cos(ang).astype(np.float32), np.sin(ang).astype(np.float32)
    C = np.ones((128, 48 + LP), np.float32)
    S = np.zeros((128, 48 + LP), np.float32)
    for b in (0, 64):
        C[b:b + 8, 48:] = cos.T
        C[b + 8:b + 16, 48:] = cos.T
        S[b:b + 8, 48:] = sin.T
        S[b + 8:b + 16, 48:] = sin.T
    return C, S


_CACHE = {}
import os
CUT = int(os.environ.get('CUT', '0'))


def _inmaps(inp):
    f = lambda a: np.ascontiguousarray(np.asarray(a, dtype=np.float32))
    x = f(inp["x"])
    B, T, _ = x.shape
    TQ = T // 4
    NG1 = TQ // 256
    LP = 16 + T
    g = lambda k: f(inp[k])[0]
    cst = _consts()
    ropec, ropes = _rope(LP)
    gains = np.stack([np.broadcast_to(g(k)[None, :], (128, D)) for k in ("ffn1_norm", "mix_norm", "ffn2_norm")]).astype(np.float32).copy()
    w_in = g("w_in")
    w_out = g("w_out")
    mix = g("rw_shift_mix")
    lamv = np.concatenate([np.broadcast_to(g(k)[None, :], (128, 64)) for k in ("da_lambda_q1", "da_lambda_k1", "da_lambda_q2", "da_lambda_k2")], 1).astype(np.float32).copy()
    subln = np.broadcast_to(g("da_subln")[None, :], (128, 128)).astype(np.float32).copy()
    worows = np.concatenate([np.concatenate([np.arange(hd * 128, hd * 128 + 128), 512 + np.arange(hd * 128, hd * 128 + 128)]) for hd in range(4)])
    wout_p = np.ascontiguousarray(w_out[worows, :])
    RWO = 1536
    in_maps = []
    for c in range(8):
        b, q = c // 4, c % 4
        hd = q
        cols = np.concatenate([
            np.arange(hd * 128, hd * 128 + 128), 512 + np.arange(hd * 128, hd * 128 + 128), 1024 + np.arange(hd * 128, hd * 128 + 128),
            RWO + np.arange(hd * 128, hd * 128 + 128), RWO + 512 + np.arange(hd * 128, hd * 128 + 128), RWO + 1024 + np.arange(hd * 128, hd * 128 + 128),
            RWO + 1536 + np.arange(288)])
        win_c = np.ascontiguousarray(w_in[:, cols])
        pp = np.zeros((128, NPP), np.float32)
        pp[:, 0] = np.tile(g("da_q_norm"), 2)
        pp[:, 1] = np.tile(g("da_k_norm"), 2)
        for h in range(2):
            ch = slice(hd * 128 + h * 64, hd * 128 + h * 64 + 64)
            pb = 2 + 10 * h
            pp[0:64, pb + 0] = mix[0:512][ch]
            pp[0:64, pb + 1] = mix[512:1024][ch]
            pp[0:64, pb + 2] = mix[1024:1536][ch]
            pp[0:64, pb + 3] = g("rw_w0")[ch]
            pp[0:64, pb + 4] = g("rw_a0")[ch]
            pp[0:64, pb + 5] = g("rw_k_k")[ch]
            pp[0:64, pb + 6] = g("rw_k_a")[ch]
            pp[0:64, pb + 7] = g("rw_r_k").reshape(-1)[ch]
            pp[0:64, pb + 8] = g("rw_ln_w")[ch]
            pp[0:64, pb + 9] = g("rw_ln_b")[ch]
        pp[0:64, 22] = mix[1536:1600]
        pp[0:64, 23] = mix[1600:1664]
        pp[0:128, 24] = mix[1664:1792]
        pp[0:32, 25] = mix[1792:1824]
        chs = slice(hd * 128, hd * 128 + 128)
        oidx = np.zeros((128, 8), np.int32)
        for hh in range(4):
            for fc in range(2):
                oidx[:, hh * 2 + fc] = hh * 1024 + q * 256 + fc * 128 + np.arange(128)
        in_maps.append({
            "x": np.ascontiguousarray(x[b, q * TQ:(q + 1) * TQ, :]), "meta": f(inp["meta_tokens"]), "gains": gains,
            "wg1": g("ffn1_w_gate"), "wu1": g("ffn1_w_up"), "wd1": g("ffn1_w_down"),
            "wg2": g("ffn2_w_gate"), "wu2": g("ffn2_w_up"), "wd2": g("ffn2_w_down"),
            "win": win_c, "wout": wout_p, "pp": pp,
            "w2": np.ascontiguousarray(g("rw_w2")[:, chs]), "a2": np.ascontiguousarray(g("rw_a2")[:, chs]), "g2w": np.ascontiguousarray(g("rw_g2")[:, chs]),
            "lamv": lamv, "subln": subln, "ropec": ropec, "ropes": ropes, "cst": cst, "oidx": oidx,
        })
    return in_maps, B, T, TQ


def kernel(**inp):
    in_maps, B, T, TQ = _inmaps(inp)
    if T not in _CACHE:
        _CACHE[T] = build(T)
    nc = _CACHE[T]
    res = run_bass_kernel_spmd(nc, in_maps, core_ids=list(range(8)))
    out = np.zeros((B, T, D), np.float32)
    for c in range(8):
        b, q = c // 4, c % 4
        out[b, q * TQ:(q + 1) * TQ, :] = res.results[c]["out"]
    return out
```

```python
import contextlib
import numpy as np
import concourse.bass as bass
import concourse.mybir as mybir
from concourse.bass_utils import run_bass_kernel_spmd

F32 = mybir.dt.float32
BF16 = mybir.dt.bfloat16
I32 = mybir.dt.int32
ALU = mybir.AluOpType
AF = mybir.ActivationFunctionType

D = 1024
FF = 2816
NFF = 22
NPP = 26
NCST = 1088
GN_EPS = 64e-5
EPS = 1e-6


class Buf:
    __slots__ = ("name", "last_w", "readers", "excl")

    def __init__(self, name="", excl=False):
        self.name = name
        self.last_w = None
        self.readers = []
        self.excl = excl


class Op:
    __slots__ = ("eng", "fn", "deps", "signal", "tick", "dkey", "dval", "dinc", "idx")


class Prog:
    ENGS = ("pe", "act", "dve", "pool", "sp")

    def __init__(self, nc):
        self.nc = nc
        self.ops = {e: [] for e in self.ENGS}
        self.n = 0
        self.dcount = {}
        self.dma_ops = []

    def add(self, eng, fn, r=(), w=(), dma=None, dinc=16):
        op = Op()
        op.eng = eng
        op.fn = fn
        op.signal = False
        op.tick = None
        op.dkey = dma
        op.dinc = dinc
        op.dval = None
        op.idx = self.n
        self.n += 1
        snap = self.dcount
        if any(b.excl for b in r):
            w = list(w) + [b for b in r if b.excl]
            r = [b for b in r if not b.excl]
        deps = {}
        for b in r:
            if b.last_w is not None:
                deps[b.last_w.idx] = b.last_w
        for b in w:
            if b.last_w is not None:
                deps[b.last_w.idx] = b.last_w
            for o in b.readers:
                deps[o.idx] = o
        dl = []
        for d in deps.values():
            if d.dkey is None and d.eng == "pe" and eng == "pe" and dma is None:
                continue
            d.signal = True
            dl.append((d, snap[d.dkey] if d.dkey is not None else None))
        op.deps = dl
        if dma is not None:
            self.dcount[dma] = self.dcount.get(dma, 0) + dinc
            op.dval = self.dcount[dma]
            self.dma_ops.append(op)
        for b in r:
            b.readers.append(op)
        for b in w:
            b.last_w = op
            b.readers = []
        self.ops[eng].append(op)
        return op

    def fence(self, bufs):
        lasts = [self.ops[e][-1] for e in self.ENGS if self.ops[e] and self.ops[e][-1].dkey is None]
        last_dma = {}
        for o in self.dma_ops:
            last_dma[o.dkey] = o
        lasts += list(last_dma.values())
        for b in bufs:
            b.readers = list(b.readers) + lasts

    def emit(self, final_wait_ops=()):
        nc = self.nc
        st = contextlib.ExitStack()
        esem = {e: st.enter_context(nc.semaphore("s_" + e)) for e in self.ENGS}
        dsem = {k: st.enter_context(nc.semaphore("d_" + str(k))) for k in self.dcount}
        for d in final_wait_ops:
            d.signal = True
        for e in self.ENGS:
            t = 0
            for op in self.ops[e]:
                if op.dkey is None and op.signal:
                    t += 1
                    op.tick = t
        block = st.enter_context(nc.Block())
        ops = self.ops

        def run(e, eng):
            known = {}
            for op in ops[e]:
                need = {}
                for (d, dv) in op.deps:
                    if d.dkey is not None:
                        s, v = dsem[d.dkey], dv
                    else:
                        s, v = esem[d.eng], d.tick
                    key = id(s)
                    if key not in need or need[key][1] < v:
                        need[key] = (s, v)
                for key, (s, v) in need.items():
                    if known.get(key, 0) >= v:
                        continue
                    eng.wait_ge(s, v)
                    known[key] = v
                inst = op.fn(eng)
                if op.dkey is not None:
                    inst.then_inc(dsem[op.dkey], op.dinc)
                elif op.signal:
                    inst.then_inc(esem[e], 1)
            if e == "sp":
                for d in final_wait_ops:
                    if d.dkey is not None:
                        eng.wait_ge(dsem[d.dkey], d.dval)
                    else:
                        eng.wait_ge(esem[d.eng], d.tick)

        @block.tensor
        def _(eng):
            run("pe", eng)

        @block.scalar
        def _(eng):
            run("act", eng)

        @block.vector
        def _(eng):
            run("dve", eng)

        @block.gpsimd
        def _(eng):
            run("pool", eng)

        @block.sync
        def _(eng):
            run("sp", eng)

        st.close()


class Rot:
    def __init__(self, items):
        self.items = items
        self.i = 0

    def next(self):
        it = self.items[self.i % len(self.items)]
        self.i += 1
        return it


def build(T, stage=9):
    TQ = T // 4
    NG1 = TQ // 256
    NG2 = T // 256
    LP = 16 + T
    NB = T // 128
    nc = bass.Bass("TRN2", target_bir_lowering=False)
    st = contextlib.ExitStack()
    P = Prog(nc)

    def din(name, shape, dt=F32):
        return nc.dram_tensor(name, shape, dt, kind="ExternalInput").ap()

    x_d = din("x", [TQ, D])
    meta_d = din("meta", [16, D])
    gains_d = din("gains", [3, 128, D])
    wg_d = [din("wg1", [D, FF]), din("wg2", [D, FF])]
    wu_d = [din("wu1", [D, FF]), din("wu2", [D, FF])]
    wd_d = [din("wd1", [FF, D]), din("wd2", [FF, D])]
    win_d = din("win", [D, 1056])
    wout_d = din("wout", [D, D])
    pp_d = din("pp", [128, NPP])
    w2_d = din("w2", [64, 128])
    a2_d = din("a2", [64, 128])
    g2_d = din("g2w", [160, 128])
    lam_d = din("lamv", [128, 256])
    subln_d = din("subln", [128, 128])
    ropec_d = din("ropec", [128, 48 + LP])
    ropes_d = din("ropes", [128, 48 + LP])
    cst_d = din("cst", [128, NCST])
    oidx_d = din("oidx", [128, 8], I32)
    out_d = nc.dram_tensor("out", [TQ, D], F32, kind="ExternalOutput").ap()
    dbg = stage < 9
    if stage == 3:
        dbg_d = nc.dram_tensor("dbgo", [128, 8, 128], F32, kind="ExternalOutput").ap()
        dbg_s = nc.dram_tensor("dbgs", [128, 80], F32, kind="ExternalOutput").ap()
    kw_ = dict(kind="ExternalOutput") if dbg else {}
    nTl = [nc.dram_tensor("nTl%d" % g, [D, 256], BF16, **(kw_ if stage == 1 else {})) for g in range(NG1)]
    nTa = [nc.dram_tensor("nTa%d" % g, [4 * D, 256], BF16) for g in range(NG1)]
    oTl = [nc.dram_tensor("oTl%d" % g, [4 * 256, 256], BF16, **(kw_ if stage in (3, 4) else {})) for g in range(NG1)]
    oTa = [nc.dram_tensor("oTa%d" % g, [16 * 256, 256], BF16) for g in range(NG1)]
    h1s = nc.dram_tensor("h1s", [TQ, D], F32, **kw_)
    B_nTl = [Buf("nTl") for _ in range(NG1)]
    B_nTa = [Buf("nTa") for _ in range(NG1)]
    B_oTl = [Buf("oTl") for _ in range(NG1)]
    B_oTa = [Buf("oTa") for _ in range(NG1)]
    RG = [[0, 1, 2, 3], [4, 5, 6, 7]]

    def allgather(src, dst, sB, dB, key):
        return P.add("pool", lambda e: e.collective_compute("AllGather", ALU.bypass, replica_groups=RG,
                                                            ins=[src.ap().bitcast(F32).opt()], outs=[dst.ap().bitcast(F32).opt()]),
                     [sB], [dB], dma=key, dinc=1)
    B_h1s = [Buf("h1s%d" % i) for i in range(TQ // 128)]

    def sb(name, shape, dt=F32):
        return st.enter_context(nc.sbuf_tensor("sb_" + name, shape, dt))

    def mm(out, lhsT, rhs, r, w, start=True, stop=True, skip=False):
        if skip:
            return P.add("pe", lambda e: e.matmul(out, lhsT=lhsT, rhs=rhs, start=start, stop=stop, skip_group_check=True), r, w)
        return P.add("pe", lambda e: e.matmul(out, lhsT=lhsT, rhs=rhs, start=start, stop=stop), r, w)

    def tr(out, in_, ident_ap, r, w):
        return P.add("pe", lambda e: e.transpose(out=out, in_=in_, identity=ident_ap), r, w)

    def act(out, in_, func, r, w, bias=None, scale=None, accum=None):
        kw = {}
        if bias is not None:
            kw["bias"] = bias
        if scale is not None:
            kw["scale"] = scale
        if accum is not None:
            kw["accum_out"] = accum
        return P.add("act", lambda e: e.activation(out=out, in_=in_, func=func, **kw), r, w)

    def cp(eng, out, in_, r, w):
        if eng == "act":
            return P.add("act", lambda e: e.copy(out=out, in_=in_), r, w)
        return P.add(eng, lambda e: e.tensor_copy(out=out, in_=in_), r, w)

    def tt(eng, out, in0, in1, op, r, w):
        return P.add(eng, lambda e: e.tensor_tensor(out=out, in0=in0, in1=in1, op=op), r, w)

    def ts(eng, out, in0, s1, op0, r, w, s2=None, op1=None):
        if s2 is None:
            return P.add(eng, lambda e: e.tensor_scalar(out=out, in0=in0, scalar1=s1, scalar2=None, op0=op0), r, w)
        return P.add(eng, lambda e: e.tensor_scalar(out=out, in0=in0, scalar1=s1, scalar2=s2, op0=op0, op1=op1), r, w)

    def stt(eng, out, in0, scalar, in1, op0, op1, r, w):
        return P.add("dve", lambda e: e.scalar_tensor_tensor(out=out, in0=in0, scalar=scalar, in1=in1, op0=op0, op1=op1), r, w)

    def recip(out, in_, r, w):
        return P.add("dve", lambda e: e.reciprocal(out=out, in_=in_), r, w)

    def memset(eng, ap, val, w):
        return P.add(eng, lambda e: e.memset(ap, val), (), w)

    def dma(q, out, in_, r, w, key):
        return P.add(q, lambda e: e.dma_start(out=out, in_=in_), r, w, dma=key)

    PS = [st.enter_context(nc.psum_tensor("ps%d" % i, [128, 512], F32)) for i in range(8)]
    B_PS = [Buf("ps%d" % i, excl=True) for i in range(8)]

    cst = sb("cst", [128, NCST]); B_cst = Buf("cst")
    ident = cst[:, 0:128]
    onesblk = cst[:, 128:256]
    ones64 = cst[0:64, 128:192]
    rblk = cst[:, 256:384]
    tri_f = cst[:, 384:512]
    suiu2 = cst[0:64, 512:768]
    slm = cst[0:64, 768:832]
    scanmask = cst[0:64, 832:1088]
    ident64 = cst[0:64, 0:64]
    tri_b = sb("tri_b", [128, 128], BF16); B_trib = Buf("trib")
    pp = sb("pp", [128, NPP]); B_pp = Buf("pp")
    ppd = sb("ppd", [128, 8]); B_ppd = Buf("ppd")
    lamv = sb("lamv", [128, 256]); B_lam = Buf("lam")
    lamt = sb("lamt", [128, 72]); B_lamt = Buf("lamt")
    subln = sb("subln", [128, 128]); B_subln = Buf("subln")
    w2s = sb("w2s", [64, 128]); a2s = sb("a2s", [64, 128]); g2a = sb("g2a", [128, 128]); g2b = sb("g2b", [32, 128]); B_lw = Buf("lw")
    gainA = sb("gainA", [128, D]); B_gA = Buf("gA")
    gainM = sb("gainM", [128, D]); B_gM = Buf("gM")
    nTm = sb("nTm", [128, 8, 64], BF16); B_nTm = Buf("nTm")
    nTmA = sb("nTmA", [128, 8, 16], BF16); B_nTmA = Buf("nTmA")
    oidx = sb("oidx", [128, 8], I32); B_oidx = Buf("oidx")
    stat = sb("stat", [128, 32]); statR = Rot([(stat[:, i:i + 1], Buf("st%d" % i)) for i in range(32)])

    dma("sp", cst[:], cst_d, (), [B_cst], "c0")
    dma("sp", pp[:], pp_d, (), [B_pp], "c0")
    dma("sp", lamv[:], lam_d, (), [B_lam], "c0")
    dma("sp", subln[:], subln_d, (), [B_subln], "c0")
    dma("sp", w2s[:], w2_d, (), [B_lw], "c0")
    dma("sp", a2s[:], a2_d, (), [B_lw], "c0")
    dma("sp", g2a[:], g2_d[0:128, :], (), [B_lw], "c0")
    dma("sp", g2b[:], g2_d[128:160, :], (), [B_lw], "c0")
    dma("sp", gainA[:], gains_d[0], (), [B_gA], "c0")
    dma("sp", gainM[:], gains_d[1], (), [B_gM], "c0")
    dma("sp", oidx[:], oidx_d, (), [B_oidx], "c0")
    cp("dve", tri_b[:], tri_f, [B_cst], [B_trib])
    memset("pool", nTm[:], 0.0, [B_nTm])
    for h in range(2):
        ts("dve", ppd[:, h:h + 1], pp[:, 5 + 10 * h:6 + 10 * h], -1.0, ALU.mult, [B_pp], [B_ppd])
        ts("dve", ppd[:, 2 + h:3 + h], pp[:, 8 + 10 * h:9 + 10 * h], -1.0, ALU.mult, [B_pp], [B_ppd], s2=1.0, op1=ALU.add)
    for i in range(2):
        tt("dve", lamt[:, 0:64], lamv[:, 128 * i:128 * i + 64], lamv[:, 128 * i + 64:128 * i + 128], ALU.mult, [B_lam], [B_lamt])
        P.add("dve", (lambda i: lambda e: e.reduce_sum(out=lamt[:, 64 + i:65 + i], in_=lamt[:, 0:64], axis=mybir.AxisListType.X))(i), [B_lamt], [B_lamt])
    act(lamt[:, 66:68], lamt[:, 64:66], AF.Exp, [B_lamt], [B_lamt])
    tt("dve", lamt[:, 68:69], lamt[:, 67:68], lamt[:, 66:67], ALU.subtract, [B_lamt], [B_lamt])
    ts("dve", lamt[:, 70:71], lamt[:, 68:69], -0.2, ALU.add, [B_lamt], [B_lamt])
    neglam = lamt[:, 70:71]
    ts("dve", subln[:], subln[:], 0.8, ALU.mult, [B_subln], [B_subln])

    AR_BYTES = 151552 + 5120
    arena = sb("arena", [128, AR_BYTES // 4])

    class Arena:
        def __init__(self):
            self.off = 0

        def alloc(self, parts, free, dt):
            esz = 4 if dt in (F32, I32) else 2
            n = int(np.prod(free))
            nb = (n * esz + 3) // 4 * 4
            assert self.off + nb <= AR_BYTES, (self.off, nb)
            a = arena[:, self.off // 4:(self.off + nb) // 4]
            self.off += nb
            if dt != F32:
                a = a.bitcast(dt)
            a = a[0:parts, 0:n]
            if len(free) == 2:
                a = a.rearrange("p (a b) -> p a b", b=free[1])
            elif len(free) == 3:
                a = a.rearrange("p (a b c) -> p a b c", b=free[1], c=free[2])
            return a

    A13 = Arena()
    WG = A13.alloc(128, [8, FF], BF16)
    WU = A13.alloc(128, [8, FF], BF16)
    WD = A13.alloc(128, [NFF, D], BF16)
    WO = A13.alloc(128, [8, D], BF16)
    B_WG, B_WU, B_WD, B_WO = Buf("WG"), Buf("WU"), Buf("WD"), Buf("WO")

    def load_ffn_weights(i):
        for k in range(8):
            P.add("pool", (lambda k: lambda e: e.dma_start(out=WG[:, k, :].rearrange("p (a b) -> p a b", b=704),
                                                            in_=wg_d[i][k * 128:(k + 1) * 128, :].rearrange("p (a b) -> p a b", b=704)))(k),
                  (), [B_WG], dma="wg")
        for k in range(8):
            P.add("pool", (lambda k: lambda e: e.dma_start(out=WU[:, k, :].rearrange("p (a b) -> p a b", b=704),
                                                            in_=wu_d[i][k * 128:(k + 1) * 128, :].rearrange("p (a b) -> p a b", b=704)))(k),
                  (), [B_WU], dma="wu")
        for f in range(NFF):
            P.add("pool", (lambda f: lambda e: e.dma_start(out=WD[:, f, :].rearrange("p (a b) -> p a b", b=512),
                                                            in_=wd_d[i][f * 128:(f + 1) * 128, :].rearrange("p (a b) -> p a b", b=512)))(f),
                  (), [B_WD], dma="wd")

    load_ffn_weights(0)

    XT = Rot([(sb("xt%d" % i, [128, D]), Buf("xt%d" % i)) for i in range(3)])
    HT = Rot([(sb("ht%d" % i, [128, D]), Buf("ht%d" % i)) for i in range(2)])
    NF = Rot([(sb("nf%d" % i, [128, D]), Buf("nf%d" % i)) for i in range(1)])
    NT = Rot([(sb("nt%d" % i, [128, 8, 256], BF16), Buf("nt%d" % i)) for i in range(1)])
    NT2 = Rot([(sb("nt2_%d" % i, [128, 8, 256], BF16), Buf("nt2_%d" % i)) for i in range(1)])
    SG = Rot([(sb("sg%d" % i, [128, 256]), Buf("sg%d" % i)) for i in range(2)])
    ACTT = Rot([(sb("actt%d" % i, [128, 256], BF16), Buf("actt%d" % i)) for i in range(2)])

    def norm_T(h, hB, npart, gain, gB, dst, dB, c0):
        nf, nfB = NF.next()
        ss, ssB = statR.next()
        sd, sdB = statR.next()
        memset("pool", ss[0:npart], 0.0, [ssB])
        act(nf[0:npart, :], h, AF.Square, [hB], [nfB, ssB], accum=ss[0:npart])
        act(sd[0:npart], ss[0:npart], AF.Sqrt, [ssB], [sdB], bias=EPS, scale=1.0 / D)
        recip(sd[0:npart], sd[0:npart], [sdB], [sdB])
        stt("dve", nf[0:npart, :], h, sd[0:npart], gain[0:npart, :], ALU.mult, ALU.mult, [hB, sdB, gB], [nfB])
        for b in range(2):
            bank = 6 + b
            for j in range(4):
                k = 4 * b + j
                tr(PS[bank][:, j * 128:j * 128 + npart], nf[0:npart, k * 128:(k + 1) * 128], ident[0:npart, 0:npart], [nfB, B_cst], [B_PS[bank]])
            src = PS[bank][:, :].rearrange("p (a b) -> p a b", b=128)[:, :, 0:npart]
            cp("act" if b == 0 else "dve", dst[:, 4 * b:4 * b + 4, c0:c0 + npart], src, [B_PS[bank]], [dB])

    def ffn_group(tiles, gain, gB, outs):
        nT, nTB = NT.next()
        offs = []
        N = 0
        for (h, hB, npart) in tiles:
            offs.append(N)
            norm_T(h, hB, npart, gain, gB, nT, nTB, N)
            N += npart

        def gu(f):
            bank = 4 + f % 2
            for k in range(8):
                mm(PS[bank][:, 0:N], WG[:, k, f * 128:(f + 1) * 128], nT[:, k, 0:N], [B_WG, nTB], [B_PS[bank]], start=(k == 0), stop=(k == 7))
            for k in range(8):
                mm(PS[bank][:, 256:256 + N], WU[:, k, f * 128:(f + 1) * 128], nT[:, k, 0:N], [B_WU, nTB], [B_PS[bank]], start=(k == 0), stop=(k == 7))

        gu(0)
        for f in range(NFF):
            if f + 1 < NFF:
                gu(f + 1)
            bank = 4 + f % 2
            sg, sgB = SG.next()
            at, atB = ACTT.next()
            act(sg[:, 0:N], PS[bank][:, 0:N], AF.Silu, [B_PS[bank]], [sgB])
            tt("dve", at[:, 0:N], sg[:, 0:N], PS[bank][:, 256:256 + N], ALU.mult, [sgB, B_PS[bank]], [atB])
            for i, (h, hB, npart) in enumerate(tiles):
                for half in range(2):
                    yb = 2 * i + half
                    mm(PS[yb][0:npart, :], at[:, offs[i]:offs[i] + npart], WD[:, f, half * 512:(half + 1) * 512], [atB, B_WD], [B_PS[yb]],
                       start=(f == 0), stop=(f == NFF - 1))
        for i, (h, hB, npart) in enumerate(tiles):
            o, oB = outs[i]
            for half in range(2):
                yb = 2 * i + half
                stt("dve", o[0:npart, half * 512:(half + 1) * 512], PS[yb][0:npart, :], 0.5, h[:, half * 512:(half + 1) * 512], ALU.mult, ALU.add,
                    [B_PS[yb], hB], [oB])

    xm, xmB = XT.next()
    dma("sp", xm[0:16, :], meta_d, (), [xmB], "x")
    hm, hmB = HT.next()
    ffn_group([(xm[0:16, :], xmB, 16)], gainA, B_gA, [(hm, hmB)])
    norm_T(hm[0:16, :], hmB, 16, gainM, B_gM, nTm, B_nTm, 48)
    cp("pool", nTmA[:, :, :], nTm[:, :, 48:64], [B_nTm], [B_nTmA])
    for g in range(NG1):
        tiles = []
        outs = []
        for i in range(2):
            xt, xB = XT.next()
            r0 = g * 256 + i * 128
            dma("sp", xt[:], x_d[r0:r0 + 128, :], (), [xB], "x")
            tiles.append((xt[:], xB, 128))
            outs.append(HT.next())
        ffn_group(tiles, gainA, B_gA, outs)
        n2, n2B = NT2.next()
        for i in range(2):
            ht, hB = outs[i]
            r0 = g * 256 + i * 128
            dma("sp", h1s.ap()[r0:r0 + 128, :], ht[:], [hB], [B_h1s[2 * g + i]], "h1w")
            norm_T(ht[:], hB, 128, gainM, B_gM, n2, n2B, i * 128)
        dma("sp", nTl[g].ap().rearrange("(k p) t -> p k t", p=128), n2[:], [n2B], [B_nTl[g]], "nTw")
        if stage != 1:
            cc1 = allgather(nTl[g], nTa[g], B_nTl[g], B_nTa[g], "cc1")
    if stage == 1:
        lastw = [o for o in P.dma_ops if o.dkey in ("nTw", "h1w")]
        P.emit(final_wait_ops=[lastw[-1]] + [o for o in lastw if o.dkey == "h1w"][-1:])
        st.close()
        return nc
    if stage == 2:
        P.emit(final_wait_ops=[cc1])
        st.close()
        return nc

    A2 = Arena()
    tenants = []

    def ten(parts, free, dt, name):
        b = Buf(name)
        tenants.append(b)
        return A2.alloc(parts, free, dt), b

    WIN, B_WIN = ten(128, [8, 1056], BF16, "win")
    KT, _ = ten(128, [LP], BF16, "KT")
    B_KT = [Buf("kt%d" % j) for j in range(NB + 1)]
    VX, _ = ten(128, [NB, 129], BF16, "VX")
    B_VX = [Buf("vx%d" % j) for j in range(NB)]
    VM, B_VM = ten(16, [129], BF16, "VM")
    tenants += B_KT + B_VX
    NTG = Rot([ten(128, [8, 256], BF16, "ntg%d" % i) for i in range(2)])
    RC = Rot([ten(128, [256], F32, "rc%d" % i) for i in range(1)])
    RS = Rot([ten(128, [256], F32, "rs%d" % i) for i in range(1)])
    QTB = Rot([ten(128, [256], BF16, "qtb%d" % i) for i in range(2)])
    PTB = Rot([ten(128, [512], BF16, "ptb%d" % i) for i in range(3)])
    F128 = Rot([ten(128, [256], F32, "f128_%d" % i) for i in range(6)])
    OTD = Rot([ten(128, [256], BF16, "otd%d" % i) for i in range(2)])
    ATT = Rot([ten(128, [128], F32, "att%d" % i) for i in range(4)])
    RECS, B_RECS = ten(128, [8], F32, "recs")
    RAW = {}
    for nm, parts in (("r0", 64), ("k0", 64), ("v0", 64), ("r1", 64), ("k1", 64), ("v1", 64), ("wlo", 64), ("alo", 64), ("ga", 128), ("gb", 32)):
        RAW[nm] = ten(parts, [257], F32, "raw_" + nm) + (parts,)
    F64 = Rot([ten(64, [256], F32, "f64_%d" % i) for i in range(37)])
    ALOS = ten(64, [256], F32, "alos")
    WLOS = ten(64, [256], F32, "wlos")
    THB = ten(64, [256], F32, "thb")
    SGA, B_SGA = ten(128, [256], F32, "sga")
    SGB, B_SGB = ten(32, [256], F32, "sgb")
    ARb = Rot([ten(64, [512], F32, "ar%d" % i) for i in range(2)])
    OTR = Rot([ten(64, [256], BF16, "otr%d" % i) for i in range(2)])
    TMP = [ten(64, [3, 64], F32, "tm%d" % i) for i in range(4)]
    M12P = [ten(64, [256], F32, "m12_%d" % i) for i in range(4)]
    XXP = [[ten(64, [128], F32, "xx%d_%d" % (i, j)) for j in range(2)] for i in range(4)]
    YYP = [[ten(64, [128], F32, "yy%d_%d" % (i, j)) for j in range(2)] for i in range(4)]
    SMP = [[ten(64, [64], F32, "sm%d_%d" % (i, j)) for j in range(3)] for i in range(4)]
    HS = [[ten(64, [64], F32, "hs%d_%d" % (h, i)) for i in range(2)] for h in range(2)]

    print("arena phase2 bytes", A2.off, "of", AR_BYTES)
    P.fence(tenants)
    SKIP = int(os.environ.get("SKIP", "0"))
    if not SKIP & 1:
        for k in range(8):
            P.add("pool", (lambda k: lambda e: e.dma_start(out=WIN[:, k, :].rearrange("p (a b) -> p a b", b=528),
                                                            in_=win_d[k * 128:(k + 1) * 128, :].rearrange("p (a b) -> p a b", b=528)))(k),
                  (), [B_WIN], dma="win")
    if not SKIP & 2:
        for h in range(2):
            memset("pool", HS[h][0][0], 0.0, [HS[h][0][1]])
        for nm in RAW:
            memset("pool", RAW[nm][0][:, 0:1], 0.0, [RAW[nm][1]])
    if not SKIP & 4:
        memset("pool", VX[:, :, 128:129], 1.0, B_VX)
    if not SKIP & 8:
        memset("pool", VM[:, 128:129], 1.0, [B_VM])

    PJ = Rot([(PS[4][:, 0:256], B_PS[4]), (PS[5][:, 0:256], B_PS[5]), (PS[6][:, 0:256], B_PS[6]), (PS[7][:, 0:256], B_PS[7])])
    RW = PJ
    STS = [[(PS[c][:, 0:256], B_PS[c]), (PS[c][:, 0:256], B_PS[c])] for c in range(2)]
    hcur = [0, 0]
    blk_count = [0]

    def proj(col0, M, nT, nTB, ntok):
        pj, pjB = PJ.next()
        for k in range(8):
            mm(pj[0:M, 0:ntok], WIN[:, k, col0:col0 + M], nT[:, k, 0:ntok], [B_WIN, nTB], [pjB], start=(k == 0), stop=(k == 7))
        return pj, pjB

    def tokshift(nm, psrc, psB, ntok, mixcol, dst=None):
        raw, rawB, parts = RAW[nm]
        cp("act", raw[:, 1:ntok + 1], psrc[0:parts, 0:ntok], [psB], [rawB])
        d, dB = (F128.next() if parts > 64 else F64.next())
        o, oB = dst if dst is not None else (F128.next() if parts > 64 else F64.next())
        tt("pool", d[0:parts, 0:ntok], raw[:, 0:ntok], raw[:, 1:ntok + 1], ALU.subtract, [rawB], [dB])
        stt("pool", o[0:parts, 0:ntok], d[0:parts, 0:ntok], pp[0:parts, mixcol:mixcol + 1], raw[:, 1:ntok + 1], ALU.mult, ALU.add, [dB, rawB, B_pp], [oB])
        cp("pool", raw[:, 0:1], raw[:, ntok:ntok + 1], [rawB, dB, oB], [rawB])
        return o, oB

    class StopBuild(Exception):
        pass
    ckc = [0]
    CUTN = int(os.environ.get("CUTN", "0"))

    CUTTAG = os.environ.get("CUTTAG", "")

    def ck(tag=None):
        if tag is not None:
            if tag == CUTTAG:
                raise StopBuild()
            return
        ckc[0] += 1
        if ckc[0] == CUTN:
            raise StopBuild()

    def phase2_group(gi):
        is_meta = gi < 0
        if CUT == 5:
            return
        ntok = 64 if is_meta else 256
        nch = ntok // 64
        if is_meta:
            nT, nTB = nTm, B_nTm
            tcol = 0
        else:
            nT, nTB = NTG.next()
            q, gl = gi // NG1, gi % NG1
            src = nTa[gl].ap().rearrange("(q k p) t -> q p k t", q=4, k=8, p=128)[q]
            dma("sp", nT[:], src, [B_nTa[gl]], [nTB], "ntg")
            tcol = 64 + gi * 256
        rc, rcB = RC.next()
        rs, rsB = RS.next()
        dma("sp", rc[:, 0:ntok], ropec_d[:, tcol:tcol + ntok], (), [rcB], "rope")
        dma("sp", rs[:, 0:ntok], ropes_d[:, tcol:tcol + ntok], (), [rsB], "rope")

        qtb = None
        for which in (["k"] if is_meta else ["q", "k"]):
            col0 = 0 if which == "q" else 128
            gcol = 0 if which == "q" else 1
            pj, pjB = proj(col0, 128, nT, nTB, ntok)
            ck()
            sq, sqB = F128.next()
            act(sq[:, 0:ntok], pj[:, 0:ntok], AF.Square, [pjB], [sqB])
            ck()
            p2, p2B = PJ.next()
            mm(p2[:, 0:ntok], onesblk, sq[:, 0:ntok], [sqB, B_cst], [p2B])
            ck()
            rn, rnB = F128.next()
            act(rn[:, 0:ntok], p2[:, 0:ntok], AF.Sqrt, [p2B], [rnB], bias=EPS, scale=1.0 / 64)
            recip(rn[:, 0:ntok], rn[:, 0:ntok], [rnB], [rnB])
            ck()
            qn, qnB = F128.next()
            stt("dve", qn[:, 0:ntok], pj[:, 0:ntok], pp[:, gcol:gcol + 1], rn[:, 0:ntok], ALU.mult, ALU.mult, [pjB, rnB, B_pp], [qnB])
            ck()
            p3, p3B = PJ.next()
            mm(p3[:, 0:ntok], rblk, qn[:, 0:ntok], [qnB, B_cst], [p3B])
            ck()
            t1, t1B = F128.next()
            tt("pool", t1[:, 0:ntok], qn[:, 0:ntok], rc[:, 0:ntok], ALU.mult, [qnB, rcB], [t1B])
            t2, t2B = F128.next()
            tt("dve", t2[:, 0:ntok], p3[:, 0:ntok], rs[:, 0:ntok], ALU.mult, [p3B, rsB], [t2B])
            ck()
            if which == "q":
                qtb, qtbB = QTB.next()
                tt("pool", qtb[:, 0:ntok], t1[:, 0:ntok], t2[:, 0:ntok], ALU.add, [t1B, t2B], [qtbB])
            elif is_meta:
                tt("pool", KT[:, 0:16], t1[:, 48:64], t2[:, 48:64], ALU.add, [t1B, t2B], [B_KT[0]])
            else:
                for i in range(2):
                    p0 = 16 + gi * 256 + i * 128
                    tt("pool", KT[:, p0:p0 + 128], t1[:, i * 128:(i + 1) * 128], t2[:, i * 128:(i + 1) * 128], ALU.add, [t1B, t2B], [B_KT[1 + 2 * gi + i]])
        ck()
        if is_meta:
            if os.environ.get("VARS"):
                PJ.next()
            pj, pjB = PJ.next()
            for k in range(8):
                if os.environ.get("VARM") == "rhs0":
                    mm(pj[0:64, 0:128], nT[:, k, 0:64], WIN[:, k, 0:128], [B_WIN, nTB], [pjB], start=(k == 0), stop=(k == 7))
                elif os.environ.get("VARM") == "swap":
                    mm(pj[0:128, 0:64], WIN[:, k, 256:384], nT[:, k, 0:64], [B_WIN, nTB], [pjB], start=(k == 0), stop=(k == 7))
                elif os.environ.get("VARM") == "64":
                    mm(pj[0:64, 0:128], nT[:, k, 0:64], WIN[:, k, 256:384], [B_WIN, nTB], [pjB], start=(k == 0), stop=(k == 7))
                else:
                    mm(pj[0:16, 0:128], nTmA[:, k, :], WIN[:, k, 256:384], [B_WIN, B_nTmA], [pjB], start=(k == 0), stop=(k == 7))
            ck()
            cp("act", VM[:, 0:128], pj[0:16, 0:128], [pjB], [B_VM])
            ck()
        else:
            for i in range(2):
                pj, pjB = PJ.next()
                for k in range(8):
                    mm(pj[:, 0:128], nT[:, k, i * 128:(i + 1) * 128], WIN[:, k, 256:384], [B_WIN, nTB], [pjB], start=(k == 0), stop=(k == 7))
                cp("act", VX[:, 2 * gi + i, 0:128], pj[:, 0:128], [pjB], [B_VX[2 * gi + i]])

        if not is_meta and CUT != 3:
            blocks = [(-1, 16)] + [(j, 128) for j in range(2 * gi + 2)]
            first = [True, True]
            lastj = [2 * gi, 2 * gi + 1]
            for (j, kb) in blocks:
                sl = blk_count[0] % 2
                blk_count[0] += 1
                kB = B_KT[0] if j < 0 else B_KT[1 + j]
                k0 = 0 if j < 0 else 16 + j * 128
                vap = VM[:, :] if j < 0 else VX[:, j, :]
                vB = B_VM if j < 0 else B_VX[j]
                pt, ptB = PTB.next()
                for c in range(2):
                    s_ap, sB = STS[c][sl]
                    mm(s_ap[0:kb, :], KT[c * 64:(c + 1) * 64, k0:k0 + kb], qtb[c * 64:(c + 1) * 64, :], [kB, qtbB], [sB])
                    act(pt[0:kb, c * 256:(c + 1) * 256], s_ap[0:kb, :], AF.Exp, [sB], [ptB], scale=0.125)
                for it in range(2):
                    if j == lastj[it]:
                        for c in range(2):
                            a = pt[:, c * 256 + it * 128:c * 256 + (it + 1) * 128]
                            tt("pool", a, a, tri_b[:], ALU.mult, [ptB, B_trib], [ptB])
                for c in range(2):
                    for it in range(2):
                        if j > lastj[it]:
                            continue
                        mm(PS[2 + c][:, it * 129:(it + 1) * 129], pt[0:kb, c * 256 + it * 128:c * 256 + (it + 1) * 128], vap[0:kb, :],
                           [ptB, vB], [B_PS[2 + c]], start=first[c], stop=(j == lastj[it]), skip=True)
                        first[c] = False
            otd, otdB = OTD.next()
            for it in range(2):
                for c in range(2):
                    recip(RECS[:, 2 * it + c:2 * it + c + 1], PS[2 + c][:, it * 129 + 128:it * 129 + 129], [B_PS[2 + c]], [B_RECS])
                o1, o1B = ATT.next()
                t2, t2B = ATT.next()
                ts("dve", o1[:], PS[2][:, it * 129:it * 129 + 128], RECS[:, 2 * it:2 * it + 1], ALU.mult, [B_PS[2], B_RECS], [o1B])
                ts("dve", t2[:], PS[3][:, it * 129:it * 129 + 128], RECS[:, 2 * it + 1:2 * it + 2], ALU.mult, [B_PS[3], B_RECS, B_lamt], [t2B],
                   s2=neglam, op1=ALU.mult)
                od, odB = ATT.next()
                tt("pool", od[:], o1[:], t2[:], ALU.add, [o1B, t2B], [odB])
                ss, ssB = statR.next()
                sd, sdB = statR.next()
                memset("pool", ss, 0.0, [ssB])
                act(o1[:], od[:], AF.Square, [odB], [o1B, ssB], accum=ss)
                act(sd, ss, AF.Sqrt, [ssB], [sdB], bias=EPS, scale=1.0 / 128)
                recip(sd, sd, [sdB], [sdB])
                on, onB = ATT.next()
                stt("dve", on[:], od[:], sd, subln[:], ALU.mult, ALU.mult, [odB, sdB, B_subln], [onB])
                if stage == 3 and gi == 0:
                    dma("sp", dbg_d[:, 4 * it + 0, :], o1[:], [o1B], (), "dbg")
                    dma("sp", dbg_d[:, 4 * it + 1, :], t2[:], [t2B], (), "dbg")
                    dma("sp", dbg_d[:, 4 * it + 2, :], od[:], [odB], (), "dbg")
                    dma("sp", dbg_d[:, 4 * it + 3, :], on[:], [onB], (), "dbg")
                    if it == 1:
                        dma("sp", dbg_s[:, 0:8], RECS[:, :], [B_RECS], (), "dbg")
                        dma("sp", dbg_s[:, 8:80], lamt[:, :], [B_lamt], (), "dbg")
                rw, rwB = RW.next()
                tr(rw[:, 0:128], on[:], ident, [onB, B_cst], [rwB])
                cp("act", otd[:, it * 128:(it + 1) * 128], rw[:, 0:128], [rwB], [otdB])
            q, gl = gi // NG1, gi % NG1
            r0 = q * 256
            dma("sp", oTl[gl].ap()[r0:r0 + 128, :], otd[:], [otdB], [B_oTl[gl]], "oTw")

        if CUT == 1 or (CUT == 2 and not is_meta):
            return
        pj, pjB = proj(768, 64, nT, nTB, ntok)
        wlo, wloB = tokshift("wlo", pj, pjB, ntok, 22, dst=WLOS)
        pj, pjB = proj(832, 64, nT, nTB, ntok)
        alo, aloB = tokshift("alo", pj, pjB, ntok, 23, dst=ALOS)
        pj, pjB = proj(896, 128, nT, nTB, ntok)
        ga, gaB = tokshift("ga", pj, pjB, ntok, 24)
        pj, pjB = proj(1024, 32, nT, nTB, ntok)
        gb, gbB = tokshift("gb", pj, pjB, ntok, 25)
        if not is_meta:
            ck("A")
        th, thB = THB
        act(th[:, 0:ntok], wlo[0:64, 0:ntok], AF.Tanh, [wloB], [thB])
        act(SGA[:, 0:ntok], ga[:, 0:ntok], AF.Sigmoid, [gaB], [B_SGA])
        act(SGB[:, 0:ntok], gb[0:32, 0:ntok], AF.Sigmoid, [gbB], [B_SGB])
        for h in range(2):
            pb = 2 + 10 * h
            pj, pjB = proj(384 + 64 * h, 64, nT, nTB, ntok)
            r_s, rB = tokshift("r%d" % h, pj, pjB, ntok, pb + 0)
            pj, pjB = proj(512 + 64 * h, 64, nT, nTB, ntok)
            k_s, kB_ = tokshift("k%d" % h, pj, pjB, ntok, pb + 1)
            pj, pjB = proj(640 + 64 * h, 64, nT, nTB, ntok)
            v_s, vB_ = tokshift("v%d" % h, pj, pjB, ntok, pb + 2)
            N_ = slice(0, ntok)
            pj, pjB = PJ.next()
            mm(pj[0:64, N_], w2s[:, 64 * h:64 * h + 64], th[:, N_], [B_lw, thB], [pjB])
            e1, e1B = F64.next()
            act(e1[:, N_], pj[0:64, N_], AF.Exp, [pjB, B_ppd], [e1B], bias=ppd[0:64, h:h + 1], scale=-1.0)
            act(e1[:, N_], e1[:, N_], AF.Ln, [e1B], [e1B], bias=1.0)
            e2, e2B = F64.next()
            act(e2[:, N_], e1[:, N_], AF.Exp, [e1B], [e2B], bias=-0.5, scale=-1.0)
            pj, pjB = PJ.next()
            mm(pj[0:64, N_], a2s[:, 64 * h:64 * h + 64], alo[0:64, N_], [B_lw, aloB], [pjB])
            lr, lrB = F64.next()
            act(lr[:, N_], pj[0:64, N_], AF.Sigmoid, [pjB, B_pp], [lrB], bias=pp[0:64, pb + 4:pb + 5])
            pj, pjB = PJ.next()
            mm(pj[0:64, N_], g2a[:, 64 * h:64 * h + 64], SGA[:, N_], [B_lw, B_SGA], [pjB], start=True, stop=False)
            mm(pj[0:64, N_], g2b[:, 64 * h:64 * h + 64], SGB[:, N_], [B_lw, B_SGB], [pjB], start=False, stop=True)
            gT, gTB = F64.next()
            cp("act", gT[:, N_], pj[0:64, N_], [pjB], [gTB])
            kk, kkB = F64.next()
            ts("pool", kk[:, N_], k_s[0:64, N_], pp[0:64, pb + 5:pb + 6], ALU.mult, [kB_, B_pp], [kkB])
            ksq, ksqB = F64.next()
            act(ksq[:, N_], kk[:, N_], AF.Square, [kkB], [ksqB])
            pj, pjB = PJ.next()
            mm(pj[0:64, N_], ones64, ksq[:, N_], [ksqB, B_cst], [pjB])
            rn, rnB = F64.next()
            act(rn[:, N_], pj[0:64, N_], AF.Sqrt, [pjB], [rnB])
            ts("dve", rn[:, N_], rn[:, N_], 1e-12, ALU.max, [rnB], [rnB])
            recip(rn[:, N_], rn[:, N_], [rnB], [rnB])
            kkn, kknB = F64.next()
            tt("pool", kkn[:, N_], kk[:, N_], rn[:, N_], ALU.mult, [kkB, rnB], [kknB])
            t1, t1B = F64.next()
            ts("dve", t1[:, N_], lr[:, N_], pp[0:64, pb + 6:pb + 7], ALU.mult, [lrB, B_pp, B_ppd], [t1B], s2=ppd[0:64, 2 + h:3 + h], op1=ALU.add)
            kmod, kmB = F64.next()
            tt("pool", kmod[:, N_], k_s[0:64, N_], t1[:, N_], ALU.mult, [kB_, t1B], [kmB])
            bv, bvB = F64.next()
            tt("pool", bv[:, N_], kkn[:, N_], lr[:, N_], ALU.mult, [kknB, lrB], [bvB])
            rk, rkB = F64.next()
            stt("pool", rk[:, N_], r_s[0:64, N_], pp[0:64, pb + 7:pb + 8], kmod[:, N_], ALU.mult, ALU.mult, [rB, kmB, B_pp], [rkB])
            pj, pjB = PJ.next()
            mm(pj[0:64, N_], ones64, rk[:, N_], [rkB, B_cst], [pjB])
            bon, bonB = F64.next()
            tt("dve", bon[:, N_], pj[0:64, N_], v_s[0:64, N_], ALU.mult, [pjB, vB_], [bonB])
            gneg, gnB = F64.next()
            P.add("dve", (lambda o, d0, d1: lambda e: e.tensor_tensor_scan(out=o, data0=d0, data1=d1, initial=0.0, op0=ALU.mult, op1=ALU.add))(
                gneg[:, N_], scanmask[:, N_], e2[:, N_]), [e2B, B_cst], [gnB])
            Ep, EpB = F64.next()
            Em, EmB = F64.next()
            Ea, EaB = F64.next()
            act(Ep[:, N_], gneg[:, N_], AF.Exp, [gnB], [EpB], scale=-1.0)
            act(Em[:, N_], gneg[:, N_], AF.Exp, [gnB], [EmB])
            tt("pool", Ea[:, N_], e2[:, N_], gneg[:, N_], ALU.subtract, [e2B, gnB], [EaB])
            act(Ea[:, N_], Ea[:, N_], AF.Exp, [EaB], [EaB])
            AR, ARB = ARb.next()
            ARv = AR[:, 0:2 * ntok].rearrange("p (c t) -> p c t", t=128)
            stt("pool", ARv[:, :, 0:64], kkn[:, N_].rearrange("p (c t) -> p c t", t=64), -1.0, Ea[:, N_].rearrange("p (c t) -> p c t", t=64),
                ALU.mult, ALU.mult, [kknB, EaB], [ARB])
            tt("pool", ARv[:, :, 64:128], r_s[0:64, N_].rearrange("p (c t) -> p c t", t=64), Ep[:, N_].rearrange("p (c t) -> p c t", t=64), ALU.mult,
               [rB, EpB], [ARB])
            BT, BTB = F64.next()
            KTl, KTlB = F64.next()
            tt("pool", BT[:, N_], bv[:, N_], Em[:, N_], ALU.mult, [bvB, EmB], [BTB])
            tt("pool", KTl[:, N_], kmod[:, N_], Em[:, N_], ALU.mult, [kmB, EmB], [KTlB])
            BH, BHB = F64.next()
            KH, KHB = F64.next()
            for c in range(nch):
                cc = slice(c * 64, (c + 1) * 64)
                gc = Ep[:, c * 64 + 63:c * 64 + 64]
                ts("pool", BH[:, cc], BT[:, cc], gc, ALU.mult, [BTB, EpB], [BHB])
                ts("pool", KH[:, cc], KTl[:, cc], gc, ALU.mult, [KTlB, EpB], [KHB])
            yT, yTB = F64.next()
            if not is_meta:
                ck("B%d" % h)
            CH = range(nch)
            cc = [slice(c * 64, (c + 1) * 64) for c in CH]
            at_c = [AR[:, c * 128:c * 128 + 64] for c in CH]
            rt_c = [AR[:, c * 128 + 64:c * 128 + 128] for c in CH]
            ar_c = [AR[:, c * 128:(c + 1) * 128] for c in CH]
            gcs = [Ep[:, c * 64 + 63:c * 64 + 64] for c in CH]
            yy = [YYP[c][0] for c in CH]
            yyB = [YYP[c][1] for c in CH]
            for c in CH:
                rw, rwB = RW.next()
                tr(rw[0:64, 0:64], BH[:, cc[c]], ident64, [BHB, B_cst], [rwB])
                tr(rw[0:64, 64:128], KH[:, cc[c]], ident64, [KHB, B_cst], [rwB])
                tr(rw[0:64, 128:192], v_s[0:64, cc[c]], ident64, [vB_, B_cst], [rwB])
                tr(rw[0:64, 192:256], at_c[c], ident64, [ARB, B_cst], [rwB])
                tm, tmB = TMP[c]
                cp("act", tm[:, :, :], rw[0:64, 0:192].rearrange("p (a b) -> p a b", b=64), [rwB], [tmB])
                cp("dve", yy[c][0][:, 0:64], rw[0:64, 192:256], [rwB], [yy[c][1]])
            for c in CH:
                rw, rwB = RW.next()
                mm(rw[0:64, 0:128], BT[:, cc[c]], ar_c[c], [BTB, ARB], [rwB])
                mm(rw[0:64, 128:256], KTl[:, cc[c]], ar_c[c], [KTlB, ARB], [rwB])
                m12, m12B = M12P[c]
                tt("dve", m12[:, :], rw[0:64, 0:256], suiu2, ALU.mult, [rwB, B_cst], [m12B])
            Xs, XTs, XBs = [None] * nch, [None] * nch, [None] * nch
            for c in CH:
                tm, tmB = TMP[c]
                m12, m12B = M12P[c]
                rw, rwB = RW.next()
                mm(rw[0:64, 0:64], at_c[c], BT[:, cc[c]], [ARB, BTB], [rwB])
                mm(rw[0:64, 64:128], m12[:, 128:192], tm[:, 2, :], [m12B, tmB], [rwB])
                xx, xxB = XXP[c][0]
                tt("dve", xx[:, 0:64], rw[0:64, 0:64], slm, ALU.mult, [rwB, B_cst], [xxB])
                cp("act", yy[c][0][:, 64:128], rw[0:64, 64:128], [rwB], [yy[c][1]])
                Xs[c], XTs[c], XBs[c] = xx[:, 0:64], m12[:, 0:64], [xxB, m12B]
            cur = [0] * nch
            for m in range(6):
                for c in CH:
                    ya, yaB = YYP[c][cur[c]]
                    yb, ybB = YYP[c][1 - cur[c]]
                    rw, rwB = RW.next()
                    mm(rw[0:64, 0:128], XTs[c], ya[:, :], XBs[c] + [yaB], [rwB])
                    tt("dve", yb[:, :], rw[0:64, 0:128], ya[:, :], ALU.add, [rwB, yaB], [ybB])
                    cur[c] = 1 - cur[c]
                if m < 5:
                    for c in CH:
                        rw, rwB = RW.next()
                        mm(rw[0:64, 0:64], XTs[c], Xs[c], XBs[c], [rwB])
                        mm(rw[0:64, 64:128], Xs[c], XTs[c], XBs[c], [rwB])
                        xn, xnB = XXP[c][(m + 1) % 2]
                        cp("act", xn[:, :], rw[0:64, 0:128], [rwB], [xnB])
                        Xs[c], XTs[c], XBs[c] = xn[:, 0:64], xn[:, 64:128], [xnB]
            for c in CH:
                yf, yfB = YYP[c][cur[c]]
                tm, tmB = TMP[c]
                m12, m12B = M12P[c]
                mts, mtsB = SMP[c][0]
                rhs_, rhsB = SMP[c][1]
                rw, rwB = RW.next()
                mm(rw[0:64, 0:64], yf[:, 0:64], tm[:, 0, :], [yfB, tmB], [rwB])
                mm(rw[0:64, 64:128], yf[:, 0:64], m12[:, 64:128], [yfB, m12B], [rwB])
                stt("dve", mts[:, :], ident64, gcs[c], rw[0:64, 0:64], ALU.mult, ALU.add, [B_cst, EpB, rwB], [mtsB])
                tt("dve", rhs_[:, :], rw[0:64, 64:128], rt_c[c], ALU.add, [rwB, ARB], [rhsB])
            for c in CH:
                yf, yfB = YYP[c][cur[c]]
                tm, tmB = TMP[c]
                gs, gsB = SMP[c][2]
                rw, rwB = RW.next()
                mm(rw[0:64, 0:64], tm[:, 0, :], yf[:, 64:128], [tmB, yfB], [rwB], start=True, stop=False)
                mm(rw[0:64, 0:64], tm[:, 1, :], tm[:, 2, :], [tmB], [rwB], start=False, stop=True)
                cp("act", gs[:, :], rw[0:64, 0:64], [rwB], [gsB])
            for c in CH:
                yf, yfB = YYP[c][cur[c]]
                tm, tmB = TMP[c]
                m12, m12B = M12P[c]
                mts, mtsB = SMP[c][0]
                rhs_, rhsB = SMP[c][1]
                gs, gsB = SMP[c][2]
                hc, hcB = HS[h][hcur[h]]
                hn, hnB = HS[h][1 - hcur[h]]
                rw, rwB = RW.next()
                mm(rw[0:64, 0:64], yf[:, 64:128], m12[:, 64:128], [yfB, m12B], [rwB], start=True, stop=False)
                mm(rw[0:64, 0:64], tm[:, 2, :], m12[:, 192:256], [tmB, m12B], [rwB], start=False, stop=False)
                mm(rw[0:64, 0:64], hc[:, :], rhs_[:, :], [hcB, rhsB], [rwB], start=False, stop=True)
                rw2, rw2B = RW.next()
                mm(rw2[0:64, 0:64], mts[:, :], hc[:, :], [mtsB, hcB], [rw2B])
                tt("dve", hn[:, :], rw2[0:64, 0:64], gs[:, :], ALU.add, [rw2B, gsB], [hnB])
                cp("act", yT[:, cc[c]], rw[0:64, 0:64], [rwB], [yTB])
                hcur[h] = 1 - hcur[h]
            if not is_meta:
                ck("C%d" % h)
            if not is_meta:
                pj, pjB = PJ.next()
                mm(pj[0:64, N_], ones64, yT[:, N_], [yTB, B_cst], [pjB])
                dd, ddB = F64.next()
                stt("dve", dd[:, N_], pj[0:64, N_], -1.0 / 64, yT[:, N_], ALU.mult, ALU.add, [pjB, yTB], [ddB])
                dq, dqB = F64.next()
                act(dq[:, N_], dd[:, N_], AF.Square, [ddB], [dqB])
                pj, pjB = PJ.next()
                mm(pj[0:64, N_], ones64, dq[:, N_], [dqB, B_cst], [pjB])
                act(dq[:, N_], pj[0:64, N_], AF.Sqrt, [pjB], [dqB], bias=GN_EPS, scale=1.0 / 64)
                recip(dq[:, N_], dq[:, N_], [dqB], [dqB])
                tt("pool", dd[:, N_], dd[:, N_], dq[:, N_], ALU.mult, [ddB, dqB], [ddB])
                act(dd[:, N_], dd[:, N_], AF.Identity, [ddB, B_pp], [ddB], bias=pp[0:64, pb + 9:pb + 10], scale=pp[0:64, pb + 8:pb + 9])
                tt("pool", dd[:, N_], dd[:, N_], bon[:, N_], ALU.add, [ddB, bonB], [ddB])
                ot, otB = OTR.next()
                tt("pool", ot[:, N_], dd[:, N_], gT[:, N_], ALU.mult, [ddB, gTB], [otB])
                ck("D%d" % h)
                q, gl = gi // NG1, gi % NG1
                r0 = q * 256 + 128 + 64 * h
                dma("sp", oTl[gl].ap()[r0:r0 + 64, :], ot[:, :], [otB], [B_oTl[gl]], "oTw")

    try:
        phase2_group(-1)
        for gi in range(NG2 if stage != 3 else (0 if CUT in (1, 4, 5) else 1)):
            phase2_group(gi)
    except StopBuild:
        P.emit(final_wait_ops=[P.ops[e][-1] for e in ("pe", "act", "dve", "pool")] + [o for o in P.dma_ops if o.dkey in ("win", "rope", "ntg")][-3:])
        st.close()
        return nc
    if stage in (3, 4):
        lastw = [o for o in P.dma_ops if o.dkey == "oTw"]
        lastd = [o for o in P.dma_ops if o.dkey == "dbg"]
        P.emit(final_wait_ops=[lastw[-1]] + lastd[-1:] if lastw else [P.ops[e][-1] for e in ("pe", "act", "dve", "pool")])
        st.close()
        return nc
    for gl in range(NG1):
        allgather(oTl[gl], oTa[gl], B_oTl[gl], B_oTa[gl], "cc2")

    P.fence([B_WG, B_WU, B_WD, B_WO])
    for k in range(8):
        P.add("pool", (lambda k: lambda e: e.dma_start(out=WO[:, k, :].rearrange("p (a b) -> p a b", b=512),
                                                        in_=wout_d[k * 128:(k + 1) * 128, :].rearrange("p (a b) -> p a b", b=512)))(k),
              (), [B_WO], dma="wo")
    load_ffn_weights(1)
    dma("sp", gainA[:], gains_d[2], (), [B_gA], "c1")
    last_out = None
    for g in range(NG1):
        oT, oTB = NT2.next()
        for j in range(8):
            P.add("pool", (lambda j, g, oT: lambda e: e.indirect_dma_start(out=oT[:, j, :], out_offset=None, in_=oTa[g].ap()[:, :],
                                                                           in_offset=bass.IndirectOffsetOnAxis(ap=oidx[:, j:j + 1], axis=0)))(j, g, oT),
                  [B_oTa[g], B_oidx], [oTB], dma="og")
        tiles = []
        outs = []
        xts = []
        for i in range(2):
            xt, xB = XT.next()
            r0 = g * 256 + i * 128
            dma("sp", xt[:], h1s.ap()[r0:r0 + 128, :], [B_h1s[2 * g + i]], [xB], "x")
            xts.append((xt, xB))
        for i in range(2):
            xt, xB = xts[i]
            for half in range(2):
                yb = 2 * i + half
                for k in range(8):
                    mm(PS[yb][:, :], oT[:, k, i * 128:(i + 1) * 128], WO[:, k, half * 512:(half + 1) * 512], [oTB, B_WO], [B_PS[yb]], start=(k == 0), stop=(k == 7))
            ht, hB = HT.next()
            for half in range(2):
                yb = 2 * i + half
                tt("dve", ht[:, half * 512:(half + 1) * 512], PS[yb][:, :], xt[:, half * 512:(half + 1) * 512], ALU.add, [B_PS[yb], xB], [hB])
            tiles.append((ht[:], hB, 128))
            outs.append((xt, xB))
        ffn_group(tiles, gainA, B_gA, outs)
        for i in range(2):
            xt, xB = outs[i]
            r0 = g * 256 + i * 128
            last_out = dma("sp", out_d[r0:r0 + 128, :], xt[:], [xB], (), "out")
    P.emit(final_wait_ops=[last_out])
    st.close()
    return nc


def _consts():
    c = np.zeros((128, NCST), np.float32)
    c[:, 0:128] = np.eye(128)
    c[0:64, 128:192] = 1.0
    c[64:128, 192:256] = 1.0
    rb = np.zeros((128, 128), np.float32)
    for b in (0, 64):
        for m in range(8):
            rb[b + m + 8, b + m] = -1.0
            rb[b + m, b + m + 8] = 1.0
    c[:, 256:384] = rb
    kq = np.arange(128)
    c[:, 384:512] = (kq[:, None] <= kq[None, :]).astype(np.float32)
    j = np.arange(64)
    su = (j[:, None] < j[None, :]).astype(np.float32)
    iu = (j[:, None] <= j[None, :]).astype(np.float32)
    c[0:64, 512:576] = su
    c[0:64, 576:640] = iu
    c[0:64, 640:704] = su
    c[0:64, 704:768] = iu
    c[0:64, 768:832] = su.T
    sm = np.ones(256, np.float32)
    sm[::64] = 0.0
    c[:, 832:1088] = sm[None, :]
    return c


def _rope(LP):
    pos = np.arange(LP, dtype=np.float32)
    inv = (np.float32(500000.0) ** (-np.arange(0, 16, 2, dtype=np.float32) / np.float32(16))).astype(np.float32)
    ang = (pos[:, None] * inv[None, :]).astype(np.float32)
    cos, sin = np.cos(ang).astype(np.float32), np.sin(ang).astype(np.float32)
    C = np.ones((128, 48 + LP), np.float32)
    S = np.zeros((128, 48 + LP), np.float32)
    for b in (0, 64):
        C[b:b + 8, 48:] = cos.T
        C[b + 8:b + 16, 48:] = cos.T
        S[b:b + 8, 48:] = sin.T
        S[b + 8:b + 16, 48:] = sin.T
    return C, S


_CACHE = {}
import os
CUT = int(os.environ.get('CUT', '0'))


def _inmaps(inp):
    f = lambda a: np.ascontiguousarray(np.asarray(a, dtype=np.float32))
    x = f(inp["x"])
    B, T, _ = x.shape
    TQ = T // 4
    NG1 = TQ // 256
    LP = 16 + T
    g = lambda k: f(inp[k])[0]
    cst = _consts()
    ropec, ropes = _rope(LP)
    gains = np.stack([np.broadcast_to(g(k)[None, :], (128, D)) for k in ("ffn1_norm", "mix_norm", "ffn2_norm")]).astype(np.float32).copy()
    w_in = g("w_in")
    w_out = g("w_out")
    mix = g("rw_shift_mix")
    lamv = np.concatenate([np.broadcast_to(g(k)[None, :], (128, 64)) for k in ("da_lambda_q1", "da_lambda_k1", "da_lambda_q2", "da_lambda_k2")], 1).astype(np.float32).copy()
    subln = np.broadcast_to(g("da_subln")[None, :], (128, 128)).astype(np.float32).copy()
    worows = np.concatenate([np.concatenate([np.arange(hd * 128, hd * 128 + 128), 512 + np.arange(hd * 128, hd * 128 + 128)]) for hd in range(4)])
    wout_p = np.ascontiguousarray(w_out[worows, :])
    RWO = 1536
    in_maps = []
    for c in range(8):
        b, q = c // 4, c % 4
        hd = q
        cols = np.concatenate([
            np.arange(hd * 128, hd * 128 + 128), 512 + np.arange(hd * 128, hd * 128 + 128), 1024 + np.arange(hd * 128, hd * 128 + 128),
            RWO + np.arange(hd * 128, hd * 128 + 128), RWO + 512 + np.arange(hd * 128, hd * 128 + 128), RWO + 1024 + np.arange(hd * 128, hd * 128 + 128),
            RWO + 1536 + np.arange(288)])
        win_c = np.ascontiguousarray(w_in[:, cols])
        pp = np.zeros((128, NPP), np.float32)
        pp[:, 0] = np.tile(g("da_q_norm"), 2)
        pp[:, 1] = np.tile(g("da_k_norm"), 2)
        for h in range(2):
            ch = slice(hd * 128 + h * 64, hd * 128 + h * 64 + 64)
            pb = 2 + 10 * h
            pp[0:64, pb + 0] = mix[0:512][ch]
            pp[0:64, pb + 1] = mix[512:1024][ch]
            pp[0:64, pb + 2] = mix[1024:1536][ch]
            pp[0:64, pb + 3] = g("rw_w0")[ch]
            pp[0:64, pb + 4] = g("rw_a0")[ch]
            pp[0:64, pb + 5] = g("rw_k_k")[ch]
            pp[0:64, pb + 6] = g("rw_k_a")[ch]
            pp[0:64, pb + 7] = g("rw_r_k").reshape(-1)[ch]
            pp[0:64, pb + 8] = g("rw_ln_w")[ch]
            pp[0:64, pb + 9] = g("rw_ln_b")[ch]
        pp[0:64, 22] = mix[1536:1600]
        pp[0:64, 23] = mix[1600:1664]
        pp[0:128, 24] = mix[1664:1792]
        pp[0:32, 25] = mix[1792:1824]
        chs = slice(hd * 128, hd * 128 + 128)
        oidx = np.zeros((128, 8), np.int32)
        for hh in range(4):
            for fc in range(2):
                oidx[:, hh * 2 + fc] = hh * 1024 + q * 256 + fc * 128 + np.arange(128)
        in_maps.append({
            "x": np.ascontiguousarray(x[b, q * TQ:(q + 1) * TQ, :]), "meta": f(inp["meta_tokens"]), "gains": gains,
            "wg1": g("ffn1_w_gate"), "wu1": g("ffn1_w_up"), "wd1": g("ffn1_w_down"),
            "wg2": g("ffn2_w_gate"), "wu2": g("ffn2_w_up"), "wd2": g("ffn2_w_down"),
            "win": win_c, "wout": wout_p, "pp": pp,
            "w2": np.ascontiguousarray(g("rw_w2")[:, chs]), "a2": np.ascontiguousarray(g("rw_a2")[:, chs]), "g2w": np.ascontiguousarray(g("rw_g2")[:, chs]),
            "lamv": lamv, "subln": subln, "ropec": ropec, "ropes": ropes, "cst": cst, "oidx": oidx,
        })
    return in_maps, B, T, TQ


def kernel(**inp):
    in_maps, B, T, TQ = _inmaps(inp)
    if T not in _CACHE:
        _CACHE[T] = build(T)
    nc = _CACHE[T]
    res = run_bass_kernel_spmd(nc, in_maps, core_ids=list(range(8)))
    out = np.zeros((B, T, D), np.float32)
    for c in range(8):
        b, q = c // 4, c % 4
        out[b, q * TQ:(q + 1) * TQ, :] = res.results[c]["out"]
    return out
```

```python
import contextlib
import numpy as np
import concourse.bass as bass
import concourse.mybir as mybir
from concourse.bass_utils import run_bass_kernel_spmd

F32 = mybir.dt.float32
BF16 = mybir.dt.bfloat16
I32 = mybir.dt.int32
ALU = mybir.AluOpType
AF = mybir.ActivationFunctionType

D = 1024
FF = 2816
NFF = 22
NPP = 26
NCST = 1088
GN_EPS = 64e-5
EPS = 1e-6


class Buf:
    __slots__ = ("name", "last_w", "readers", "excl")

    def __init__(self, name="", excl=False):
        self.name = name
        self.last_w = None
        self.readers = []
        self.excl = excl


class Op:
    __slots__ = ("eng", "fn", "deps", "signal", "tick", "dkey", "dval", "dinc", "idx")


class Prog:
    ENGS = ("pe", "act", "dve", "pool", "sp")

    def __init__(self, nc):
        self.nc = nc
        self.ops = {e: [] for e in self.ENGS}
        self.n = 0
        self.dcount = {}
        self.dma_ops = []

    def add(self, eng, fn, r=(), w=(), dma=None, dinc=16):
        op = Op()
        op.eng = eng
        op.fn = fn
        op.signal = False
        op.tick = None
        op.dkey = dma
        op.dinc = dinc
        op.dval = None
        op.idx = self.n
        self.n += 1
        snap = self.dcount
        if any(b.excl for b in r):
            w = list(w) + [b for b in r if b.excl]
            r = [b for b in r if not b.excl]
        deps = {}
        for b in r:
            if b.last_w is not None:
                deps[b.last_w.idx] = b.last_w
        for b in w:
            if b.last_w is not None:
                deps[b.last_w.idx] = b.last_w
            for o in b.readers:
                deps[o.idx] = o
        dl = []
        for d in deps.values():
            if d.dkey is None and d.eng == "pe" and eng == "pe" and dma is None:
                continue
            d.signal = True
            dl.append((d, snap[d.dkey] if d.dkey is not None else None))
        op.deps = dl
        if dma is not None:
            self.dcount[dma] = self.dcount.get(dma, 0) + dinc
            op.dval = self.dcount[dma]
            self.dma_ops.append(op)
        for b in r:
            b.readers.append(op)
        for b in w:
            b.last_w = op
            b.readers = []
        self.ops[eng].append(op)
        return op

    def fence(self, bufs):
        lasts = [self.ops[e][-1] for e in self.ENGS if self.ops[e] and self.ops[e][-1].dkey is None]
        last_dma = {}
        for o in self.dma_ops:
            last_dma[o.dkey] = o
        lasts += list(last_dma.values())
        for b in bufs:
            b.readers = list(b.readers) + lasts

    def emit(self, final_wait_ops=()):
        nc = self.nc
        st = contextlib.ExitStack()
        esem = {e: st.enter_context(nc.semaphore("s_" + e)) for e in self.ENGS}
        dsem = {k: st.enter_context(nc.semaphore("d_" + str(k))) for k in self.dcount}
        for d in final_wait_ops:
            d.signal = True
        for e in self.ENGS:
            t = 0
            for op in self.ops[e]:
                if op.dkey is None and op.signal:
                    t += 1
                    op.tick = t
        block = st.enter_context(nc.Block())
        ops = self.ops

        def run(e, eng):
            known = {}
            for op in ops[e]:
                need = {}
                for (d, dv) in op.deps:
                    if d.dkey is not None:
                        s, v = dsem[d.dkey], dv
                    else:
                        s, v = esem[d.eng], d.tick
                    key = id(s)
                    if key not in need or need[key][1] < v:
                        need[key] = (s, v)
                for key, (s, v) in need.items():
                    if known.get(key, 0) >= v:
                        continue
                    eng.wait_ge(s, v)
                    known[key] = v
                inst = op.fn(eng)
                if op.dkey is not None:
                    inst.then_inc(dsem[op.dkey], op.dinc)
                elif op.signal:
                    inst.then_inc(esem[e], 1)
            if e == "sp":
                for d in final_wait_ops:
                    if d.dkey is not None:
                        eng.wait_ge(dsem[d.dkey], d.dval)
                    else:
                        eng.wait_ge(esem[d.eng], d.tick)

        @block.tensor
        def _(eng):
            run("pe", eng)

        @block.scalar
        def _(eng):
            run("act", eng)

        @block.vector
        def _(eng):
            run("dve", eng)

        @block.gpsimd
        def _(eng):
            run("pool", eng)

        @block.sync
        def _(eng):
            run("sp", eng)

        st.close()


class Rot:
    def __init__(self, items):
        self.items = items
        self.i = 0

    def next(self):
        it = self.items[self.i % len(self.items)]
        self.i += 1
        return it


def build(T, stage=9):
    TQ = T // 4
    NG1 = TQ // 256
    NG2 = T // 256
    LP = 16 + T
    NB = T // 128
    nc = bass.Bass("TRN2", target_bir_lowering=False)
    st = contextlib.ExitStack()
    P = Prog(nc)

    def din(name, shape, dt=F32):
        return nc.dram_tensor(name, shape, dt, kind="ExternalInput").ap()

    x_d = din("x", [TQ, D])
    meta_d = din("meta", [16, D])
    gains_d = din("gains", [3, 128, D])
    wg_d = [din("wg1", [D, FF]), din("wg2", [D, FF])]
    wu_d = [din("wu1", [D, FF]), din("wu2", [D, FF])]
    wd_d = [din("wd1", [FF, D]), din("wd2", [FF, D])]
    win_d = din("win", [D, 1056])
    wout_d = din("wout", [D, D])
    pp_d = din("pp", [128, NPP])
    w2_d = din("w2", [64, 128])
    a2_d = din("a2", [64, 128])
    g2_d = din("g2w", [160, 128])
    lam_d = din("lamv", [128, 256])
    subln_d = din("subln", [128, 128])
    ropec_d = din("ropec", [128, 48 + LP])
    ropes_d = din("ropes", [128, 48 + LP])
    cst_d = din("cst", [128, NCST])
    oidx_d = din("oidx", [128, 8], I32)
    out_d = nc.dram_tensor("out", [TQ, D], F32, kind="ExternalOutput").ap()
    dbg = stage < 9
    if stage == 3:
        dbg_d = nc.dram_tensor("dbgo", [128, 8, 128], F32, kind="ExternalOutput").ap()
        dbg_s = nc.dram_tensor("dbgs", [128, 80], F32, kind="ExternalOutput").ap()
    kw_ = dict(kind="ExternalOutput") if dbg else {}
    nTl = [nc.dram_tensor("nTl%d" % g, [D, 256], BF16, **(kw_ if stage == 1 else {})) for g in range(NG1)]
    nTa = [nc.dram_tensor("nTa%d" % g, [4 * D, 256], BF16) for g in range(NG1)]
    oTl = [nc.dram_tensor("oTl%d" % g, [4 * 256, 256], BF16, **(kw_ if stage in (3, 4) else {})) for g in range(NG1)]
    oTa = [nc.dram_tensor("oTa%d" % g, [16 * 256, 256], BF16) for g in range(NG1)]
    h1s = nc.dram_tensor("h1s", [TQ, D], F32, **kw_)
    B_nTl = [Buf("nTl") for _ in range(NG1)]
    B_nTa = [Buf("nTa") for _ in range(NG1)]
    B_oTl = [Buf("oTl") for _ in range(NG1)]
    B_oTa = [Buf("oTa") for _ in range(NG1)]
    RG = [[0, 1, 2, 3], [4, 5, 6, 7]]

    def allgather(src, dst, sB, dB, key):
        return P.add("pool", lambda e: e.collective_compute("AllGather", ALU.bypass, replica_groups=RG,
                                                            ins=[src.ap().bitcast(F32).opt()], outs=[dst.ap().bitcast(F32).opt()]),
                     [sB], [dB], dma=key, dinc=1)
    B_h1s = [Buf("h1s%d" % i) for i in range(TQ // 128)]

    def sb(name, shape, dt=F32):
        return st.enter_context(nc.sbuf_tensor("sb_" + name, shape, dt))

    def mm(out, lhsT, rhs, r, w, start=True, stop=True, skip=False):
        if skip:
            return P.add("pe", lambda e: e.matmul(out, lhsT=lhsT, rhs=rhs, start=start, stop=stop, skip_group_check=True), r, w)
        return P.add("pe", lambda e: e.matmul(out, lhsT=lhsT, rhs=rhs, start=start, stop=stop), r, w)

    def tr(out, in_, ident_ap, r, w):
        return P.add("pe", lambda e: e.transpose(out=out, in_=in_, identity=ident_ap), r, w)

    def act(out, in_, func, r, w, bias=None, scale=None, accum=None):
        kw = {}
        if bias is not None:
            kw["bias"] = bias
        if scale is not None:
            kw["scale"] = scale
        if accum is not None:
            kw["accum_out"] = accum
        return P.add("act", lambda e: e.activation(out=out, in_=in_, func=func, **kw), r, w)

    def cp(eng, out, in_, r, w):
        if eng == "act":
            return P.add("act", lambda e: e.copy(out=out, in_=in_), r, w)
        return P.add(eng, lambda e: e.tensor_copy(out=out, in_=in_), r, w)

    def tt(eng, out, in0, in1, op, r, w):
        return P.add(eng, lambda e: e.tensor_tensor(out=out, in0=in0, in1=in1, op=op), r, w)

    def ts(eng, out, in0, s1, op0, r, w, s2=None, op1=None):
        if s2 is None:
            return P.add(eng, lambda e: e.tensor_scalar(out=out, in0=in0, scalar1=s1, scalar2=None, op0=op0), r, w)
        return P.add(eng, lambda e: e.tensor_scalar(out=out, in0=in0, scalar1=s1, scalar2=s2, op0=op0, op1=op1), r, w)

    def stt(eng, out, in0, scalar, in1, op0, op1, r, w):
        return P.add("dve", lambda e: e.scalar_tensor_tensor(out=out, in0=in0, scalar=scalar, in1=in1, op0=op0, op1=op1), r, w)

    def recip(out, in_, r, w):
        return P.add("dve", lambda e: e.reciprocal(out=out, in_=in_), r, w)

    def memset(eng, ap, val, w):
        return P.add(eng, lambda e: e.memset(ap, val), (), w)

    def dma(q, out, in_, r, w, key):
        return P.add(q, lambda e: e.dma_start(out=out, in_=in_), r, w, dma=key)

    PS = [st.enter_context(nc.psum_tensor("ps%d" % i, [128, 512], F32)) for i in range(8)]
    B_PS = [Buf("ps%d" % i, excl=True) for i in range(8)]

    cst = sb("cst", [128, NCST]); B_cst = Buf("cst")
    ident = cst[:, 0:128]
    onesblk = cst[:, 128:256]
    ones64 = cst[0:64, 128:192]
    rblk = cst[:, 256:384]
    tri_f = cst[:, 384:512]
    suiu2 = cst[0:64, 512:768]
    slm = cst[0:64, 768:832]
    scanmask = cst[0:64, 832:1088]
    ident64 = cst[0:64, 0:64]
    tri_b = sb("tri_b", [128, 128], BF16); B_trib = Buf("trib")
    pp = sb("pp", [128, NPP]); B_pp = Buf("pp")
    ppd = sb("ppd", [128, 8]); B_ppd = Buf("ppd")
    lamv = sb("lamv", [128, 256]); B_lam = Buf("lam")
    lamt = sb("lamt", [128, 72]); B_lamt = Buf("lamt")
    subln = sb("subln", [128, 128]); B_subln = Buf("subln")
    w2s = sb("w2s", [64, 128]); a2s = sb("a2s", [64, 128]); g2a = sb("g2a", [128, 128]); g2b = sb("g2b", [32, 128]); B_lw = Buf("lw")
    gainA = sb("gainA", [128, D]); B_gA = Buf("gA")
    gainM = sb("gainM", [128, D]); B_gM = Buf("gM")
    nTm = sb("nTm", [128, 8, 64], BF16); B_nTm = Buf("nTm")
    nTmA = sb("nTmA", [128, 8, 16], BF16); B_nTmA = Buf("nTmA")
    oidx = sb("oidx", [128, 8], I32); B_oidx = Buf("oidx")
    stat = sb("stat", [128, 32]); statR = Rot([(stat[:, i:i + 1], Buf("st%d" % i)) for i in range(32)])

    dma("sp", cst[:], cst_d, (), [B_cst], "c0")
    dma("sp", pp[:], pp_d, (), [B_pp], "c0")
    dma("sp", lamv[:], lam_d, (), [B_lam], "c0")
    dma("sp", subln[:], subln_d, (), [B_subln], "c0")
    dma("sp", w2s[:], w2_d, (), [B_lw], "c0")
    dma("sp", a2s[:], a2_d, (), [B_lw], "c0")
    dma("sp", g2a[:], g2_d[0:128, :], (), [B_lw], "c0")
    dma("sp", g2b[:], g2_d[128:160, :], (), [B_lw], "c0")
    dma("sp", gainA[:], gains_d[0], (), [B_gA], "c0")
    dma("sp", gainM[:], gains_d[1], (), [B_gM], "c0")
    dma("sp", oidx[:], oidx_d, (), [B_oidx], "c0")
    cp("dve", tri_b[:], tri_f, [B_cst], [B_trib])
    memset("pool", nTm[:], 0.0, [B_nTm])
    for h in range(2):
        ts("dve", ppd[:, h:h + 1], pp[:, 5 + 10 * h:6 + 10 * h], -1.0, ALU.mult, [B_pp], [B_ppd])
        ts("dve", ppd[:, 2 + h:3 + h], pp[:, 8 + 10 * h:9 + 10 * h], -1.0, ALU.mult, [B_pp], [B_ppd], s2=1.0, op1=ALU.add)
    for i in range(2):
        tt("dve", lamt[:, 0:64], lamv[:, 128 * i:128 * i + 64], lamv[:, 128 * i + 64:128 * i + 128], ALU.mult, [B_lam], [B_lamt])
        P.add("dve", (lambda i: lambda e: e.reduce_sum(out=lamt[:, 64 + i:65 + i], in_=lamt[:, 0:64], axis=mybir.AxisListType.X))(i), [B_lamt], [B_lamt])
    act(lamt[:, 66:68], lamt[:, 64:66], AF.Exp, [B_lamt], [B_lamt])
    tt("dve", lamt[:, 68:69], lamt[:, 67:68], lamt[:, 66:67], ALU.subtract, [B_lamt], [B_lamt])
    ts("dve", lamt[:, 70:71], lamt[:, 68:69], -0.2, ALU.add, [B_lamt], [B_lamt])
    neglam = lamt[:, 70:71]
    ts("dve", subln[:], subln[:], 0.8, ALU.mult, [B_subln], [B_subln])

    AR_BYTES = 151552 + 5120
    arena = sb("arena", [128, AR_BYTES // 4])

    class Arena:
        def __init__(self):
            self.off = 0

        def alloc(self, parts, free, dt):
            esz = 4 if dt in (F32, I32) else 2
            n = int(np.prod(free))
            nb = (n * esz + 3) // 4 * 4
            assert self.off + nb <= AR_BYTES, (self.off, nb)
            a = arena[:, self.off // 4:(self.off + nb) // 4]
            self.off += nb
            if dt != F32:
                a = a.bitcast(dt)
            a = a[0:parts, 0:n]
            if len(free) == 2:
                a = a.rearrange("p (a b) -> p a b", b=free[1])
            elif len(free) == 3:
                a = a.rearrange("p (a b c) -> p a b c", b=free[1], c=free[2])
            return a

    A13 = Arena()
    WG = A13.alloc(128, [8, FF], BF16)
    WU = A13.alloc(128, [8, FF], BF16)
    WD = A13.alloc(128, [NFF, D], BF16)
    WO = A13.alloc(128, [8, D], BF16)
    B_WG, B_WU, B_WD, B_WO = Buf("WG"), Buf("WU"), Buf("WD"), Buf("WO")

    def load_ffn_weights(i):
        for k in range(8):
            P.add("pool", (lambda k: lambda e: e.dma_start(out=WG[:, k, :].rearrange("p (a b) -> p a b", b=704),
                                                            in_=wg_d[i][k * 128:(k + 1) * 128, :].rearrange("p (a b) -> p a b", b=704)))(k),
                  (), [B_WG], dma="wg")
        for k in range(8):
            P.add("pool", (lambda k: lambda e: e.dma_start(out=WU[:, k, :].rearrange("p (a b) -> p a b", b=704),
                                                            in_=wu_d[i][k * 128:(k + 1) * 128, :].rearrange("p (a b) -> p a b", b=704)))(k),
                  (), [B_WU], dma="wu")
        for f in range(NFF):
            P.add("pool", (lambda f: lambda e: e.dma_start(out=WD[:, f, :].rearrange("p (a b) -> p a b", b=512),
                                                            in_=wd_d[i][f * 128:(f + 1) * 128, :].rearrange("p (a b) -> p a b", b=512)))(f),
                  (), [B_WD], dma="wd")

    load_ffn_weights(0)

    XT = Rot([(sb("xt%d" % i, [128, D]), Buf("xt%d" % i)) for i in range(3)])
    HT = Rot([(sb("ht%d" % i, [128, D]), Buf("ht%d" % i)) for i in range(2)])
    NF = Rot([(sb("nf%d" % i, [128, D]), Buf("nf%d" % i)) for i in range(1)])
    NT = Rot([(sb("nt%d" % i, [128, 8, 256], BF16), Buf("nt%d" % i)) for i in range(1)])
    NT2 = Rot([(sb("nt2_%d" % i, [128, 8, 256], BF16), Buf("nt2_%d" % i)) for i in range(1)])
    SG = Rot([(sb("sg%d" % i, [128, 256]), Buf("sg%d" % i)) for i in range(2)])
    ACTT = Rot([(sb("actt%d" % i, [128, 256], BF16), Buf("actt%d" % i)) for i in range(2)])

    def norm_T(h, hB, npart, gain, gB, dst, dB, c0):
        nf, nfB = NF.next()
        ss, ssB = statR.next()
        sd, sdB = statR.next()
        memset("pool", ss[0:npart], 0.0, [ssB])
        act(nf[0:npart, :], h, AF.Square, [hB], [nfB, ssB], accum=ss[0:npart])
        act(sd[0:npart], ss[0:npart], AF.Sqrt, [ssB], [sdB], bias=EPS, scale=1.0 / D)
        recip(sd[0:npart], sd[0:npart], [sdB], [sdB])
        stt("dve", nf[0:npart, :], h, sd[0:npart], gain[0:npart, :], ALU.mult, ALU.mult, [hB, sdB, gB], [nfB])
        for b in range(2):
            bank = 6 + b
            for j in range(4):
                k = 4 * b + j
                tr(PS[bank][:, j * 128:j * 128 + npart], nf[0:npart, k * 128:(k + 1) * 128], ident[0:npart, 0:npart], [nfB, B_cst], [B_PS[bank]])
            src = PS[bank][:, :].rearrange("p (a b) -> p a b", b=128)[:, :, 0:npart]
            cp("act" if b == 0 else "dve", dst[:, 4 * b:4 * b + 4, c0:c0 + npart], src, [B_PS[bank]], [dB])

    def ffn_group(tiles, gain, gB, outs):
        nT, nTB = NT.next()
        offs = []
        N = 0
        for (h, hB, npart) in tiles:
            offs.append(N)
            norm_T(h, hB, npart, gain, gB, nT, nTB, N)
            N += npart

        def gu(f):
            bank = 4 + f % 2
            for k in range(8):
                mm(PS[bank][:, 0:N], WG[:, k, f * 128:(f + 1) * 128], nT[:, k, 0:N], [B_WG, nTB], [B_PS[bank]], start=(k == 0), stop=(k == 7))
            for k in range(8):
                mm(PS[bank][:, 256:256 + N], WU[:, k, f * 128:(f + 1) * 128], nT[:, k, 0:N], [B_WU, nTB], [B_PS[bank]], start=(k == 0), stop=(k == 7))

        gu(0)
        for f in range(NFF):
            if f + 1 < NFF:
                gu(f + 1)
            bank = 4 + f % 2
            sg, sgB = SG.next()
            at, atB = ACTT.next()
            act(sg[:, 0:N], PS[bank][:, 0:N], AF.Silu, [B_PS[bank]], [sgB])
            tt("dve", at[:, 0:N], sg[:, 0:N], PS[bank][:, 256:256 + N], ALU.mult, [sgB, B_PS[bank]], [atB])
            for i, (h, hB, npart) in enumerate(tiles):
                for half in range(2):
                    yb = 2 * i + half
                    mm(PS[yb][0:npart, :], at[:, offs[i]:offs[i] + npart], WD[:, f, half * 512:(half + 1) * 512], [atB, B_WD], [B_PS[yb]],
                       start=(f == 0), stop=(f == NFF - 1))
        for i, (h, hB, npart) in enumerate(tiles):
            o, oB = outs[i]
            for half in range(2):
                yb = 2 * i + half
                stt("dve", o[0:npart, half * 512:(half + 1) * 512], PS[yb][0:npart, :], 0.5, h[:, half * 512:(half + 1) * 512], ALU.mult, ALU.add,
                    [B_PS[yb], hB], [oB])

    xm, xmB = XT.next()
    dma("sp", xm[0:16, :], meta_d, (), [xmB], "x")
    hm, hmB = HT.next()
    ffn_group([(xm[0:16, :], xmB, 16)], gainA, B_gA, [(hm, hmB)])
    norm_T(hm[0:16, :], hmB, 16, gainM, B_gM, nTm, B_nTm, 48)
    cp("pool", nTmA[:, :, :], nTm[:, :, 48:64], [B_nTm], [B_nTmA])
    for g in range(NG1):
        tiles = []
        outs = []
        for i in range(2):
            xt, xB = XT.next()
            r0 = g * 256 + i * 128
            dma("sp", xt[:], x_d[r0:r0 + 128, :], (), [xB], "x")
            tiles.append((xt[:], xB, 128))
            outs.append(HT.next())
        ffn_group(tiles, gainA, B_gA, outs)
        n2, n2B = NT2.next()
        for i in range(2):
            ht, hB = outs[i]
            r0 = g * 256 + i * 128
            dma("sp", h1s.ap()[r0:r0 + 128, :], ht[:], [hB], [B_h1s[2 * g + i]], "h1w")
            norm_T(ht[:], hB, 128, gainM, B_gM, n2, n2B, i * 128)
        dma("sp", nTl[g].ap().rearrange("(k p) t -> p k t", p=128), n2[:], [n2B], [B_nTl[g]], "nTw")
        if stage != 1:
            cc1 = allgather(nTl[g], nTa[g], B_nTl[g], B_nTa[g], "cc1")
    if stage == 1:
        lastw = [o for o in P.dma_ops if o.dkey in ("nTw", "h1w")]
        P.emit(final_wait_ops=[lastw[-1]] + [o for o in lastw if o.dkey == "h1w"][-1:])
        st.close()
        return nc
    if stage == 2:
        P.emit(final_wait_ops=[cc1])
        st.close()
        return nc

    A2 = Arena()
    tenants = []

    def ten(parts, free, dt, name):
        b = Buf(name)
        tenants.append(b)
        return A2.alloc(parts, free, dt), b

    WIN, B_WIN = ten(128, [8, 1056], BF16, "win")
    KT, _ = ten(128, [LP], BF16, "KT")
    B_KT = [Buf("kt%d" % j) for j in range(NB + 1)]
    VX, _ = ten(128, [NB, 129], BF16, "VX")
    B_VX = [Buf("vx%d" % j) for j in range(NB)]
    VM, B_VM = ten(16, [129], BF16, "VM")
    tenants += B_KT + B_VX
    NTG = Rot([ten(128, [8, 256], BF16, "ntg%d" % i) for i in range(2)])
    RC = Rot([ten(128, [256], F32, "rc%d" % i) for i in range(1)])
    RS = Rot([ten(128, [256], F32, "rs%d" % i) for i in range(1)])
    QTB = Rot([ten(128, [256], BF16, "qtb%d" % i) for i in range(2)])
    PTB = Rot([ten(128, [512], BF16, "ptb%d" % i) for i in range(3)])
    F128 = Rot([ten(128, [256], F32, "f128_%d" % i) for i in range(6)])
    OTD = Rot([ten(128, [256], BF16, "otd%d" % i) for i in range(2)])
    ATT = Rot([ten(128, [128], F32, "att%d" % i) for i in range(4)])
    RECS, B_RECS = ten(128, [8], F32, "recs")
    RAW = {}
    for nm, parts in (("r0", 64), ("k0", 64), ("v0", 64), ("r1", 64), ("k1", 64), ("v1", 64), ("wlo", 64), ("alo", 64), ("ga", 128), ("gb", 32)):
        RAW[nm] = ten(parts, [257], F32, "raw_" + nm) + (parts,)
    F64 = Rot([ten(64, [256], F32, "f64_%d" % i) for i in range(37)])
    ALOS = ten(64, [256], F32, "alos")
    WLOS = ten(64, [256], F32, "wlos")
    THB = ten(64, [256], F32, "thb")
    SGA, B_SGA = ten(128, [256], F32, "sga")
    SGB, B_SGB = ten(32, [256], F32, "sgb")
    ARb = Rot([ten(64, [512], F32, "ar%d" % i) for i in range(2)])
    OTR = Rot([ten(64, [256], BF16, "otr%d" % i) for i in range(2)])
    TMP = [ten(64, [3, 64], F32, "tm%d" % i) for i in range(4)]
    M12P = [ten(64, [256], F32, "m12_%d" % i) for i in range(4)]
    XXP = [[ten(64, [128], F32, "xx%d_%d" % (i, j)) for j in range(2)] for i in range(4)]
    YYP = [[ten(64, [128], F32, "yy%d_%d" % (i, j)) for j in range(2)] for i in range(4)]
    SMP = [[ten(64, [64], F32, "sm%d_%d" % (i, j)) for j in range(3)] for i in range(4)]
    HS = [[ten(64, [64], F32, "hs%d_%d" % (h, i)) for i in range(2)] for h in range(2)]

    print("arena phase2 bytes", A2.off, "of", AR_BYTES)
    P.fence(tenants)
    SKIP = int(os.environ.get("SKIP", "0"))
    if not SKIP & 1:
        for k in range(8):
            P.add("pool", (lambda k: lambda e: e.dma_start(out=WIN[:, k, :].rearrange("p (a b) -> p a b", b=528),
                                                            in_=win_d[k * 128:(k + 1) * 128, :].rearrange("p (a b) -> p a b", b=528)))(k),
                  (), [B_WIN], dma="win")
    if not SKIP & 2:
        for h in range(2):
            memset("pool", HS[h][0][0], 0.0, [HS[h][0][1]])
        for nm in RAW:
            memset("pool", RAW[nm][0][:, 0:1], 0.0, [RAW[nm][1]])
    if not SKIP & 4:
        memset("pool", VX[:, :, 128:129], 1.0, B_VX)
    if not SKIP & 8:
        memset("pool", VM[:, 128:129], 1.0, [B_VM])

    PJ = Rot([(PS[4][:, 0:256], B_PS[4]), (PS[5][:, 0:256], B_PS[5]), (PS[6][:, 0:256], B_PS[6]), (PS[7][:, 0:256], B_PS[7])])
    RW = PJ
    STS = [[(PS[c][:, 0:256], B_PS[c]), (PS[c][:, 0:256], B_PS[c])] for c in range(2)]
    hcur = [0, 0]
    blk_count = [0]

    def proj(col0, M, nT, nTB, ntok):
        pj, pjB = PJ.next()
        for k in range(8):
            mm(pj[0:M, 0:ntok], WIN[:, k, col0:col0 + M], nT[:, k, 0:ntok], [B_WIN, nTB], [pjB], start=(k == 0), stop=(k == 7))
        return pj, pjB

    def tokshift(nm, psrc, psB, ntok, mixcol, dst=None):
        raw, rawB, parts = RAW[nm]
        cp("act", raw[:, 1:ntok + 1], psrc[0:parts, 0:ntok], [psB], [rawB])
        d, dB = (F128.next() if parts > 64 else F64.next())
        o, oB = dst if dst is not None else (F128.next() if parts > 64 else F64.next())
        tt("dve", d[0:parts, 0:ntok], raw[:, 0:ntok], raw[:, 1:ntok + 1], ALU.subtract, [rawB], [dB])
        stt("pool", o[0:parts, 0:ntok], d[0:parts, 0:ntok], pp[0:parts, mixcol:mixcol + 1], raw[:, 1:ntok + 1], ALU.mult, ALU.add, [dB, rawB, B_pp], [oB])
        cp("pool", raw[:, 0:1], raw[:, ntok:ntok + 1], [rawB, dB, oB], [rawB])
        return o, oB

    class StopBuild(Exception):
        pass
    ckc = [0]
    CUTN = int(os.environ.get("CUTN", "0"))

    CUTTAG = os.environ.get("CUTTAG", "")

    def ck(tag=None):
        if tag is not None:
            if tag == CUTTAG:
                raise StopBuild()
            return
        ckc[0] += 1
        if ckc[0] == CUTN:
            raise StopBuild()

    def phase2_group(gi):
        is_meta = gi < 0
        if CUT == 5:
            return
        ntok = 64 if is_meta else 256
        nch = ntok // 64
        if is_meta:
            nT, nTB = nTm, B_nTm
            tcol = 0
        else:
            nT, nTB = NTG.next()
            q, gl = gi // NG1, gi % NG1
            src = nTa[gl].ap().rearrange("(q k p) t -> q p k t", q=4, k=8, p=128)[q]
            dma("sp", nT[:], src, [B_nTa[gl]], [nTB], "ntg")
            tcol = 64 + gi * 256
        rc, rcB = RC.next()
        rs, rsB = RS.next()
        dma("sp", rc[:, 0:ntok], ropec_d[:, tcol:tcol + ntok], (), [rcB], "rope")
        dma("sp", rs[:, 0:ntok], ropes_d[:, tcol:tcol + ntok], (), [rsB], "rope")

        qtb = None
        for which in (["k"] if is_meta else ["q", "k"]):
            col0 = 0 if which == "q" else 128
            gcol = 0 if which == "q" else 1
            pj, pjB = proj(col0, 128, nT, nTB, ntok)
            ck()
            sq, sqB = F128.next()
            act(sq[:, 0:ntok], pj[:, 0:ntok], AF.Square, [pjB], [sqB])
            ck()
            p2, p2B = PJ.next()
            mm(p2[:, 0:ntok], onesblk, sq[:, 0:ntok], [sqB, B_cst], [p2B])
            ck()
            rn, rnB = F128.next()
            act(rn[:, 0:ntok], p2[:, 0:ntok], AF.Sqrt, [p2B], [rnB], bias=EPS, scale=1.0 / 64)
            recip(rn[:, 0:ntok], rn[:, 0:ntok], [rnB], [rnB])
            ck()
            qn, qnB = F128.next()
            stt("dve", qn[:, 0:ntok], pj[:, 0:ntok], pp[:, gcol:gcol + 1], rn[:, 0:ntok], ALU.mult, ALU.mult, [pjB, rnB, B_pp], [qnB])
            ck()
            p3, p3B = PJ.next()
            mm(p3[:, 0:ntok], rblk, qn[:, 0:ntok], [qnB, B_cst], [p3B])
            ck()
            t1, t1B = F128.next()
            tt("pool", t1[:, 0:ntok], qn[:, 0:ntok], rc[:, 0:ntok], ALU.mult, [qnB, rcB], [t1B])
            t2, t2B = F128.next()
            tt("dve", t2[:, 0:ntok], p3[:, 0:ntok], rs[:, 0:ntok], ALU.mult, [p3B, rsB], [t2B])
            ck()
            if which == "q":
                qtb, qtbB = QTB.next()
                tt("pool", qtb[:, 0:ntok], t1[:, 0:ntok], t2[:, 0:ntok], ALU.add, [t1B, t2B], [qtbB])
            elif is_meta:
                tt("pool", KT[:, 0:16], t1[:, 48:64], t2[:, 48:64], ALU.add, [t1B, t2B], [B_KT[0]])
            else:
                for i in range(2):
                    p0 = 16 + gi * 256 + i * 128
                    tt("pool", KT[:, p0:p0 + 128], t1[:, i * 128:(i + 1) * 128], t2[:, i * 128:(i + 1) * 128], ALU.add, [t1B, t2B], [B_KT[1 + 2 * gi + i]])
        ck()
        if is_meta:
            if os.environ.get("VARS"):
                PJ.next()
            pj, pjB = PJ.next()
            for k in range(8):
                if os.environ.get("VARM") == "rhs0":
                    mm(pj[0:64, 0:128], nT[:, k, 0:64], WIN[:, k, 0:128], [B_WIN, nTB], [pjB], start=(k == 0), stop=(k == 7))
                elif os.environ.get("VARM") == "swap":
                    mm(pj[0:128, 0:64], WIN[:, k, 256:384], nT[:, k, 0:64], [B_WIN, nTB], [pjB], start=(k == 0), stop=(k == 7))
                elif os.environ.get("VARM") == "64":
                    mm(pj[0:64, 0:128], nT[:, k, 0:64], WIN[:, k, 256:384], [B_WIN, nTB], [pjB], start=(k == 0), stop=(k == 7))
                else:
                    mm(pj[0:16, 0:128], nTmA[:, k, :], WIN[:, k, 256:384], [B_WIN, B_nTmA], [pjB], start=(k == 0), stop=(k == 7))
            ck()
            cp("act", VM[:, 0:128], pj[0:16, 0:128], [pjB], [B_VM])
            ck()
        else:
            for i in range(2):
                pj, pjB = PJ.next()
                for k in range(8):
                    mm(pj[:, 0:128], nT[:, k, i * 128:(i + 1) * 128], WIN[:, k, 256:384], [B_WIN, nTB], [pjB], start=(k == 0), stop=(k == 7))
                cp("act", VX[:, 2 * gi + i, 0:128], pj[:, 0:128], [pjB], [B_VX[2 * gi + i]])

        if not is_meta and CUT != 3:
            blocks = [(-1, 16)] + [(j, 128) for j in range(2 * gi + 2)]
            first = [True, True]
            lastj = [2 * gi, 2 * gi + 1]
            for (j, kb) in blocks:
                sl = blk_count[0] % 2
                blk_count[0] += 1
                kB = B_KT[0] if j < 0 else B_KT[1 + j]
                k0 = 0 if j < 0 else 16 + j * 128
                vap = VM[:, :] if j < 0 else VX[:, j, :]
                vB = B_VM if j < 0 else B_VX[j]
                pt, ptB = PTB.next()
                for c in range(2):
                    s_ap, sB = STS[c][sl]
                    mm(s_ap[0:kb, :], KT[c * 64:(c + 1) * 64, k0:k0 + kb], qtb[c * 64:(c + 1) * 64, :], [kB, qtbB], [sB])
                    act(pt[0:kb, c * 256:(c + 1) * 256], s_ap[0:kb, :], AF.Exp, [sB], [ptB], scale=0.125)
                for it in range(2):
                    if j == lastj[it]:
                        for c in range(2):
                            a = pt[:, c * 256 + it * 128:c * 256 + (it + 1) * 128]
                            tt("pool", a, a, tri_b[:], ALU.mult, [ptB, B_trib], [ptB])
                for c in range(2):
                    for it in range(2):
                        if j > lastj[it]:
                            continue
                        mm(PS[2 + c][:, it * 129:(it + 1) * 129], pt[0:kb, c * 256 + it * 128:c * 256 + (it + 1) * 128], vap[0:kb, :],
                           [ptB, vB], [B_PS[2 + c]], start=first[c], stop=(j == lastj[it]), skip=True)
                        first[c] = False
            otd, otdB = OTD.next()
            for it in range(2):
                for c in range(2):
                    recip(RECS[:, 2 * it + c:2 * it + c + 1], PS[2 + c][:, it * 129 + 128:it * 129 + 129], [B_PS[2 + c]], [B_RECS])
                o1, o1B = ATT.next()
                t2, t2B = ATT.next()
                ts("dve", o1[:], PS[2][:, it * 129:it * 129 + 128], RECS[:, 2 * it:2 * it + 1], ALU.mult, [B_PS[2], B_RECS], [o1B])
                ts("dve", t2[:], PS[3][:, it * 129:it * 129 + 128], RECS[:, 2 * it + 1:2 * it + 2], ALU.mult, [B_PS[3], B_RECS, B_lamt], [t2B],
                   s2=neglam, op1=ALU.mult)
                od, odB = ATT.next()
                tt("pool", od[:], o1[:], t2[:], ALU.add, [o1B, t2B], [odB])
                ss, ssB = statR.next()
                sd, sdB = statR.next()
                memset("pool", ss, 0.0, [ssB])
                act(o1[:], od[:], AF.Square, [odB], [o1B, ssB], accum=ss)
                act(sd, ss, AF.Sqrt, [ssB], [sdB], bias=EPS, scale=1.0 / 128)
                recip(sd, sd, [sdB], [sdB])
                on, onB = ATT.next()
                stt("dve", on[:], od[:], sd, subln[:], ALU.mult, ALU.mult, [odB, sdB, B_subln], [onB])
                if stage == 3 and gi == 0:
                    dma("sp", dbg_d[:, 4 * it + 0, :], o1[:], [o1B], (), "dbg")
                    dma("sp", dbg_d[:, 4 * it + 1, :], t2[:], [t2B], (), "dbg")
                    dma("sp", dbg_d[:, 4 * it + 2, :], od[:], [odB], (), "dbg")
                    dma("sp", dbg_d[:, 4 * it + 3, :], on[:], [onB], (), "dbg")
                    if it == 1:
                        dma("sp", dbg_s[:, 0:8], RECS[:, :], [B_RECS], (), "dbg")
                        dma("sp", dbg_s[:, 8:80], lamt[:, :], [B_lamt], (), "dbg")
                rw, rwB = RW.next()
                tr(rw[:, 0:128], on[:], ident, [onB, B_cst], [rwB])
                cp("act", otd[:, it * 128:(it + 1) * 128], rw[:, 0:128], [rwB], [otdB])
            q, gl = gi // NG1, gi % NG1
            r0 = q * 256
            dma("sp", oTl[gl].ap()[r0:r0 + 128, :], otd[:], [otdB], [B_oTl[gl]], "oTw")

        if CUT == 1 or (CUT == 2 and not is_meta):
            return
        pj, pjB = proj(768, 64, nT, nTB, ntok)
        wlo, wloB = tokshift("wlo", pj, pjB, ntok, 22, dst=WLOS)
        pj, pjB = proj(832, 64, nT, nTB, ntok)
        alo, aloB = tokshift("alo", pj, pjB, ntok, 23, dst=ALOS)
        pj, pjB = proj(896, 128, nT, nTB, ntok)
        ga, gaB = tokshift("ga", pj, pjB, ntok, 24)
        pj, pjB = proj(1024, 32, nT, nTB, ntok)
        gb, gbB = tokshift("gb", pj, pjB, ntok, 25)
        if not is_meta:
            ck("A")
        th, thB = THB
        act(th[:, 0:ntok], wlo[0:64, 0:ntok], AF.Tanh, [wloB], [thB])
        act(SGA[:, 0:ntok], ga[:, 0:ntok], AF.Sigmoid, [gaB], [B_SGA])
        act(SGB[:, 0:ntok], gb[0:32, 0:ntok], AF.Sigmoid, [gbB], [B_SGB])
        for h in range(2):
            pb = 2 + 10 * h
            pj, pjB = proj(384 + 64 * h, 64, nT, nTB, ntok)
            r_s, rB = tokshift("r%d" % h, pj, pjB, ntok, pb + 0)
            pj, pjB = proj(512 + 64 * h, 64, nT, nTB, ntok)
            k_s, kB_ = tokshift("k%d" % h, pj, pjB, ntok, pb + 1)
            pj, pjB = proj(640 + 64 * h, 64, nT, nTB, ntok)
            v_s, vB_ = tokshift("v%d" % h, pj, pjB, ntok, pb + 2)
            N_ = slice(0, ntok)
            pj, pjB = PJ.next()
            mm(pj[0:64, N_], w2s[:, 64 * h:64 * h + 64], th[:, N_], [B_lw, thB], [pjB])
            e1, e1B = F64.next()
            act(e1[:, N_], pj[0:64, N_], AF.Exp, [pjB, B_ppd], [e1B], bias=ppd[0:64, h:h + 1], scale=-1.0)
            act(e1[:, N_], e1[:, N_], AF.Ln, [e1B], [e1B], bias=1.0)
            e2, e2B = F64.next()
            act(e2[:, N_], e1[:, N_], AF.Exp, [e1B], [e2B], bias=-0.5, scale=-1.0)
            pj, pjB = PJ.next()
            mm(pj[0:64, N_], a2s[:, 64 * h:64 * h + 64], alo[0:64, N_], [B_lw, aloB], [pjB])
            lr, lrB = F64.next()
            act(lr[:, N_], pj[0:64, N_], AF.Sigmoid, [pjB, B_pp], [lrB], bias=pp[0:64, pb + 4:pb + 5])
            pj, pjB = PJ.next()
            mm(pj[0:64, N_], g2a[:, 64 * h:64 * h + 64], SGA[:, N_], [B_lw, B_SGA], [pjB], start=True, stop=False)
            mm(pj[0:64, N_], g2b[:, 64 * h:64 * h + 64], SGB[:, N_], [B_lw, B_SGB], [pjB], start=False, stop=True)
            gT, gTB = F64.next()
            cp("act", gT[:, N_], pj[0:64, N_], [pjB], [gTB])
            kk, kkB = F64.next()
            ts("dve", kk[:, N_], k_s[0:64, N_], pp[0:64, pb + 5:pb + 6], ALU.mult, [kB_, B_pp], [kkB])
            ksq, ksqB = F64.next()
            act(ksq[:, N_], kk[:, N_], AF.Square, [kkB], [ksqB])
            pj, pjB = PJ.next()
            mm(pj[0:64, N_], ones64, ksq[:, N_], [ksqB, B_cst], [pjB])
            rn, rnB = F64.next()
            act(rn[:, N_], pj[0:64, N_], AF.Sqrt, [pjB], [rnB])
            ts("dve", rn[:, N_], rn[:, N_], 1e-12, ALU.max, [rnB], [rnB])
            recip(rn[:, N_], rn[:, N_], [rnB], [rnB])
            kkn, kknB = F64.next()
            tt("dve", kkn[:, N_], kk[:, N_], rn[:, N_], ALU.mult, [kkB, rnB], [kknB])
            t1, t1B = F64.next()
            ts("dve", t1[:, N_], lr[:, N_], pp[0:64, pb + 6:pb + 7], ALU.mult, [lrB, B_pp, B_ppd], [t1B], s2=ppd[0:64, 2 + h:3 + h], op1=ALU.add)
            kmod, kmB = F64.next()
            tt("dve", kmod[:, N_], k_s[0:64, N_], t1[:, N_], ALU.mult, [kB_, t1B], [kmB])
            bv, bvB = F64.next()
            tt("dve", bv[:, N_], kkn[:, N_], lr[:, N_], ALU.mult, [kknB, lrB], [bvB])
            rk, rkB = F64.next()
            stt("dve", rk[:, N_], r_s[0:64, N_], pp[0:64, pb + 7:pb + 8], kmod[:, N_], ALU.mult, ALU.mult, [rB, kmB, B_pp], [rkB])
            pj, pjB = PJ.next()
            mm(pj[0:64, N_], ones64, rk[:, N_], [rkB, B_cst], [pjB])
            bon, bonB = F64.next()
            tt("dve", bon[:, N_], pj[0:64, N_], v_s[0:64, N_], ALU.mult, [pjB, vB_], [bonB])
            gneg, gnB = F64.next()
            P.add("dve", (lambda o, d0, d1: lambda e: e.tensor_tensor_scan(out=o, data0=d0, data1=d1, initial=0.0, op0=ALU.mult, op1=ALU.add))(
                gneg[:, N_], scanmask[:, N_], e2[:, N_]), [e2B, B_cst], [gnB])
            Ep, EpB = F64.next()
            Em, EmB = F64.next()
            Ea, EaB = F64.next()
            act(Ep[:, N_], gneg[:, N_], AF.Exp, [gnB], [EpB], scale=-1.0)
            act(Em[:, N_], gneg[:, N_], AF.Exp, [gnB], [EmB])
            tt("dve", Ea[:, N_], e2[:, N_], gneg[:, N_], ALU.subtract, [e2B, gnB], [EaB])
            act(Ea[:, N_], Ea[:, N_], AF.Exp, [EaB], [EaB])
            AR, ARB = ARb.next()
            ARv = AR[:, 0:2 * ntok].rearrange("p (c t) -> p c t", t=128)
            stt("dve", ARv[:, :, 0:64], kkn[:, N_].rearrange("p (c t) -> p c t", t=64), -1.0, Ea[:, N_].rearrange("p (c t) -> p c t", t=64),
                ALU.mult, ALU.mult, [kknB, EaB], [ARB])
            tt("dve", ARv[:, :, 64:128], r_s[0:64, N_].rearrange("p (c t) -> p c t", t=64), Ep[:, N_].rearrange("p (c t) -> p c t", t=64), ALU.mult,
               [rB, EpB], [ARB])
            BT, BTB = F64.next()
            KTl, KTlB = F64.next()
            tt("dve", BT[:, N_], bv[:, N_], Em[:, N_], ALU.mult, [bvB, EmB], [BTB])
            tt("dve", KTl[:, N_], kmod[:, N_], Em[:, N_], ALU.mult, [kmB, EmB], [KTlB])
            BH, BHB = F64.next()
            KH, KHB = F64.next()
            for c in range(nch):
                cc = slice(c * 64, (c + 1) * 64)
                gc = Ep[:, c * 64 + 63:c * 64 + 64]
                ts("dve", BH[:, cc], BT[:, cc], gc, ALU.mult, [BTB, EpB], [BHB])
                ts("dve", KH[:, cc], KTl[:, cc], gc, ALU.mult, [KTlB, EpB], [KHB])
            yT, yTB = F64.next()
            if not is_meta:
                ck("B%d" % h)
            CH = range(nch)
            cc = [slice(c * 64, (c + 1) * 64) for c in CH]
            at_c = [AR[:, c * 128:c * 128 + 64] for c in CH]
            rt_c = [AR[:, c * 128 + 64:c * 128 + 128] for c in CH]
            ar_c = [AR[:, c * 128:(c + 1) * 128] for c in CH]
            gcs = [Ep[:, c * 64 + 63:c * 64 + 64] for c in CH]
            yy = [YYP[c][0] for c in CH]
            yyB = [YYP[c][1] for c in CH]
            for c in CH:
                rw, rwB = RW.next()
                tr(rw[0:64, 0:64], BH[:, cc[c]], ident64, [BHB, B_cst], [rwB])
                tr(rw[0:64, 64:128], KH[:, cc[c]], ident64, [KHB, B_cst], [rwB])
                tr(rw[0:64, 128:192], v_s[0:64, cc[c]], ident64, [vB_, B_cst], [rwB])
                tr(rw[0:64, 192:256], at_c[c], ident64, [ARB, B_cst], [rwB])
                tm, tmB = TMP[c]
                cp("act", tm[:, :, :], rw[0:64, 0:192].rearrange("p (a b) -> p a b", b=64), [rwB], [tmB])
                cp("dve", yy[c][0][:, 0:64], rw[0:64, 192:256], [rwB], [yy[c][1]])
            for c in CH:
                rw, rwB = RW.next()
                mm(rw[0:64, 0:128], BT[:, cc[c]], ar_c[c], [BTB, ARB], [rwB])
                mm(rw[0:64, 128:256], KTl[:, cc[c]], ar_c[c], [KTlB, ARB], [rwB])
                m12, m12B = M12P[c]
                tt("dve", m12[:, :], rw[0:64, 0:256], suiu2, ALU.mult, [rwB, B_cst], [m12B])
            Xs, XTs, XBs = [None] * nch, [None] * nch, [None] * nch
            for c in CH:
                tm, tmB = TMP[c]
                m12, m12B = M12P[c]
                rw, rwB = RW.next()
                mm(rw[0:64, 0:64], at_c[c], BT[:, cc[c]], [ARB, BTB], [rwB])
                mm(rw[0:64, 64:128], m12[:, 128:192], tm[:, 2, :], [m12B, tmB], [rwB])
                xx, xxB = XXP[c][0]
                tt("dve", xx[:, 0:64], rw[0:64, 0:64], slm, ALU.mult, [rwB, B_cst], [xxB])
                cp("act", yy[c][0][:, 64:128], rw[0:64, 64:128], [rwB], [yy[c][1]])
                Xs[c], XTs[c], XBs[c] = xx[:, 0:64], m12[:, 0:64], [xxB, m12B]
            cur = [0] * nch
            for m in range(6):
                for c in CH:
                    ya, yaB = YYP[c][cur[c]]
                    yb, ybB = YYP[c][1 - cur[c]]
                    rw, rwB = RW.next()
                    mm(rw[0:64, 0:128], XTs[c], ya[:, :], XBs[c] + [yaB], [rwB])
                    tt("dve", yb[:, :], rw[0:64, 0:128], ya[:, :], ALU.add, [rwB, yaB], [ybB])
                    cur[c] = 1 - cur[c]
                if m < 5:
                    for c in CH:
                        rw, rwB = RW.next()
                        mm(rw[0:64, 0:64], XTs[c], Xs[c], XBs[c], [rwB])
                        mm(rw[0:64, 64:128], Xs[c], XTs[c], XBs[c], [rwB])
                        xn, xnB = XXP[c][(m + 1) % 2]
                        cp("act", xn[:, :], rw[0:64, 0:128], [rwB], [xnB])
                        Xs[c], XTs[c], XBs[c] = xn[:, 0:64], xn[:, 64:128], [xnB]
            for c in CH:
                yf, yfB = YYP[c][cur[c]]
                tm, tmB = TMP[c]
                m12, m12B = M12P[c]
                mts, mtsB = SMP[c][0]
                rhs_, rhsB = SMP[c][1]
                rw, rwB = RW.next()
                mm(rw[0:64, 0:64], yf[:, 0:64], tm[:, 0, :], [yfB, tmB], [rwB])
                mm(rw[0:64, 64:128], yf[:, 0:64], m12[:, 64:128], [yfB, m12B], [rwB])
                stt("dve", mts[:, :], ident64, gcs[c], rw[0:64, 0:64], ALU.mult, ALU.add, [B_cst, EpB, rwB], [mtsB])
                tt("dve", rhs_[:, :], rw[0:64, 64:128], rt_c[c], ALU.add, [rwB, ARB], [rhsB])
            for c in CH:
                yf, yfB = YYP[c][cur[c]]
                tm, tmB = TMP[c]
                gs, gsB = SMP[c][2]
                rw, rwB = RW.next()
                mm(rw[0:64, 0:64], tm[:, 0, :], yf[:, 64:128], [tmB, yfB], [rwB], start=True, stop=False)
                mm(rw[0:64, 0:64], tm[:, 1, :], tm[:, 2, :], [tmB], [rwB], start=False, stop=True)
                cp("act", gs[:, :], rw[0:64, 0:64], [rwB], [gsB])
            for c in CH:
                yf, yfB = YYP[c][cur[c]]
                tm, tmB = TMP[c]
                m12, m12B = M12P[c]
                mts, mtsB = SMP[c][0]
                rhs_, rhsB = SMP[c][1]
                gs, gsB = SMP[c][2]
                hc, hcB = HS[h][hcur[h]]
                hn, hnB = HS[h][1 - hcur[h]]
                rw, rwB = RW.next()
                mm(rw[0:64, 0:64], yf[:, 64:128], m12[:, 64:128], [yfB, m12B], [rwB], start=True, stop=False)
                mm(rw[0:64, 0:64], tm[:, 2, :], m12[:, 192:256], [tmB, m12B], [rwB], start=False, stop=False)
                mm(rw[0:64, 0:64], hc[:, :], rhs_[:, :], [hcB, rhsB], [rwB], start=False, stop=True)
                rw2, rw2B = RW.next()
                mm(rw2[0:64, 0:64], mts[:, :], hc[:, :], [mtsB, hcB], [rw2B])
                tt("dve", hn[:, :], rw2[0:64, 0:64], gs[:, :], ALU.add, [rw2B, gsB], [hnB])
                cp("act", yT[:, cc[c]], rw[0:64, 0:64], [rwB], [yTB])
                hcur[h] = 1 - hcur[h]
            if not is_meta:
                ck("C%d" % h)
            if not is_meta:
                pj, pjB = PJ.next()
                mm(pj[0:64, N_], ones64, yT[:, N_], [yTB, B_cst], [pjB])
                dd, ddB = F64.next()
                stt("dve", dd[:, N_], pj[0:64, N_], -1.0 / 64, yT[:, N_], ALU.mult, ALU.add, [pjB, yTB], [ddB])
                dq, dqB = F64.next()
                act(dq[:, N_], dd[:, N_], AF.Square, [ddB], [dqB])
                pj, pjB = PJ.next()
                mm(pj[0:64, N_], ones64, dq[:, N_], [dqB, B_cst], [pjB])
                act(dq[:, N_], pj[0:64, N_], AF.Sqrt, [pjB], [dqB], bias=GN_EPS, scale=1.0 / 64)
                recip(dq[:, N_], dq[:, N_], [dqB], [dqB])
                tt("pool", dd[:, N_], dd[:, N_], dq[:, N_], ALU.mult, [ddB, dqB], [ddB])
                act(dd[:, N_], dd[:, N_], AF.Identity, [ddB, B_pp], [ddB], bias=pp[0:64, pb + 9:pb + 10], scale=pp[0:64, pb + 8:pb + 9])
                tt("pool", dd[:, N_], dd[:, N_], bon[:, N_], ALU.add, [ddB, bonB], [ddB])
                ot, otB = OTR.next()
                tt("pool", ot[:, N_], dd[:, N_], gT[:, N_], ALU.mult, [ddB, gTB], [otB])
                ck("D%d" % h)
                q, gl = gi // NG1, gi % NG1
                r0 = q * 256 + 128 + 64 * h
                dma("sp", oTl[gl].ap()[r0:r0 + 64, :], ot[:, :], [otB], [B_oTl[gl]], "oTw")

    try:
        phase2_group(-1)
        for gi in range(NG2 if stage != 3 else (0 if CUT in (1, 4, 5) else 1)):
            phase2_group(gi)
    except StopBuild:
        P.emit(final_wait_ops=[P.ops[e][-1] for e in ("pe", "act", "dve", "pool")] + [o for o in P.dma_ops if o.dkey in ("win", "rope", "ntg")][-3:])
        st.close()
        return nc
    if stage in (3, 4):
        lastw = [o for o in P.dma_ops if o.dkey == "oTw"]
        lastd = [o for o in P.dma_ops if o.dkey == "dbg"]
        P.emit(final_wait_ops=[lastw[-1]] + lastd[-1:] if lastw else [P.ops[e][-1] for e in ("pe", "act", "dve", "pool")])
        st.close()
        return nc
    for gl in range(NG1):
        allgather(oTl[gl], oTa[gl], B_oTl[gl], B_oTa[gl], "cc2")

    P.fence([B_WG, B_WU, B_WD, B_WO])
    for k in range(8):
        P.add("pool", (lambda k: lambda e: e.dma_start(out=WO[:, k, :].rearrange("p (a b) -> p a b", b=512),
                                                        in_=wout_d[k * 128:(k + 1) * 128, :].rearrange("p (a b) -> p a b", b=512)))(k),
              (), [B_WO], dma="wo")
    load_ffn_weights(1)
    dma("sp", gainA[:], gains_d[2], (), [B_gA], "c1")
    last_out = None
    for g in range(NG1):
        oT, oTB = NT2.next()
        for j in range(8):
            P.add("pool", (lambda j, g, oT: lambda e: e.indirect_dma_start(out=oT[:, j, :], out_offset=None, in_=oTa[g].ap()[:, :],
                                                                           in_offset=bass.IndirectOffsetOnAxis(ap=oidx[:, j:j + 1], axis=0)))(j, g, oT),
                  [B_oTa[g], B_oidx], [oTB], dma="og")
        tiles = []
        outs = []
        xts = []
        for i in range(2):
            xt, xB = XT.next()
            r0 = g * 256 + i * 128
            dma("sp", xt[:], h1s.ap()[r0:r0 + 128, :], [B_h1s[2 * g + i]], [xB], "x")
            xts.append((xt, xB))
        for i in range(2):
            xt, xB = xts[i]
            for half in range(2):
                yb = 2 * i + half
                for k in range(8):
                    mm(PS[yb][:, :], oT[:, k, i * 128:(i + 1) * 128], WO[:, k, half * 512:(half + 1) * 512], [oTB, B_WO], [B_PS[yb]], start=(k == 0), stop=(k == 7))
            ht, hB = HT.next()
            for half in range(2):
                yb = 2 * i + half
                tt("dve", ht[:, half * 512:(half + 1) * 512], PS[yb][:, :], xt[:, half * 512:(half + 1) * 512], ALU.add, [B_PS[yb], xB], [hB])
            tiles.append((ht[:], hB, 128))
            outs.append((xt, xB))
        ffn_group(tiles, gainA, B_gA, outs)
        for i in range(2):
            xt, xB = outs[i]
            r0 = g * 256 + i * 128
            last_out = dma("sp", out_d[r0:r0 + 128, :], xt[:], [xB], (), "out")
    P.emit(final_wait_ops=[last_out])
    st.close()
    return nc


def _consts():
    c = np.zeros((128, NCST), np.float32)
    c[:, 0:128] = np.eye(128)
    c[0:64, 128:192] = 1.0
    c[64:128, 192:256] = 1.0
    rb = np.zeros((128, 128), np.float32)
    for b in (0, 64):
        for m in range(8):
            rb[b + m + 8, b + m] = -1.0
            rb[b + m, b + m + 8] = 1.0
    c[:, 256:384] = rb
    kq = np.arange(128)
    c[:, 384:512] = (kq[:, None] <= kq[None, :]).astype(np.float32)
    j = np.arange(64)
    su = (j[:, None] < j[None, :]).astype(np.float32)
    iu = (j[:, None] <= j[None, :]).astype(np.float32)
    c[0:64, 512:576] = su
    c[0:64, 576:640] = iu
    c[0:64, 640:704] = su
    c[0:64, 704:768] = iu
    c[0:64, 768:832] = su.T
    sm = np.ones(256, np.float32)
    sm[::64] = 0.0
    c[:, 832:1088] = sm[None, :]
    return c


def _rope(LP):
    pos = np.arange(LP, dtype=np.float32)
    inv = (np.float32(500000.0) ** (-np.arange(0, 16, 2, dtype=np.float32) / np.float32(16))).astype(np.float32)
    ang = (pos[:, None] * inv[None, :]).astype(np.float32)
    cos, sin = np.cos(ang).astype(np.float32), np.sin(ang).astype(np.float32)
    C = np.ones((128, 48 + LP), np.float32)
    S = np.zeros((128, 48 + LP), np.float32)
    for b in (0, 64):
        C[b:b + 8, 48:] = cos.T
        C[b + 8:b + 16, 48:] = cos.T
        S[b:b + 8, 48:] = sin.T
        S[b + 8:b + 16, 48:] = sin.T
    return C, S


_CACHE = {}
import os
CUT = int(os.environ.get('CUT', '0'))


def _inmaps(inp):
    f = lambda a: np.ascontiguousarray(np.asarray(a, dtype=np.float32))
    x = f(inp["x"])
    B, T, _ = x.shape
    TQ = T // 4
    NG1 = TQ // 256
    LP = 16 + T
    g = lambda k: f(inp[k])[0]
    cst = _consts()
    ropec, ropes = _rope(LP)
    gains = np.stack([np.broadcast_to(g(k)[None, :], (128, D)) for k in ("ffn1_norm", "mix_norm", "ffn2_norm")]).astype(np.float32).copy()
    w_in = g("w_in")
    w_out = g("w_out")
    mix = g("rw_shift_mix")
    lamv = np.concatenate([np.broadcast_to(g(k)[None, :], (128, 64)) for k in ("da_lambda_q1", "da_lambda_k1", "da_lambda_q2", "da_lambda_k2")], 1).astype(np.float32).copy()
    subln = np.broadcast_to(g("da_subln")[None, :], (128, 128)).astype(np.float32).copy()
    worows = np.concatenate([np.concatenate([np.arange(hd * 128, hd * 128 + 128), 512 + np.arange(hd * 128, hd * 128 + 128)]) for hd in range(4)])
    wout_p = np.ascontiguousarray(w_out[worows, :])
    RWO = 1536
    in_maps = []
    for c in range(8):
        b, q = c // 4, c % 4
        hd = q
        cols = np.concatenate([
            np.arange(hd * 128, hd * 128 + 128), 512 + np.arange(hd * 128, hd * 128 + 128), 1024 + np.arange(hd * 128, hd * 128 + 128),
            RWO + np.arange(hd * 128, hd * 128 + 128), RWO + 512 + np.arange(hd * 128, hd * 128 + 128), RWO + 1024 + np.arange(hd * 128, hd * 128 + 128),
            RWO + 1536 + np.arange(288)])
        win_c = np.ascontiguousarray(w_in[:, cols])
        pp = np.zeros((128, NPP), np.float32)
        pp[:, 0] = np.tile(g("da_q_norm"), 2)
        pp[:, 1] = np.tile(g("da_k_norm"), 2)
        for h in range(2):
            ch = slice(hd * 128 + h * 64, hd * 128 + h * 64 + 64)
            pb = 2 + 10 * h
            pp[0:64, pb + 0] = mix[0:512][ch]
            pp[0:64, pb + 1] = mix[512:1024][ch]
            pp[0:64, pb + 2] = mix[1024:1536][ch]
            pp[0:64, pb + 3] = g("rw_w0")[ch]
            pp[0:64, pb + 4] = g("rw_a0")[ch]
            pp[0:64, pb + 5] = g("rw_k_k")[ch]
            pp[0:64, pb + 6] = g("rw_k_a")[ch]
            pp[0:64, pb + 7] = g("rw_r_k").reshape(-1)[ch]
            pp[0:64, pb + 8] = g("rw_ln_w")[ch]
            pp[0:64, pb + 9] = g("rw_ln_b")[ch]
        pp[0:64, 22] = mix[1536:1600]
        pp[0:64, 23] = mix[1600:1664]
        pp[0:128, 24] = mix[1664:1792]
        pp[0:32, 25] = mix[1792:1824]
        chs = slice(hd * 128, hd * 128 + 128)
        oidx = np.zeros((128, 8), np.int32)
        for hh in range(4):
            for fc in range(2):
                oidx[:, hh * 2 + fc] = hh * 1024 + q * 256 + fc * 128 + np.arange(128)
        in_maps.append({
            "x": np.ascontiguousarray(x[b, q * TQ:(q + 1) * TQ, :]), "meta": f(inp["meta_tokens"]), "gains": gains,
            "wg1": g("ffn1_w_gate"), "wu1": g("ffn1_w_up"), "wd1": g("ffn1_w_down"),
            "wg2": g("ffn2_w_gate"), "wu2": g("ffn2_w_up"), "wd2": g("ffn2_w_down"),
            "win": win_c, "wout": wout_p, "pp": pp,
            "w2": np.ascontiguousarray(g("rw_w2")[:, chs]), "a2": np.ascontiguousarray(g("rw_a2")[:, chs]), "g2w": np.ascontiguousarray(g("rw_g2")[:, chs]),
            "lamv": lamv, "subln": subln, "ropec": ropec, "ropes": ropes, "cst": cst, "oidx": oidx,
        })
    return in_maps, B, T, TQ


def kernel(**inp):
    in_maps, B, T, TQ = _inmaps(inp)
    if T not in _CACHE:
        _CACHE[T] = build(T)
    nc = _CACHE[T]
    res = run_bass_kernel_spmd(nc, in_maps, core_ids=list(range(8)))
    out = np.zeros((B, T, D), np.float32)
    for c in range(8):
        b, q = c // 4, c % 4
        out[b, q * TQ:(q + 1) * TQ, :] = res.results[c]["out"]
    return out
```

```python
import contextlib
import numpy as np
import concourse.bass as bass
import concourse.mybir as mybir
from concourse.bass_utils import run_bass_kernel_spmd

F32 = mybir.dt.float32
BF16 = mybir.dt.bfloat16
I32 = mybir.dt.int32
ALU = mybir.AluOpType
AF = mybir.ActivationFunctionType

D = 1024
FF = 2816
NFF = 22
NPP = 26
NCST = 1088
GN_EPS = 64e-5
EPS = 1e-6


class Buf:
    __slots__ = ("name", "last_w", "readers", "excl")

    def __init__(self, name="", excl=False):
        self.name = name
        self.last_w = None
        self.readers = []
        self.excl = excl


class Op:
    __slots__ = ("eng", "fn", "deps", "signal", "tick", "dkey", "dval", "dinc", "idx")


class Prog:
    ENGS = ("pe", "act", "dve", "pool", "sp")

    def __init__(self, nc):
        self.nc = nc
        self.ops = {e: [] for e in self.ENGS}
        self.n = 0
        self.dcount = {}
        self.dma_ops = []

    def add(self, eng, fn, r=(), w=(), dma=None, dinc=16):
        op = Op()
        op.eng = eng
        op.fn = fn
        op.signal = False
        op.tick = None
        op.dkey = dma
        op.dinc = dinc
        op.dval = None
        op.idx = self.n
        self.n += 1
        snap = self.dcount
        if any(b.excl for b in r):
            w = list(w) + [b for b in r if b.excl]
            r = [b for b in r if not b.excl]
        deps = {}
        for b in r:
            if b.last_w is not None:
                deps[b.last_w.idx] = b.last_w
        for b in w:
            if b.last_w is not None:
                deps[b.last_w.idx] = b.last_w
            for o in b.readers:
                deps[o.idx] = o
        dl = []
        for d in deps.values():
            if d.dkey is None and d.eng == "pe" and eng == "pe" and dma is None:
                continue
            d.signal = True
            dl.append((d, snap[d.dkey] if d.dkey is not None else None))
        op.deps = dl
        if dma is not None:
            self.dcount[dma] = self.dcount.get(dma, 0) + dinc
            op.dval = self.dcount[dma]
            self.dma_ops.append(op)
        for b in r:
            b.readers.append(op)
        for b in w:
            b.last_w = op
            b.readers = []
        self.ops[eng].append(op)
        return op

    def fence(self, bufs):
        lasts = [self.ops[e][-1] for e in self.ENGS if self.ops[e] and self.ops[e][-1].dkey is None]
        last_dma = {}
        for o in self.dma_ops:
            last_dma[o.dkey] = o
        lasts += list(last_dma.values())
        for b in bufs:
            b.readers = list(b.readers) + lasts

    def emit(self, final_wait_ops=()):
        nc = self.nc
        st = contextlib.ExitStack()
        esem = {e: st.enter_context(nc.semaphore("s_" + e)) for e in self.ENGS}
        dsem = {k: st.enter_context(nc.semaphore("d_" + str(k))) for k in self.dcount}
        for d in final_wait_ops:
            d.signal = True
        for e in self.ENGS:
            t = 0
            for op in self.ops[e]:
                if op.dkey is None and op.signal:
                    t += 1
                    op.tick = t
        block = st.enter_context(nc.Block())
        ops = self.ops

        def run(e, eng):
            known = {}
            for op in ops[e]:
                need = {}
                for (d, dv) in op.deps:
                    if d.dkey is not None:
                        s, v = dsem[d.dkey], dv
                    else:
                        s, v = esem[d.eng], d.tick
                    key = id(s)
                    if key not in need or need[key][1] < v:
                        need[key] = (s, v)
                for key, (s, v) in need.items():
                    if known.get(key, 0) >= v:
                        continue
                    eng.wait_ge(s, v)
                    known[key] = v
                inst = op.fn(eng)
                if op.dkey is not None:
                    inst.then_inc(dsem[op.dkey], op.dinc)
                elif op.signal:
                    inst.then_inc(esem[e], 1)
            if e == "sp":
                for d in final_wait_ops:
                    if d.dkey is not None:
                        eng.wait_ge(dsem[d.dkey], d.dval)
                    else:
                        eng.wait_ge(esem[d.eng], d.tick)

        @block.tensor
        def _(eng):
            run("pe", eng)

        @block.scalar
        def _(eng):
            run("act", eng)

        @block.vector
        def _(eng):
            run("dve", eng)

        @block.gpsimd
        def _(eng):
            run("pool", eng)

        @block.sync
        def _(eng):
            run("sp", eng)

        st.close()


class Rot:
    def __init__(self, items):
        self.items = items
        self.i = 0

    def next(self):
        it = self.items[self.i % len(self.items)]
        self.i += 1
        return it


def build(T, stage=9):
    TQ = T // 4
    NG1 = TQ // 256
    NG2 = T // 256
    LP = 16 + T
    NB = T // 128
    nc = bass.Bass("TRN2", target_bir_lowering=False)
    st = contextlib.ExitStack()
    P = Prog(nc)

    def din(name, shape, dt=F32):
        return nc.dram_tensor(name, shape, dt, kind="ExternalInput").ap()

    x_d = din("x", [TQ, D])
    meta_d = din("meta", [16, D])
    gains_d = din("gains", [3, 128, D])
    wg_d = [din("wg1", [D, FF]), din("wg2", [D, FF])]
    wu_d = [din("wu1", [D, FF]), din("wu2", [D, FF])]
    wd_d = [din("wd1", [FF, D]), din("wd2", [FF, D])]
    win_d = din("win", [D, 1056])
    wout_d = din("wout", [D, D])
    pp_d = din("pp", [128, NPP])
    w2_d = din("w2", [64, 128])
    a2_d = din("a2", [64, 128])
    g2_d = din("g2w", [160, 128])
    lam_d = din("lamv", [128, 256])
    subln_d = din("subln", [128, 128])
    ropec_d = din("ropec", [128, 48 + LP])
    ropes_d = din("ropes", [128, 48 + LP])
    cst_d = din("cst", [128, NCST])
    oidx_d = din("oidx", [128, 8], I32)
    out_d = nc.dram_tensor("out", [TQ, D], F32, kind="ExternalOutput").ap()
    dbg = stage < 9
    if stage == 3:
        dbg_d = nc.dram_tensor("dbgo", [128, 8, 128], F32, kind="ExternalOutput").ap()
        dbg_s = nc.dram_tensor("dbgs", [128, 80], F32, kind="ExternalOutput").ap()
    kw_ = dict(kind="ExternalOutput") if dbg else {}
    nTl = [nc.dram_tensor("nTl%d" % g, [D, 256], BF16, **(kw_ if stage == 1 else {})) for g in range(NG1)]
    nTa = [nc.dram_tensor("nTa%d" % g, [4 * D, 256], BF16) for g in range(NG1)]
    oTl = [nc.dram_tensor("oTl%d" % g, [4 * 256, 256], BF16, **(kw_ if stage in (3, 4) else {})) for g in range(NG1)]
    oTa = [nc.dram_tensor("oTa%d" % g, [16 * 256, 256], BF16) for g in range(NG1)]
    h1s = nc.dram_tensor("h1s", [TQ, D], F32, **kw_)
    B_nTl = [Buf("nTl") for _ in range(NG1)]
    B_nTa = [Buf("nTa") for _ in range(NG1)]
    B_oTl = [Buf("oTl") for _ in range(NG1)]
    B_oTa = [Buf("oTa") for _ in range(NG1)]
    RG = [[0, 1, 2, 3], [4, 5, 6, 7]]

    def allgather(src, dst, sB, dB, key):
        return P.add("pool", lambda e: e.collective_compute("AllGather", ALU.bypass, replica_groups=RG,
                                                            ins=[src.ap().bitcast(F32).opt()], outs=[dst.ap().bitcast(F32).opt()]),
                     [sB], [dB], dma=key, dinc=1)
    B_h1s = [Buf("h1s%d" % i) for i in range(TQ // 128)]

    def sb(name, shape, dt=F32):
        return st.enter_context(nc.sbuf_tensor("sb_" + name, shape, dt))

    def mm(out, lhsT, rhs, r, w, start=True, stop=True, skip=False):
        if skip:
            return P.add("pe", lambda e: e.matmul(out, lhsT=lhsT, rhs=rhs, start=start, stop=stop, skip_group_check=True), r, w)
        return P.add("pe", lambda e: e.matmul(out, lhsT=lhsT, rhs=rhs, start=start, stop=stop), r, w)

    def tr(out, in_, ident_ap, r, w):
        return P.add("pe", lambda e: e.transpose(out=out, in_=in_, identity=ident_ap), r, w)

    def act(out, in_, func, r, w, bias=None, scale=None, accum=None):
        kw = {}
        if bias is not None:
            kw["bias"] = bias
        if scale is not None:
            kw["scale"] = scale
        if accum is not None:
            kw["accum_out"] = accum
        return P.add("act", lambda e: e.activation(out=out, in_=in_, func=func, **kw), r, w)

    def cp(eng, out, in_, r, w):
        if eng == "act":
            return P.add("act", lambda e: e.copy(out=out, in_=in_), r, w)
        return P.add(eng, lambda e: e.tensor_copy(out=out, in_=in_), r, w)

    def tt(eng, out, in0, in1, op, r, w):
        return P.add(eng, lambda e: e.tensor_tensor(out=out, in0=in0, in1=in1, op=op), r, w)

    def ts(eng, out, in0, s1, op0, r, w, s2=None, op1=None):
        if s2 is None:
            return P.add(eng, lambda e: e.tensor_scalar(out=out, in0=in0, scalar1=s1, scalar2=None, op0=op0), r, w)
        return P.add(eng, lambda e: e.tensor_scalar(out=out, in0=in0, scalar1=s1, scalar2=s2, op0=op0, op1=op1), r, w)

    def stt(eng, out, in0, scalar, in1, op0, op1, r, w):
        return P.add("dve", lambda e: e.scalar_tensor_tensor(out=out, in0=in0, scalar=scalar, in1=in1, op0=op0, op1=op1), r, w)

    def recip(out, in_, r, w):
        return P.add("dve", lambda e: e.reciprocal(out=out, in_=in_), r, w)

    def memset(eng, ap, val, w):
        return P.add(eng, lambda e: e.memset(ap, val), (), w)

    def dma(q, out, in_, r, w, key):
        return P.add(q, lambda e: e.dma_start(out=out, in_=in_), r, w, dma=key)

    PS = [st.enter_context(nc.psum_tensor("ps%d" % i, [128, 512], F32)) for i in range(8)]
    B_PS = [Buf("ps%d" % i, excl=True) for i in range(8)]

    cst = sb("cst", [128, NCST]); B_cst = Buf("cst")
    ident = cst[:, 0:128]
    onesblk = cst[:, 128:256]
    ones64 = cst[0:64, 128:192]
    rblk = cst[:, 256:384]
    tri_f = cst[:, 384:512]
    suiu2 = cst[0:64, 512:768]
    slm = cst[0:64, 768:832]
    scanmask = cst[0:64, 832:1088]
    ident64 = cst[0:64, 0:64]
    tri_b = sb("tri_b", [128, 128], BF16); B_trib = Buf("trib")
    pp = sb("pp", [128, NPP]); B_pp = Buf("pp")
    ppd = sb("ppd", [128, 8]); B_ppd = Buf("ppd")
    lamv = sb("lamv", [128, 256]); B_lam = Buf("lam")
    lamt = sb("lamt", [128, 72]); B_lamt = Buf("lamt")
    subln = sb("subln", [128, 128]); B_subln = Buf("subln")
    w2s = sb("w2s", [64, 128]); a2s = sb("a2s", [64, 128]); g2a = sb("g2a", [128, 128]); g2b = sb("g2b", [32, 128]); B_lw = Buf("lw")
    gainA = sb("gainA", [128, D]); B_gA = Buf("gA")
    gainM = sb("gainM", [128, D]); B_gM = Buf("gM")
    nTm = sb("nTm", [128, 8, 64], BF16); B_nTm = Buf("nTm")
    nTmA = sb("nTmA", [128, 8, 16], BF16); B_nTmA = Buf("nTmA")
    oidx = sb("oidx", [128, 8], I32); B_oidx = Buf("oidx")
    stat = sb("stat", [128, 32]); statR = Rot([(stat[:, i:i + 1], Buf("st%d" % i)) for i in range(32)])

    dma("sp", cst[:], cst_d, (), [B_cst], "c0")
    dma("sp", pp[:], pp_d, (), [B_pp], "c0")
    dma("sp", lamv[:], lam_d, (), [B_lam], "c0")
    dma("sp", subln[:], subln_d, (), [B_subln], "c0")
    dma("sp", w2s[:], w2_d, (), [B_lw], "c0")
    dma("sp", a2s[:], a2_d, (), [B_lw], "c0")
    dma("sp", g2a[:], g2_d[0:128, :], (), [B_lw], "c0")
    dma("sp", g2b[:], g2_d[128:160, :], (), [B_lw], "c0")
    dma("sp", gainA[:], gains_d[0], (), [B_gA], "c0")
    dma("sp", gainM[:], gains_d[1], (), [B_gM], "c0")
    dma("sp", oidx[:], oidx_d, (), [B_oidx], "c0")
    cp("dve", tri_b[:], tri_f, [B_cst], [B_trib])
    memset("pool", nTm[:], 0.0, [B_nTm])
    for h in range(2):
        ts("dve", ppd[:, h:h + 1], pp[:, 5 + 10 * h:6 + 10 * h], -1.0, ALU.mult, [B_pp], [B_ppd])
        ts("dve", ppd[:, 2 + h:3 + h], pp[:, 8 + 10 * h:9 + 10 * h], -1.0, ALU.mult, [B_pp], [B_ppd], s2=1.0, op1=ALU.add)
    for i in range(2):
        tt("dve", lamt[:, 0:64], lamv[:, 128 * i:128 * i + 64], lamv[:, 128 * i + 64:128 * i + 128], ALU.mult, [B_lam], [B_lamt])
        P.add("dve", (lambda i: lambda e: e.reduce_sum(out=lamt[:, 64 + i:65 + i], in_=lamt[:, 0:64], axis=mybir.AxisListType.X))(i), [B_lamt], [B_lamt])
    act(lamt[:, 66:68], lamt[:, 64:66], AF.Exp, [B_lamt], [B_lamt])
    tt("dve", lamt[:, 68:69], lamt[:, 67:68], lamt[:, 66:67], ALU.subtract, [B_lamt], [B_lamt])
    ts("dve", lamt[:, 70:71], lamt[:, 68:69], -0.2, ALU.add, [B_lamt], [B_lamt])
    neglam = lamt[:, 70:71]
    ts("dve", subln[:], subln[:], 0.8, ALU.mult, [B_subln], [B_subln])

    AR_BYTES = 151552 + 5120
    arena = sb("arena", [128, AR_BYTES // 4])

    class Arena:
        def __init__(self):
            self.off = 0

        def alloc(self, parts, free, dt):
            esz = 4 if dt in (F32, I32) else 2
            n = int(np.prod(free))
            nb = (n * esz + 3) // 4 * 4
            assert self.off + nb <= AR_BYTES, (self.off, nb)
            a = arena[:, self.off // 4:(self.off + nb) // 4]
            self.off += nb
            if dt != F32:
                a = a.bitcast(dt)
            a = a[0:parts, 0:n]
            if len(free) == 2:
                a = a.rearrange("p (a b) -> p a b", b=free[1])
            elif len(free) == 3:
                a = a.rearrange("p (a b c) -> p a b c", b=free[1], c=free[2])
            return a

    A13 = Arena()
    WG = A13.alloc(128, [8, FF], BF16)
    WU = A13.alloc(128, [8, FF], BF16)
    WD = A13.alloc(128, [NFF, D], BF16)
    WO = A13.alloc(128, [8, D], BF16)
    B_WG, B_WU, B_WD, B_WO = Buf("WG"), Buf("WU"), Buf("WD"), Buf("WO")

    def load_ffn_weights(i):
        for k in range(8):
            P.add("pool", (lambda k: lambda e: e.dma_start(out=WG[:, k, :].rearrange("p (a b) -> p a b", b=704),
                                                            in_=wg_d[i][k * 128:(k + 1) * 128, :].rearrange("p (a b) -> p a b", b=704)))(k),
                  (), [B_WG], dma="wg")
        for k in range(8):
            P.add("pool", (lambda k: lambda e: e.dma_start(out=WU[:, k, :].rearrange("p (a b) -> p a b", b=704),
                                                            in_=wu_d[i][k * 128:(k + 1) * 128, :].rearrange("p (a b) -> p a b", b=704)))(k),
                  (), [B_WU], dma="wu")
        for f in range(NFF):
            P.add("pool", (lambda f: lambda e: e.dma_start(out=WD[:, f, :].rearrange("p (a b) -> p a b", b=512),
                                                            in_=wd_d[i][f * 128:(f + 1) * 128, :].rearrange("p (a b) -> p a b", b=512)))(f),
                  (), [B_WD], dma="wd")

    load_ffn_weights(0)

    XT = Rot([(sb("xt%d" % i, [128, D]), Buf("xt%d" % i)) for i in range(3)])
    HT = Rot([(sb("ht%d" % i, [128, D]), Buf("ht%d" % i)) for i in range(2)])
    NF = Rot([(sb("nf%d" % i, [128, D]), Buf("nf%d" % i)) for i in range(1)])
    NT = Rot([(sb("nt%d" % i, [128, 8, 256], BF16), Buf("nt%d" % i)) for i in range(1)])
    NT2 = Rot([(sb("nt2_%d" % i, [128, 8, 256], BF16), Buf("nt2_%d" % i)) for i in range(1)])
    SG = Rot([(sb("sg%d" % i, [128, 256]), Buf("sg%d" % i)) for i in range(2)])
    ACTT = Rot([(sb("actt%d" % i, [128, 256], BF16), Buf("actt%d" % i)) for i in range(2)])

    def norm_T(h, hB, npart, gain, gB, dst, dB, c0):
        nf, nfB = NF.next()
        ss, ssB = statR.next()
        sd, sdB = statR.next()
        memset("pool", ss[0:npart], 0.0, [ssB])
        act(nf[0:npart, :], h, AF.Square, [hB], [nfB, ssB], accum=ss[0:npart])
        act(sd[0:npart], ss[0:npart], AF.Sqrt, [ssB], [sdB], bias=EPS, scale=1.0 / D)
        recip(sd[0:npart], sd[0:npart], [sdB], [sdB])
        stt("dve", nf[0:npart, :], h, sd[0:npart], gain[0:npart, :], ALU.mult, ALU.mult, [hB, sdB, gB], [nfB])
        for b in range(2):
            bank = 6 + b
            for j in range(4):
                k = 4 * b + j
                tr(PS[bank][:, j * 128:j * 128 + npart], nf[0:npart, k * 128:(k + 1) * 128], ident[0:npart, 0:npart], [nfB, B_cst], [B_PS[bank]])
            src = PS[bank][:, :].rearrange("p (a b) -> p a b", b=128)[:, :, 0:npart]
            cp("act" if b == 0 else "dve", dst[:, 4 * b:4 * b + 4, c0:c0 + npart], src, [B_PS[bank]], [dB])

    def ffn_group(tiles, gain, gB, outs):
        nT, nTB = NT.next()
        offs = []
        N = 0
        for (h, hB, npart) in tiles:
            offs.append(N)
            norm_T(h, hB, npart, gain, gB, nT, nTB, N)
            N += npart

        def gu(f):
            bank = 4 + f % 2
            for k in range(8):
                mm(PS[bank][:, 0:N], WG[:, k, f * 128:(f + 1) * 128], nT[:, k, 0:N], [B_WG, nTB], [B_PS[bank]], start=(k == 0), stop=(k == 7))
            for k in range(8):
                mm(PS[bank][:, 256:256 + N], WU[:, k, f * 128:(f + 1) * 128], nT[:, k, 0:N], [B_WU, nTB], [B_PS[bank]], start=(k == 0), stop=(k == 7))

        gu(0)
        for f in range(NFF):
            if f + 1 < NFF:
                gu(f + 1)
            bank = 4 + f % 2
            sg, sgB = SG.next()
            at, atB = ACTT.next()
            act(sg[:, 0:N], PS[bank][:, 0:N], AF.Silu, [B_PS[bank]], [sgB])
            tt("dve", at[:, 0:N], sg[:, 0:N], PS[bank][:, 256:256 + N], ALU.mult, [sgB, B_PS[bank]], [atB])
            for i, (h, hB, npart) in enumerate(tiles):
                for half in range(2):
                    yb = 2 * i + half
                    mm(PS[yb][0:npart, :], at[:, offs[i]:offs[i] + npart], WD[:, f, half * 512:(half + 1) * 512], [atB, B_WD], [B_PS[yb]],
                       start=(f == 0), stop=(f == NFF - 1))
        for i, (h, hB, npart) in enumerate(tiles):
            o, oB = outs[i]
            for half in range(2):
                yb = 2 * i + half
                stt("dve", o[0:npart, half * 512:(half + 1) * 512], PS[yb][0:npart, :], 0.5, h[:, half * 512:(half + 1) * 512], ALU.mult, ALU.add,
                    [B_PS[yb], hB], [oB])

    xm, xmB = XT.next()
    dma("sp", xm[0:16, :], meta_d, (), [xmB], "x")
    hm, hmB = HT.next()
    ffn_group([(xm[0:16, :], xmB, 16)], gainA, B_gA, [(hm, hmB)])
    norm_T(hm[0:16, :], hmB, 16, gainM, B_gM, nTm, B_nTm, 48)
    cp("pool", nTmA[:, :, :], nTm[:, :, 48:64], [B_nTm], [B_nTmA])
    for g in range(NG1):
        tiles = []
        outs = []
        for i in range(2):
            xt, xB = XT.next()
            r0 = g * 256 + i * 128
            dma("sp", xt[:], x_d[r0:r0 + 128, :], (), [xB], "x")
            tiles.append((xt[:], xB, 128))
            outs.append(HT.next())
        ffn_group(tiles, gainA, B_gA, outs)
        n2, n2B = NT2.next()
        for i in range(2):
            ht, hB = outs[i]
            r0 = g * 256 + i * 128
            dma("sp", h1s.ap()[r0:r0 + 128, :], ht[:], [hB], [B_h1s[2 * g + i]], "h1w")
            norm_T(ht[:], hB, 128, gainM, B_gM, n2, n2B, i * 128)
        dma("sp", nTl[g].ap().rearrange("(k p) t -> p k t", p=128), n2[:], [n2B], [B_nTl[g]], "nTw")
        if stage != 1:
            cc1 = allgather(nTl[g], nTa[g], B_nTl[g], B_nTa[g], "cc1")
    if stage == 1:
        lastw = [o for o in P.dma_ops if o.dkey in ("nTw", "h1w")]
        P.emit(final_wait_ops=[lastw[-1]] + [o for o in lastw if o.dkey == "h1w"][-1:])
        st.close()
        return nc
    if stage == 2:
        P.emit(final_wait_ops=[cc1])
        st.close()
        return nc

    A2 = Arena()
    tenants = []

    def ten(parts, free, dt, name):
        b = Buf(name)
        tenants.append(b)
        return A2.alloc(parts, free, dt), b

    WIN, B_WIN = ten(128, [8, 1056], BF16, "win")
    KT, _ = ten(128, [LP], BF16, "KT")
    B_KT = [Buf("kt%d" % j) for j in range(NB + 1)]
    VX, _ = ten(128, [NB, 129], BF16, "VX")
    B_VX = [Buf("vx%d" % j) for j in range(NB)]
    VM, B_VM = ten(16, [129], BF16, "VM")
    tenants += B_KT + B_VX
    NTG = Rot([ten(128, [8, 256], BF16, "ntg%d" % i) for i in range(2)])
    RC = Rot([ten(128, [256], F32, "rc%d" % i) for i in range(1)])
    RS = Rot([ten(128, [256], F32, "rs%d" % i) for i in range(1)])
    QTB = Rot([ten(128, [256], BF16, "qtb%d" % i) for i in range(2)])
    PTB = Rot([ten(128, [512], BF16, "ptb%d" % i) for i in range(3)])
    F128 = Rot([ten(128, [256], F32, "f128_%d" % i) for i in range(6)])
    OTD = Rot([ten(128, [256], BF16, "otd%d" % i) for i in range(2)])
    ATT = Rot([ten(128, [128], F32, "att%d" % i) for i in range(4)])
    RECS, B_RECS = ten(128, [8], F32, "recs")
    RAW = {}
    for nm, parts in (("r0", 64), ("k0", 64), ("v0", 64), ("r1", 64), ("k1", 64), ("v1", 64), ("wlo", 64), ("alo", 64), ("ga", 128), ("gb", 32)):
        RAW[nm] = ten(parts, [257], F32, "raw_" + nm) + (parts,)
    F64 = Rot([ten(64, [256], F32, "f64_%d" % i) for i in range(37)])
    ALOS = ten(64, [256], F32, "alos")
    WLOS = ten(64, [256], F32, "wlos")
    THB = ten(64, [256], F32, "thb")
    SGA, B_SGA = ten(128, [256], F32, "sga")
    SGB, B_SGB = ten(32, [256], F32, "sgb")
    ARb = Rot([ten(64, [512], F32, "ar%d" % i) for i in range(2)])
    OTR = Rot([ten(64, [256], BF16, "otr%d" % i) for i in range(2)])
    TMP = [ten(64, [3, 64], F32, "tm%d" % i) for i in range(4)]
    M12P = [ten(64, [256], F32, "m12_%d" % i) for i in range(4)]
    XXP = [[ten(64, [128], F32, "xx%d_%d" % (i, j)) for j in range(2)] for i in range(4)]
    YYP = [[ten(64, [128], F32, "yy%d_%d" % (i, j)) for j in range(2)] for i in range(4)]
    SMP = [[ten(64, [64], F32, "sm%d_%d" % (i, j)) for j in range(3)] for i in range(4)]
    HS = [[ten(64, [64], F32, "hs%d_%d" % (h, i)) for i in range(2)] for h in range(2)]

    print("arena phase2 bytes", A2.off, "of", AR_BYTES)
    P.fence(tenants)
    SKIP = int(os.environ.get("SKIP", "0"))
    if not SKIP & 1:
        for k in range(8):
            P.add("pool", (lambda k: lambda e: e.dma_start(out=WIN[:, k, :].rearrange("p (a b) -> p a b", b=528),
                                                            in_=win_d[k * 128:(k + 1) * 128, :].rearrange("p (a b) -> p a b", b=528)))(k),
                  (), [B_WIN], dma="win")
    if not SKIP & 2:
        for h in range(2):
            memset("pool", HS[h][0][0], 0.0, [HS[h][0][1]])
        for nm in RAW:
            memset("pool", RAW[nm][0][:, 0:1], 0.0, [RAW[nm][1]])
    if not SKIP & 4:
        memset("pool", VX[:, :, 128:129], 1.0, B_VX)
    if not SKIP & 8:
        memset("pool", VM[:, 128:129], 1.0, [B_VM])

    PJ = Rot([(PS[4][:, 0:256], B_PS[4]), (PS[5][:, 0:256], B_PS[5]), (PS[6][:, 0:256], B_PS[6]), (PS[7][:, 0:256], B_PS[7])])
    RW = PJ
    STS = [[(PS[c][:, 0:256], B_PS[c]), (PS[c][:, 0:256], B_PS[c])] for c in range(2)]
    hcur = [0, 0]
    blk_count = [0]

    def proj(col0, M, nT, nTB, ntok):
        pj, pjB = PJ.next()
        for k in range(8):
            mm(pj[0:M, 0:ntok], WIN[:, k, col0:col0 + M], nT[:, k, 0:ntok], [B_WIN, nTB], [pjB], start=(k == 0), stop=(k == 7))
        return pj, pjB

    def tokshift(nm, psrc, psB, ntok, mixcol, dst=None):
        raw, rawB, parts = RAW[nm]
        cp("act", raw[:, 1:ntok + 1], psrc[0:parts, 0:ntok], [psB], [rawB])
        d, dB = (F128.next() if parts > 64 else F64.next())
        o, oB = dst if dst is not None else (F128.next() if parts > 64 else F64.next())
        tt("dve", d[0:parts, 0:ntok], raw[:, 0:ntok], raw[:, 1:ntok + 1], ALU.subtract, [rawB], [dB])
        stt("pool", o[0:parts, 0:ntok], d[0:parts, 0:ntok], pp[0:parts, mixcol:mixcol + 1], raw[:, 1:ntok + 1], ALU.mult, ALU.add, [dB, rawB, B_pp], [oB])
        cp("pool", raw[:, 0:1], raw[:, ntok:ntok + 1], [rawB, dB, oB], [rawB])
        return o, oB

    class StopBuild(Exception):
        pass
    ckc = [0]
    CUTN = int(os.environ.get("CUTN", "0"))

    CUTTAG = os.environ.get("CUTTAG", "")

    def ck(tag=None):
        if tag is not None:
            if tag == CUTTAG:
                raise StopBuild()
            return
        ckc[0] += 1
        if ckc[0] == CUTN:
            raise StopBuild()

    def phase2_group(gi):
        is_meta = gi < 0
        if CUT == 5:
            return
        ntok = 64 if is_meta else 256
        nch = ntok // 64
        if is_meta:
            nT, nTB = nTm, B_nTm
            tcol = 0
        else:
            nT, nTB = NTG.next()
            q, gl = gi // NG1, gi % NG1
            src = nTa[gl].ap().rearrange("(q k p) t -> q p k t", q=4, k=8, p=128)[q]
            dma("sp", nT[:], src, [B_nTa[gl]], [nTB], "ntg")
            tcol = 64 + gi * 256
        rc, rcB = RC.next()
        rs, rsB = RS.next()
        dma("sp", rc[:, 0:ntok], ropec_d[:, tcol:tcol + ntok], (), [rcB], "rope")
        dma("sp", rs[:, 0:ntok], ropes_d[:, tcol:tcol + ntok], (), [rsB], "rope")

        qtb = None
        for which in (["k"] if is_meta else ["q", "k"]):
            col0 = 0 if which == "q" else 128
            gcol = 0 if which == "q" else 1
            pj, pjB = proj(col0, 128, nT, nTB, ntok)
            ck()
            sq, sqB = F128.next()
            act(sq[:, 0:ntok], pj[:, 0:ntok], AF.Square, [pjB], [sqB])
            ck()
            p2, p2B = PJ.next()
            mm(p2[:, 0:ntok], onesblk, sq[:, 0:ntok], [sqB, B_cst], [p2B])
            ck()
            rn, rnB = F128.next()
            act(rn[:, 0:ntok], p2[:, 0:ntok], AF.Sqrt, [p2B], [rnB], bias=EPS, scale=1.0 / 64)
            recip(rn[:, 0:ntok], rn[:, 0:ntok], [rnB], [rnB])
            ck()
            qn, qnB = F128.next()
            stt("dve", qn[:, 0:ntok], pj[:, 0:ntok], pp[:, gcol:gcol + 1], rn[:, 0:ntok], ALU.mult, ALU.mult, [pjB, rnB, B_pp], [qnB])
            ck()
            p3, p3B = PJ.next()
            mm(p3[:, 0:ntok], rblk, qn[:, 0:ntok], [qnB, B_cst], [p3B])
            ck()
            t1, t1B = F128.next()
            tt("pool", t1[:, 0:ntok], qn[:, 0:ntok], rc[:, 0:ntok], ALU.mult, [qnB, rcB], [t1B])
            t2, t2B = F128.next()
            tt("dve", t2[:, 0:ntok], p3[:, 0:ntok], rs[:, 0:ntok], ALU.mult, [p3B, rsB], [t2B])
            ck()
            if which == "q":
                qtb, qtbB = QTB.next()
                tt("pool", qtb[:, 0:ntok], t1[:, 0:ntok], t2[:, 0:ntok], ALU.add, [t1B, t2B], [qtbB])
            elif is_meta:
                tt("pool", KT[:, 0:16], t1[:, 48:64], t2[:, 48:64], ALU.add, [t1B, t2B], [B_KT[0]])
            else:
                for i in range(2):
                    p0 = 16 + gi * 256 + i * 128
                    tt("pool", KT[:, p0:p0 + 128], t1[:, i * 128:(i + 1) * 128], t2[:, i * 128:(i + 1) * 128], ALU.add, [t1B, t2B], [B_KT[1 + 2 * gi + i]])
        ck()
        if is_meta:
            if os.environ.get("VARS"):
                PJ.next()
            pj, pjB = PJ.next()
            for k in range(8):
                if os.environ.get("VARM") == "rhs0":
                    mm(pj[0:64, 0:128], nT[:, k, 0:64], WIN[:, k, 0:128], [B_WIN, nTB], [pjB], start=(k == 0), stop=(k == 7))
                elif os.environ.get("VARM") == "swap":
                    mm(pj[0:128, 0:64], WIN[:, k, 256:384], nT[:, k, 0:64], [B_WIN, nTB], [pjB], start=(k == 0), stop=(k == 7))
                elif os.environ.get("VARM") == "64":
                    mm(pj[0:64, 0:128], nT[:, k, 0:64], WIN[:, k, 256:384], [B_WIN, nTB], [pjB], start=(k == 0), stop=(k == 7))
                else:
                    mm(pj[0:16, 0:128], nTmA[:, k, :], WIN[:, k, 256:384], [B_WIN, B_nTmA], [pjB], start=(k == 0), stop=(k == 7))
            ck()
            cp("act", VM[:, 0:128], pj[0:16, 0:128], [pjB], [B_VM])
            ck()
        else:
            for i in range(2):
                pj, pjB = PJ.next()
                for k in range(8):
                    mm(pj[:, 0:128], nT[:, k, i * 128:(i + 1) * 128], WIN[:, k, 256:384], [B_WIN, nTB], [pjB], start=(k == 0), stop=(k == 7))
                cp("act", VX[:, 2 * gi + i, 0:128], pj[:, 0:128], [pjB], [B_VX[2 * gi + i]])

        def att_blocks():
            blocks = [(-1, 16)] + [(j, 128) for j in range(2 * gi + 2)]
            first = [True, True]
            lastj = [2 * gi, 2 * gi + 1]
            for (j, kb) in blocks:
                sl = blk_count[0] % 2
                blk_count[0] += 1
                kB = B_KT[0] if j < 0 else B_KT[1 + j]
                k0 = 0 if j < 0 else 16 + j * 128
                vap = VM[:, :] if j < 0 else VX[:, j, :]
                vB = B_VM if j < 0 else B_VX[j]
                pt, ptB = PTB.next()
                for c in range(2):
                    s_ap, sB = STS[c][sl]
                    mm(s_ap[0:kb, :], KT[c * 64:(c + 1) * 64, k0:k0 + kb], qtb[c * 64:(c + 1) * 64, :], [kB, qtbB], [sB])
                    act(pt[0:kb, c * 256:(c + 1) * 256], s_ap[0:kb, :], AF.Exp, [sB], [ptB], scale=0.125)
                for it in range(2):
                    if j == lastj[it]:
                        for c in range(2):
                            a = pt[:, c * 256 + it * 128:c * 256 + (it + 1) * 128]
                            tt("pool", a, a, tri_b[:], ALU.mult, [ptB, B_trib], [ptB])
                for c in range(2):
                    for it in range(2):
                        if j > lastj[it]:
                            continue
                        mm(PS[2 + c][:, it * 129:(it + 1) * 129], pt[0:kb, c * 256 + it * 128:c * 256 + (it + 1) * 128], vap[0:kb, :],
                           [ptB, vB], [B_PS[2 + c]], start=first[c], stop=(j == lastj[it]), skip=True)
                        first[c] = False
                yield

        def att_epilogue():
            otd, otdB = OTD.next()
            for it in range(2):
                for c in range(2):
                    recip(RECS[:, 2 * it + c:2 * it + c + 1], PS[2 + c][:, it * 129 + 128:it * 129 + 129], [B_PS[2 + c]], [B_RECS])
                o1, o1B = ATT.next()
                t2, t2B = ATT.next()
                ts("dve", o1[:], PS[2][:, it * 129:it * 129 + 128], RECS[:, 2 * it:2 * it + 1], ALU.mult, [B_PS[2], B_RECS], [o1B])
                ts("dve", t2[:], PS[3][:, it * 129:it * 129 + 128], RECS[:, 2 * it + 1:2 * it + 2], ALU.mult, [B_PS[3], B_RECS, B_lamt], [t2B],
                   s2=neglam, op1=ALU.mult)
                od, odB = ATT.next()
                tt("pool", od[:], o1[:], t2[:], ALU.add, [o1B, t2B], [odB])
                ss, ssB = statR.next()
                sd, sdB = statR.next()
                memset("pool", ss, 0.0, [ssB])
                act(o1[:], od[:], AF.Square, [odB], [o1B, ssB], accum=ss)
                act(sd, ss, AF.Sqrt, [ssB], [sdB], bias=EPS, scale=1.0 / 128)
                recip(sd, sd, [sdB], [sdB])
                on, onB = ATT.next()
                stt("dve", on[:], od[:], sd, subln[:], ALU.mult, ALU.mult, [odB, sdB, B_subln], [onB])
                if stage == 3 and gi == 0:
                    dma("sp", dbg_d[:, 4 * it + 0, :], o1[:], [o1B], (), "dbg")
                    dma("sp", dbg_d[:, 4 * it + 1, :], t2[:], [t2B], (), "dbg")
                    dma("sp", dbg_d[:, 4 * it + 2, :], od[:], [odB], (), "dbg")
                    dma("sp", dbg_d[:, 4 * it + 3, :], on[:], [onB], (), "dbg")
                    if it == 1:
                        dma("sp", dbg_s[:, 0:8], RECS[:, :], [B_RECS], (), "dbg")
                        dma("sp", dbg_s[:, 8:80], lamt[:, :], [B_lamt], (), "dbg")
                rw, rwB = RW.next()
                tr(rw[:, 0:128], on[:], ident, [onB, B_cst], [rwB])
                cp("act", otd[:, it * 128:(it + 1) * 128], rw[:, 0:128], [rwB], [otdB])
            q, gl = gi // NG1, gi % NG1
            r0 = q * 256
            dma("sp", oTl[gl].ap()[r0:r0 + 128, :], otd[:], [otdB], [B_oTl[gl]], "oTw")


        att_gen = att_blocks() if (not is_meta and CUT != 3) else None
        adv_state = [0, max(1, 136 // (2 * max(gi, 0) + 3))]

        def adv():
            if att_gen is None:
                return
            adv_state[0] += 1
            if adv_state[0] % adv_state[1] == 0:
                next(att_gen, None)

        if CUT == 1 or (CUT == 2 and not is_meta):
            return
        pj, pjB = proj(768, 64, nT, nTB, ntok)
        wlo, wloB = tokshift("wlo", pj, pjB, ntok, 22, dst=WLOS)
        pj, pjB = proj(832, 64, nT, nTB, ntok)
        alo, aloB = tokshift("alo", pj, pjB, ntok, 23, dst=ALOS)
        pj, pjB = proj(896, 128, nT, nTB, ntok)
        ga, gaB = tokshift("ga", pj, pjB, ntok, 24)
        pj, pjB = proj(1024, 32, nT, nTB, ntok)
        gb, gbB = tokshift("gb", pj, pjB, ntok, 25)
        if not is_meta:
            ck("A")
        th, thB = THB
        act(th[:, 0:ntok], wlo[0:64, 0:ntok], AF.Tanh, [wloB], [thB])
        act(SGA[:, 0:ntok], ga[:, 0:ntok], AF.Sigmoid, [gaB], [B_SGA])
        act(SGB[:, 0:ntok], gb[0:32, 0:ntok], AF.Sigmoid, [gbB], [B_SGB])
        for h in range(2):
            pb = 2 + 10 * h
            pj, pjB = proj(384 + 64 * h, 64, nT, nTB, ntok)
            r_s, rB = tokshift("r%d" % h, pj, pjB, ntok, pb + 0)
            pj, pjB = proj(512 + 64 * h, 64, nT, nTB, ntok)
            k_s, kB_ = tokshift("k%d" % h, pj, pjB, ntok, pb + 1)
            pj, pjB = proj(640 + 64 * h, 64, nT, nTB, ntok)
            v_s, vB_ = tokshift("v%d" % h, pj, pjB, ntok, pb + 2)
            N_ = slice(0, ntok)
            pj, pjB = PJ.next()
            mm(pj[0:64, N_], w2s[:, 64 * h:64 * h + 64], th[:, N_], [B_lw, thB], [pjB])
            e1, e1B = F64.next()
            act(e1[:, N_], pj[0:64, N_], AF.Exp, [pjB, B_ppd], [e1B], bias=ppd[0:64, h:h + 1], scale=-1.0)
            act(e1[:, N_], e1[:, N_], AF.Ln, [e1B], [e1B], bias=1.0)
            e2, e2B = F64.next()
            act(e2[:, N_], e1[:, N_], AF.Exp, [e1B], [e2B], bias=-0.5, scale=-1.0)
            pj, pjB = PJ.next()
            mm(pj[0:64, N_], a2s[:, 64 * h:64 * h + 64], alo[0:64, N_], [B_lw, aloB], [pjB])
            lr, lrB = F64.next()
            act(lr[:, N_], pj[0:64, N_], AF.Sigmoid, [pjB, B_pp], [lrB], bias=pp[0:64, pb + 4:pb + 5])
            pj, pjB = PJ.next()
            mm(pj[0:64, N_], g2a[:, 64 * h:64 * h + 64], SGA[:, N_], [B_lw, B_SGA], [pjB], start=True, stop=False)
            mm(pj[0:64, N_], g2b[:, 64 * h:64 * h + 64], SGB[:, N_], [B_lw, B_SGB], [pjB], start=False, stop=True)
            gT, gTB = F64.next()
            cp("act", gT[:, N_], pj[0:64, N_], [pjB], [gTB])
            kk, kkB = F64.next()
            ts("dve", kk[:, N_], k_s[0:64, N_], pp[0:64, pb + 5:pb + 6], ALU.mult, [kB_, B_pp], [kkB])
            ksq, ksqB = F64.next()
            act(ksq[:, N_], kk[:, N_], AF.Square, [kkB], [ksqB])
            pj, pjB = PJ.next()
            mm(pj[0:64, N_], ones64, ksq[:, N_], [ksqB, B_cst], [pjB])
            rn, rnB = F64.next()
            act(rn[:, N_], pj[0:64, N_], AF.Sqrt, [pjB], [rnB])
            ts("dve", rn[:, N_], rn[:, N_], 1e-12, ALU.max, [rnB], [rnB])
            recip(rn[:, N_], rn[:, N_], [rnB], [rnB])
            kkn, kknB = F64.next()
            tt("dve", kkn[:, N_], kk[:, N_], rn[:, N_], ALU.mult, [kkB, rnB], [kknB])
            t1, t1B = F64.next()
            ts("dve", t1[:, N_], lr[:, N_], pp[0:64, pb + 6:pb + 7], ALU.mult, [lrB, B_pp, B_ppd], [t1B], s2=ppd[0:64, 2 + h:3 + h], op1=ALU.add)
            kmod, kmB = F64.next()
            tt("dve", kmod[:, N_], k_s[0:64, N_], t1[:, N_], ALU.mult, [kB_, t1B], [kmB])
            bv, bvB = F64.next()
            tt("dve", bv[:, N_], kkn[:, N_], lr[:, N_], ALU.mult, [kknB, lrB], [bvB])
            rk, rkB = F64.next()
            stt("dve", rk[:, N_], r_s[0:64, N_], pp[0:64, pb + 7:pb + 8], kmod[:, N_], ALU.mult, ALU.mult, [rB, kmB, B_pp], [rkB])
            pj, pjB = PJ.next()
            mm(pj[0:64, N_], ones64, rk[:, N_], [rkB, B_cst], [pjB])
            bon, bonB = F64.next()
            tt("dve", bon[:, N_], pj[0:64, N_], v_s[0:64, N_], ALU.mult, [pjB, vB_], [bonB])
            gneg, gnB = F64.next()
            P.add("dve", (lambda o, d0, d1: lambda e: e.tensor_tensor_scan(out=o, data0=d0, data1=d1, initial=0.0, op0=ALU.mult, op1=ALU.add))(
                gneg[:, N_], scanmask[:, N_], e2[:, N_]), [e2B, B_cst], [gnB])
            Ep, EpB = F64.next()
            Em, EmB = F64.next()
            Ea, EaB = F64.next()
            act(Ep[:, N_], gneg[:, N_], AF.Exp, [gnB], [EpB], scale=-1.0)
            act(Em[:, N_], gneg[:, N_], AF.Exp, [gnB], [EmB])
            tt("dve", Ea[:, N_], e2[:, N_], gneg[:, N_], ALU.subtract, [e2B, gnB], [EaB])
            act(Ea[:, N_], Ea[:, N_], AF.Exp, [EaB], [EaB])
            AR, ARB = ARb.next()
            ARv = AR[:, 0:2 * ntok].rearrange("p (c t) -> p c t", t=128)
            stt("dve", ARv[:, :, 0:64], kkn[:, N_].rearrange("p (c t) -> p c t", t=64), -1.0, Ea[:, N_].rearrange("p (c t) -> p c t", t=64),
                ALU.mult, ALU.mult, [kknB, EaB], [ARB])
            tt("dve", ARv[:, :, 64:128], r_s[0:64, N_].rearrange("p (c t) -> p c t", t=64), Ep[:, N_].rearrange("p (c t) -> p c t", t=64), ALU.mult,
               [rB, EpB], [ARB])
            BT, BTB = F64.next()
            KTl, KTlB = F64.next()
            tt("dve", BT[:, N_], bv[:, N_], Em[:, N_], ALU.mult, [bvB, EmB], [BTB])
            tt("dve", KTl[:, N_], kmod[:, N_], Em[:, N_], ALU.mult, [kmB, EmB], [KTlB])
            BH, BHB = F64.next()
            KH, KHB = F64.next()
            for c in range(nch):
                cc = slice(c * 64, (c + 1) * 64)
                gc = Ep[:, c * 64 + 63:c * 64 + 64]
                ts("dve", BH[:, cc], BT[:, cc], gc, ALU.mult, [BTB, EpB], [BHB])
                ts("dve", KH[:, cc], KTl[:, cc], gc, ALU.mult, [KTlB, EpB], [KHB])
            yT, yTB = F64.next()
            if not is_meta:
                ck("B%d" % h)
            CH = range(nch)
            cc = [slice(c * 64, (c + 1) * 64) for c in CH]
            at_c = [AR[:, c * 128:c * 128 + 64] for c in CH]
            rt_c = [AR[:, c * 128 + 64:c * 128 + 128] for c in CH]
            ar_c = [AR[:, c * 128:(c + 1) * 128] for c in CH]
            gcs = [Ep[:, c * 64 + 63:c * 64 + 64] for c in CH]
            yy = [YYP[c][0] for c in CH]
            yyB = [YYP[c][1] for c in CH]
            for c in CH:
                rw, rwB = RW.next()
                tr(rw[0:64, 0:64], BH[:, cc[c]], ident64, [BHB, B_cst], [rwB])
                tr(rw[0:64, 64:128], KH[:, cc[c]], ident64, [KHB, B_cst], [rwB])
                tr(rw[0:64, 128:192], v_s[0:64, cc[c]], ident64, [vB_, B_cst], [rwB])
                tr(rw[0:64, 192:256], at_c[c], ident64, [ARB, B_cst], [rwB])
                tm, tmB = TMP[c]
                cp("act", tm[:, :, :], rw[0:64, 0:192].rearrange("p (a b) -> p a b", b=64), [rwB], [tmB])
                cp("dve", yy[c][0][:, 0:64], rw[0:64, 192:256], [rwB], [yy[c][1]])
                adv()
            for c in CH:
                rw, rwB = RW.next()
                mm(rw[0:64, 0:128], BT[:, cc[c]], ar_c[c], [BTB, ARB], [rwB])
                mm(rw[0:64, 128:256], KTl[:, cc[c]], ar_c[c], [KTlB, ARB], [rwB])
                m12, m12B = M12P[c]
                tt("dve", m12[:, :], rw[0:64, 0:256], suiu2, ALU.mult, [rwB, B_cst], [m12B])
                adv()
            Xs, XTs, XBs = [None] * nch, [None] * nch, [None] * nch
            for c in CH:
                tm, tmB = TMP[c]
                m12, m12B = M12P[c]
                rw, rwB = RW.next()
                mm(rw[0:64, 0:64], at_c[c], BT[:, cc[c]], [ARB, BTB], [rwB])
                mm(rw[0:64, 64:128], m12[:, 128:192], tm[:, 2, :], [m12B, tmB], [rwB])
                xx, xxB = XXP[c][0]
                tt("dve", xx[:, 0:64], rw[0:64, 0:64], slm, ALU.mult, [rwB, B_cst], [xxB])
                cp("act", yy[c][0][:, 64:128], rw[0:64, 64:128], [rwB], [yy[c][1]])
                Xs[c], XTs[c], XBs[c] = xx[:, 0:64], m12[:, 0:64], [xxB, m12B]
                adv()
            cur = [0] * nch
            for m in range(6):
                for c in CH:
                    ya, yaB = YYP[c][cur[c]]
                    yb, ybB = YYP[c][1 - cur[c]]
                    rw, rwB = RW.next()
                    mm(rw[0:64, 0:128], XTs[c], ya[:, :], XBs[c] + [yaB], [rwB])
                    tt("dve", yb[:, :], rw[0:64, 0:128], ya[:, :], ALU.add, [rwB, yaB], [ybB])
                    cur[c] = 1 - cur[c]
                    adv()
                if m < 5:
                    for c in CH:
                        rw, rwB = RW.next()
                        mm(rw[0:64, 0:64], XTs[c], Xs[c], XBs[c], [rwB])
                        mm(rw[0:64, 64:128], Xs[c], XTs[c], XBs[c], [rwB])
                        xn, xnB = XXP[c][(m + 1) % 2]
                        cp("act", xn[:, :], rw[0:64, 0:128], [rwB], [xnB])
                        Xs[c], XTs[c], XBs[c] = xn[:, 0:64], xn[:, 64:128], [xnB]
                        adv()
            for c in CH:
                yf, yfB = YYP[c][cur[c]]
                tm, tmB = TMP[c]
                m12, m12B = M12P[c]
                mts, mtsB = SMP[c][0]
                rhs_, rhsB = SMP[c][1]
                rw, rwB = RW.next()
                mm(rw[0:64, 0:64], yf[:, 0:64], tm[:, 0, :], [yfB, tmB], [rwB])
                mm(rw[0:64, 64:128], yf[:, 0:64], m12[:, 64:128], [yfB, m12B], [rwB])
                stt("dve", mts[:, :], ident64, gcs[c], rw[0:64, 0:64], ALU.mult, ALU.add, [B_cst, EpB, rwB], [mtsB])
                tt("dve", rhs_[:, :], rw[0:64, 64:128], rt_c[c], ALU.add, [rwB, ARB], [rhsB])
                adv()
            for c in CH:
                yf, yfB = YYP[c][cur[c]]
                tm, tmB = TMP[c]
                gs, gsB = SMP[c][2]
                rw, rwB = RW.next()
                mm(rw[0:64, 0:64], tm[:, 0, :], yf[:, 64:128], [tmB, yfB], [rwB], start=True, stop=False)
                mm(rw[0:64, 0:64], tm[:, 1, :], tm[:, 2, :], [tmB], [rwB], start=False, stop=True)
                cp("act", gs[:, :], rw[0:64, 0:64], [rwB], [gsB])
                adv()
            for c in CH:
                yf, yfB = YYP[c][cur[c]]
                tm, tmB = TMP[c]
                m12, m12B = M12P[c]
                mts, mtsB = SMP[c][0]
                rhs_, rhsB = SMP[c][1]
                gs, gsB = SMP[c][2]
                hc, hcB = HS[h][hcur[h]]
                hn, hnB = HS[h][1 - hcur[h]]
                rw, rwB = RW.next()
                mm(rw[0:64, 0:64], yf[:, 64:128], m12[:, 64:128], [yfB, m12B], [rwB], start=True, stop=False)
                mm(rw[0:64, 0:64], tm[:, 2, :], m12[:, 192:256], [tmB, m12B], [rwB], start=False, stop=False)
                mm(rw[0:64, 0:64], hc[:, :], rhs_[:, :], [hcB, rhsB], [rwB], start=False, stop=True)
                rw2, rw2B = RW.next()
                mm(rw2[0:64, 0:64], mts[:, :], hc[:, :], [mtsB, hcB], [rw2B])
                tt("dve", hn[:, :], rw2[0:64, 0:64], gs[:, :], ALU.add, [rw2B, gsB], [hnB])
                cp("act", yT[:, cc[c]], rw[0:64, 0:64], [rwB], [yTB])
                hcur[h] = 1 - hcur[h]
                adv()
            if not is_meta:
                ck("C%d" % h)
            if not is_meta:
                pj, pjB = PJ.next()
                mm(pj[0:64, N_], ones64, yT[:, N_], [yTB, B_cst], [pjB])
                dd, ddB = F64.next()
                stt("dve", dd[:, N_], pj[0:64, N_], -1.0 / 64, yT[:, N_], ALU.mult, ALU.add, [pjB, yTB], [ddB])
                dq, dqB = F64.next()
                act(dq[:, N_], dd[:, N_], AF.Square, [ddB], [dqB])
                pj, pjB = PJ.next()
                mm(pj[0:64, N_], ones64, dq[:, N_], [dqB, B_cst], [pjB])
                act(dq[:, N_], pj[0:64, N_], AF.Sqrt, [pjB], [dqB], bias=GN_EPS, scale=1.0 / 64)
                recip(dq[:, N_], dq[:, N_], [dqB], [dqB])
                tt("pool", dd[:, N_], dd[:, N_], dq[:, N_], ALU.mult, [ddB, dqB], [ddB])
                act(dd[:, N_], dd[:, N_], AF.Identity, [ddB, B_pp], [ddB], bias=pp[0:64, pb + 9:pb + 10], scale=pp[0:64, pb + 8:pb + 9])
                tt("pool", dd[:, N_], dd[:, N_], bon[:, N_], ALU.add, [ddB, bonB], [ddB])
                ot, otB = OTR.next()
                tt("pool", ot[:, N_], dd[:, N_], gT[:, N_], ALU.mult, [ddB, gTB], [otB])
                ck("D%d" % h)
                q, gl = gi // NG1, gi % NG1
                r0 = q * 256 + 128 + 64 * h
                dma("sp", oTl[gl].ap()[r0:r0 + 64, :], ot[:, :], [otB], [B_oTl[gl]], "oTw")

        if att_gen is not None:
            for _ in att_gen:
                pass
            att_epilogue()

    try:
        phase2_group(-1)
        for gi in range(NG2 if stage != 3 else (0 if CUT in (1, 4, 5) else 1)):
            phase2_group(gi)
    except StopBuild:
        P.emit(final_wait_ops=[P.ops[e][-1] for e in ("pe", "act", "dve", "pool")] + [o for o in P.dma_ops if o.dkey in ("win", "rope", "ntg")][-3:])
        st.close()
        return nc
    if stage in (3, 4):
        lastw = [o for o in P.dma_ops if o.dkey == "oTw"]
        lastd = [o for o in P.dma_ops if o.dkey == "dbg"]
        P.emit(final_wait_ops=[lastw[-1]] + lastd[-1:] if lastw else [P.ops[e][-1] for e in ("pe", "act", "dve", "pool")])
        st.close()
        return nc
    for gl in range(NG1):
        allgather(oTl[gl], oTa[gl], B_oTl[gl], B_oTa[gl], "cc2")

    P.fence([B_WG, B_WU, B_WD, B_WO])
    for k in range(8):
        P.add("pool", (lambda k: lambda e: e.dma_start(out=WO[:, k, :].rearrange("p (a b) -> p a b", b=512),
                                                        in_=wout_d[k * 128:(k + 1) * 128, :].rearrange("p (a b) -> p a b", b=512)))(k),
              (), [B_WO], dma="wo")
    load_ffn_weights(1)
    dma("sp", gainA[:], gains_d[2], (), [B_gA], "c1")
    last_out = None
    for g in range(NG1):
        oT, oTB = NT2.next()
        for j in range(8):
            P.add("pool", (lambda j, g, oT: lambda e: e.indirect_dma_start(out=oT[:, j, :], out_offset=None, in_=oTa[g].ap()[:, :],
                                                                           in_offset=bass.IndirectOffsetOnAxis(ap=oidx[:, j:j + 1], axis=0)))(j, g, oT),
                  [B_oTa[g], B_oidx], [oTB], dma="og")
        tiles = []
        outs = []
        xts = []
        for i in range(2):
            xt, xB = XT.next()
            r0 = g * 256 + i * 128
            dma("sp", xt[:], h1s.ap()[r0:r0 + 128, :], [B_h1s[2 * g + i]], [xB], "x")
            xts.append((xt, xB))
        for i in range(2):
            xt, xB = xts[i]
            for half in range(2):
                yb = 2 * i + half
                for k in range(8):
                    mm(PS[yb][:, :], oT[:, k, i * 128:(i + 1) * 128], WO[:, k, half * 512:(half + 1) * 512], [oTB, B_WO], [B_PS[yb]], start=(k == 0), stop=(k == 7))
            ht, hB = HT.next()
            for half in range(2):
                yb = 2 * i + half
                tt("dve", ht[:, half * 512:(half + 1) * 512], PS[yb][:, :], xt[:, half * 512:(half + 1) * 512], ALU.add, [B_PS[yb], xB], [hB])
            tiles.append((ht[:], hB, 128))
            outs.append((xt, xB))
        ffn_group(tiles, gainA, B_gA, outs)
        for i in range(2):
            xt, xB = outs[i]
            r0 = g * 256 + i * 128
            last_out = dma("sp", out_d[r0:r0 + 128, :], xt[:], [xB], (), "out")
    P.emit(final_wait_ops=[last_out])
    st.close()
    return nc


def _consts():
    c = np.zeros((128, NCST), np.float32)
    c[:, 0:128] = np.eye(128)
    c[0:64, 128:192] = 1.0
    c[64:128, 192:256] = 1.0
    rb = np.zeros((128, 128), np.float32)
    for b in (0, 64):
        for m in range(8):
            rb[b + m + 8, b + m] = -1.0
            rb[b + m, b + m + 8] = 1.0
    c[:, 256:384] = rb
    kq = np.arange(128)
    c[:, 384:512] = (kq[:, None] <= kq[None, :]).astype(np.float32)
    j = np.arange(64)
    su = (j[:, None] < j[None, :]).astype(np.float32)
    iu = (j[:, None] <= j[None, :]).astype(np.float32)
    c[0:64, 512:576] = su
    c[0:64, 576:640] = iu
    c[0:64, 640:704] = su
    c[0:64, 704:768] = iu
    c[0:64, 768:832] = su.T
    sm = np.ones(256, np.float32)
    sm[::64] = 0.0
    c[:, 832:1088] = sm[None, :]
    return c


def _rope(LP):
    pos = np.arange(LP, dtype=np.float32)
    inv = (np.float32(500000.0) ** (-np.arange(0, 16, 2, dtype=np.float32) / np.float32(16))).astype(np.float32)
    ang = (pos[:, None] * inv[None, :]).astype(np.float32)
    cos, sin = np.cos(ang).astype(np.float32), np.sin(ang).astype(np.float32)
    C = np.ones((128, 48 + LP), np.float32)
    S = np.zeros((128, 48 + LP), np.float32)
    for b in (0, 64):
        C[b:b + 8, 48:] = cos.T
        C[b + 8:b + 16, 48:] = cos.T
        S[b:b + 8, 48:] = sin.T
        S[b + 8:b + 16, 48:] = sin.T
    return C, S


_CACHE = {}
import os
CUT = int(os.environ.get('CUT', '0'))


def _inmaps(inp):
    f = lambda a: np.ascontiguousarray(np.asarray(a, dtype=np.float32))
    x = f(inp["x"])
    B, T, _ = x.shape
    TQ = T // 4
    NG1 = TQ // 256
    LP = 16 + T
    g = lambda k: f(inp[k])[0]
    cst = _consts()
    ropec, ropes = _rope(LP)
    gains = np.stack([np.broadcast_to(g(k)[None, :], (128, D)) for k in ("ffn1_norm", "mix_norm", "ffn2_norm")]).astype(np.float32).copy()
    w_in = g("w_in")
    w_out = g("w_out")
    mix = g("rw_shift_mix")
    lamv = np.concatenate([np.broadcast_to(g(k)[None, :], (128, 64)) for k in ("da_lambda_q1", "da_lambda_k1", "da_lambda_q2", "da_lambda_k2")], 1).astype(np.float32).copy()
    subln = np.broadcast_to(g("da_subln")[None, :], (128, 128)).astype(np.float32).copy()
    worows = np.concatenate([np.concatenate([np.arange(hd * 128, hd * 128 + 128), 512 + np.arange(hd * 128, hd * 128 + 128)]) for hd in range(4)])
    wout_p = np.ascontiguousarray(w_out[worows, :])
    RWO = 1536
    in_maps = []
    for c in range(8):
        b, q = c // 4, c % 4
        hd = q
        cols = np.concatenate([
            np.arange(hd * 128, hd * 128 + 128), 512 + np.arange(hd * 128, hd * 128 + 128), 1024 + np.arange(hd * 128, hd * 128 + 128),
            RWO + np.arange(hd * 128, hd * 128 + 128), RWO + 512 + np.arange(hd * 128, hd * 128 + 128), RWO + 1024 + np.arange(hd * 128, hd * 128 + 128),
            RWO + 1536 + np.arange(288)])
        win_c = np.ascontiguousarray(w_in[:, cols])
        pp = np.zeros((128, NPP), np.float32)
        pp[:, 0] = np.tile(g("da_q_norm"), 2)
        pp[:, 1] = np.tile(g("da_k_norm"), 2)
        for h in range(2):
            ch = slice(hd * 128 + h * 64, hd * 128 + h * 64 + 64)
            pb = 2 + 10 * h
            pp[0:64, pb + 0] = mix[0:512][ch]
            pp[0:64, pb + 1] = mix[512:1024][ch]
            pp[0:64, pb + 2] = mix[1024:1536][ch]
            pp[0:64, pb + 3] = g("rw_w0")[ch]
            pp[0:64, pb + 4] = g("rw_a0")[ch]
            pp[0:64, pb + 5] = g("rw_k_k")[ch]
            pp[0:64, pb + 6] = g("rw_k_a")[ch]
            pp[0:64, pb + 7] = g("rw_r_k").reshape(-1)[ch]
            pp[0:64, pb + 8] = g("rw_ln_w")[ch]
            pp[0:64, pb + 9] = g("rw_ln_b")[ch]
        pp[0:64, 22] = mix[1536:1600]
        pp[0:64, 23] = mix[1600:1664]
        pp[0:128, 24] = mix[1664:1792]
        pp[0:32, 25] = mix[1792:1824]
        chs = slice(hd * 128, hd * 128 + 128)
        oidx = np.zeros((128, 8), np.int32)
        for hh in range(4):
            for fc in range(2):
                oidx[:, hh * 2 + fc] = hh * 1024 + q * 256 + fc * 128 + np.arange(128)
        in_maps.append({
            "x": np.ascontiguousarray(x[b, q * TQ:(q + 1) * TQ, :]), "meta": f(inp["meta_tokens"]), "gains": gains,
            "wg1": g("ffn1_w_gate"), "wu1": g("ffn1_w_up"), "wd1": g("ffn1_w_down"),
            "wg2": g("ffn2_w_gate"), "wu2": g("ffn2_w_up"), "wd2": g("ffn2_w_down"),
            "win": win_c, "wout": wout_p, "pp": pp,
            "w2": np.ascontiguousarray(g("rw_w2")[:, chs]), "a2": np.ascontiguousarray(g("rw_a2")[:, chs]), "g2w": np.ascontiguousarray(g("rw_g2")[:, chs]),
            "lamv": lamv, "subln": subln, "ropec": ropec, "ropes": ropes, "cst": cst, "oidx": oidx,
        })
    return in_maps, B, T, TQ


def kernel(**inp):
    in_maps, B, T, TQ = _inmaps(inp)
    if T not in _CACHE:
        _CACHE[T] = build(T)
    nc = _CACHE[T]
    res = run_bass_kernel_spmd(nc, in_maps, core_ids=list(range(8)))
    out = np.zeros((B, T, D), np.float32)
    for c in range(8):
        b, q = c // 4, c % 4
        out[b, q * TQ:(q + 1) * TQ, :] = res.results[c]["out"]
    return out
```

```python
import contextlib
import numpy as np
import concourse.bass as bass
import concourse.mybir as mybir
from concourse.bass_utils import run_bass_kernel_spmd

F32 = mybir.dt.float32
BF16 = mybir.dt.bfloat16
I32 = mybir.dt.int32
ALU = mybir.AluOpType
AF = mybir.ActivationFunctionType

D = 1024
FF = 2816
NFF = 22
NPP = 26
NCST = 1088
GN_EPS = 64e-5
EPS = 1e-6


class Buf:
    __slots__ = ("name", "last_w", "readers", "excl")

    def __init__(self, name="", excl=False):
        self.name = name
        self.last_w = None
        self.readers = []
        self.excl = excl


class Op:
    __slots__ = ("eng", "fn", "deps", "signal", "tick", "dkey", "dval", "dinc", "idx")


class Prog:
    ENGS = ("pe", "act", "dve", "pool", "sp")

    def __init__(self, nc):
        self.nc = nc
        self.ops = {e: [] for e in self.ENGS}
        self.n = 0
        self.dcount = {}
        self.dma_ops = []

    def add(self, eng, fn, r=(), w=(), dma=None, dinc=16):
        op = Op()
        op.eng = eng
        op.fn = fn
        op.signal = False
        op.tick = None
        op.dkey = dma
        op.dinc = dinc
        op.dval = None
        op.idx = self.n
        self.n += 1
        snap = self.dcount
        if any(b.excl for b in r):
            w = list(w) + [b for b in r if b.excl]
            r = [b for b in r if not b.excl]
        deps = {}
        for b in r:
            if b.last_w is not None:
                deps[b.last_w.idx] = b.last_w
        for b in w:
            if b.last_w is not None:
                deps[b.last_w.idx] = b.last_w
            for o in b.readers:
                deps[o.idx] = o
        dl = []
        for d in deps.values():
            if d.dkey is None and d.eng == "pe" and eng == "pe" and dma is None:
                continue
            d.signal = True
            dl.append((d, snap[d.dkey] if d.dkey is not None else None))
        op.deps = dl
        if dma is not None:
            self.dcount[dma] = self.dcount.get(dma, 0) + dinc
            op.dval = self.dcount[dma]
            self.dma_ops.append(op)
        for b in r:
            b.readers.append(op)
        for b in w:
            b.last_w = op
            b.readers = []
        self.ops[eng].append(op)
        return op

    def fence(self, bufs):
        lasts = [self.ops[e][-1] for e in self.ENGS if self.ops[e] and self.ops[e][-1].dkey is None]
        last_dma = {}
        for o in self.dma_ops:
            last_dma[o.dkey] = o
        lasts += list(last_dma.values())
        for b in bufs:
            b.readers = list(b.readers) + lasts

    def emit(self, final_wait_ops=()):
        nc = self.nc
        st = contextlib.ExitStack()
        esem = {e: st.enter_context(nc.semaphore("s_" + e)) for e in self.ENGS}
        dsem = {k: st.enter_context(nc.semaphore("d_" + str(k))) for k in self.dcount}
        for d in final_wait_ops:
            d.signal = True
        for e in self.ENGS:
            t = 0
            for op in self.ops[e]:
                if op.dkey is None and op.signal:
                    t += 1
                    op.tick = t
        block = st.enter_context(nc.Block())
        ops = self.ops

        def run(e, eng):
            known = {}
            for op in ops[e]:
                need = {}
                for (d, dv) in op.deps:
                    if d.dkey is not None:
                        s, v = dsem[d.dkey], dv
                    else:
                        s, v = esem[d.eng], d.tick
                    key = id(s)
                    if key not in need or need[key][1] < v:
                        need[key] = (s, v)
                for key, (s, v) in need.items():
                    if known.get(key, 0) >= v:
                        continue
                    eng.wait_ge(s, v)
                    known[key] = v
                inst = op.fn(eng)
                if op.dkey is not None:
                    inst.then_inc(dsem[op.dkey], op.dinc)
                elif op.signal:
                    inst.then_inc(esem[e], 1)
            if e == "sp":
                for d in final_wait_ops:
                    if d.dkey is not None:
                        eng.wait_ge(dsem[d.dkey], d.dval)
                    else:
                        eng.wait_ge(esem[d.eng], d.tick)

        @block.tensor
        def _(eng):
            run("pe", eng)

        @block.scalar
        def _(eng):
            run("act", eng)

        @block.vector
        def _(eng):
            run("dve", eng)

        @block.gpsimd
        def _(eng):
            run("pool", eng)

        @block.sync
        def _(eng):
            run("sp", eng)

        st.close()


class Rot:
    def __init__(self, items):
        self.items = items
        self.i = 0

    def next(self):
        it = self.items[self.i % len(self.items)]
        self.i += 1
        return it


def build(T, stage=9):
    TQ = T // 4
    NG1 = TQ // 256
    NG2 = T // 256
    LP = 16 + T
    NB = T // 128
    nc = bass.Bass("TRN2", target_bir_lowering=False)
    st = contextlib.ExitStack()
    P = Prog(nc)

    def din(name, shape, dt=F32):
        return nc.dram_tensor(name, shape, dt, kind="ExternalInput").ap()

    x_d = din("x", [TQ, D])
    meta_d = din("meta", [16, D])
    gains_d = din("gains", [3, 128, D])
    wg_d = [din("wg1", [D, FF]), din("wg2", [D, FF])]
    wu_d = [din("wu1", [D, FF]), din("wu2", [D, FF])]
    wd_d = [din("wd1", [FF, D]), din("wd2", [FF, D])]
    win_d = din("win", [D, 1056])
    wout_d = din("wout", [D, D])
    pp_d = din("pp", [128, NPP])
    w2_d = din("w2", [64, 128])
    a2_d = din("a2", [64, 128])
    g2_d = din("g2w", [160, 128])
    lam_d = din("lamv", [128, 256])
    subln_d = din("subln", [128, 128])
    ropec_d = din("ropec", [128, 48 + LP])
    ropes_d = din("ropes", [128, 48 + LP])
    cst_d = din("cst", [128, NCST])
    oidx_d = din("oidx", [128, 8], I32)
    out_d = nc.dram_tensor("out", [TQ, D], F32, kind="ExternalOutput").ap()
    dbg = stage < 9
    if stage == 3:
        dbg_d = nc.dram_tensor("dbgo", [128, 8, 128], F32, kind="ExternalOutput").ap()
        dbg_s = nc.dram_tensor("dbgs", [128, 80], F32, kind="ExternalOutput").ap()
    kw_ = dict(kind="ExternalOutput") if dbg else {}
    nTl = [nc.dram_tensor("nTl%d" % g, [D, 256], BF16, **(kw_ if stage == 1 else {})) for g in range(NG1)]
    nTa = [nc.dram_tensor("nTa%d" % g, [4 * D, 256], BF16) for g in range(NG1)]
    oTl = [nc.dram_tensor("oTl%d" % g, [4 * 256, 256], BF16, **(kw_ if stage in (3, 4) else {})) for g in range(NG1)]
    oTa = [nc.dram_tensor("oTa%d" % g, [16 * 256, 256], BF16) for g in range(NG1)]
    h1s = nc.dram_tensor("h1s", [TQ, D], F32, **kw_)
    B_nTl = [Buf("nTl") for _ in range(NG1)]
    B_nTa = [Buf("nTa") for _ in range(NG1)]
    B_oTl = [Buf("oTl") for _ in range(NG1)]
    B_oTa = [Buf("oTa") for _ in range(NG1)]
    RG = [[0, 1, 2, 3], [4, 5, 6, 7]]

    def allgather(src, dst, sB, dB, key):
        return P.add("pool", lambda e: e.collective_compute("AllGather", ALU.bypass, replica_groups=RG,
                                                            ins=[src.ap().bitcast(F32).opt()], outs=[dst.ap().bitcast(F32).opt()]),
                     [sB], [dB], dma=key, dinc=1)
    B_h1s = [Buf("h1s%d" % i) for i in range(TQ // 128)]

    def sb(name, shape, dt=F32):
        return st.enter_context(nc.sbuf_tensor("sb_" + name, shape, dt))

    def mm(out, lhsT, rhs, r, w, start=True, stop=True, skip=False):
        if skip:
            return P.add("pe", lambda e: e.matmul(out, lhsT=lhsT, rhs=rhs, start=start, stop=stop, skip_group_check=True), r, w)
        return P.add("pe", lambda e: e.matmul(out, lhsT=lhsT, rhs=rhs, start=start, stop=stop), r, w)

    def tr(out, in_, ident_ap, r, w):
        return P.add("pe", lambda e: e.transpose(out=out, in_=in_, identity=ident_ap), r, w)

    def act(out, in_, func, r, w, bias=None, scale=None, accum=None):
        kw = {}
        if bias is not None:
            kw["bias"] = bias
        if scale is not None:
            kw["scale"] = scale
        if accum is not None:
            kw["accum_out"] = accum
        return P.add("act", lambda e: e.activation(out=out, in_=in_, func=func, **kw), r, w)

    def cp(eng, out, in_, r, w):
        if eng == "act":
            return P.add("act", lambda e: e.copy(out=out, in_=in_), r, w)
        return P.add(eng, lambda e: e.tensor_copy(out=out, in_=in_), r, w)

    def tt(eng, out, in0, in1, op, r, w):
        return P.add(eng, lambda e: e.tensor_tensor(out=out, in0=in0, in1=in1, op=op), r, w)

    def ts(eng, out, in0, s1, op0, r, w, s2=None, op1=None):
        if s2 is None:
            return P.add(eng, lambda e: e.tensor_scalar(out=out, in0=in0, scalar1=s1, scalar2=None, op0=op0), r, w)
        return P.add(eng, lambda e: e.tensor_scalar(out=out, in0=in0, scalar1=s1, scalar2=s2, op0=op0, op1=op1), r, w)

    def stt(eng, out, in0, scalar, in1, op0, op1, r, w):
        return P.add("dve", lambda e: e.scalar_tensor_tensor(out=out, in0=in0, scalar=scalar, in1=in1, op0=op0, op1=op1), r, w)

    def recip(out, in_, r, w):
        return P.add("dve", lambda e: e.reciprocal(out=out, in_=in_), r, w)

    def memset(eng, ap, val, w):
        return P.add(eng, lambda e: e.memset(ap, val), (), w)

    perids = {}

    def dma(q, out, in_, r, w, key, per=None):
        if per is not None:
            key = key + "_%d" % perids.setdefault(id(per), len(perids))
        return P.add(q, lambda e: e.dma_start(out=out, in_=in_), r, w, dma=key)

    PS = [st.enter_context(nc.psum_tensor("ps%d" % i, [128, 512], F32)) for i in range(8)]
    B_PS = [Buf("ps%d" % i, excl=True) for i in range(8)]

    cst = sb("cst", [128, NCST]); B_cst = Buf("cst")
    ident = cst[:, 0:128]
    onesblk = cst[:, 128:256]
    ones64 = cst[0:64, 128:192]
    rblk = cst[:, 256:384]
    tri_f = cst[:, 384:512]
    suiu2 = cst[0:64, 512:768]
    slm = cst[0:64, 768:832]
    scanmask = cst[0:64, 832:1088]
    ident64 = cst[0:64, 0:64]
    tri_b = sb("tri_b", [128, 128], BF16); B_trib = Buf("trib")
    pp = sb("pp", [128, NPP]); B_pp = Buf("pp")
    ppd = sb("ppd", [128, 8]); B_ppd = Buf("ppd")
    lamv = sb("lamv", [128, 256]); B_lam = Buf("lam")
    lamt = sb("lamt", [128, 72]); B_lamt = Buf("lamt")
    subln = sb("subln", [128, 128]); B_subln = Buf("subln")
    w2s = sb("w2s", [64, 128]); a2s = sb("a2s", [64, 128]); g2a = sb("g2a", [128, 128]); g2b = sb("g2b", [32, 128]); B_lw = Buf("lw")
    gainA = sb("gainA", [128, D]); B_gA = Buf("gA")
    gainM = sb("gainM", [128, D]); B_gM = Buf("gM")
    nTm = sb("nTm", [128, 8, 64], BF16); B_nTm = Buf("nTm")
    nTmA = sb("nTmA", [128, 8, 16], BF16); B_nTmA = Buf("nTmA")
    oidx = sb("oidx", [128, 8], I32); B_oidx = Buf("oidx")
    stat = sb("stat", [128, 32]); statR = Rot([(stat[:, i:i + 1], Buf("st%d" % i)) for i in range(32)])

    dma("sp", cst[:], cst_d, (), [B_cst], "c0")
    dma("sp", pp[:], pp_d, (), [B_pp], "c0")
    dma("sp", lamv[:], lam_d, (), [B_lam], "c0")
    dma("sp", subln[:], subln_d, (), [B_subln], "c0")
    dma("sp", w2s[:], w2_d, (), [B_lw], "c0")
    dma("sp", a2s[:], a2_d, (), [B_lw], "c0")
    dma("sp", g2a[:], g2_d[0:128, :], (), [B_lw], "c0")
    dma("sp", g2b[:], g2_d[128:160, :], (), [B_lw], "c0")
    dma("sp", gainA[:], gains_d[0], (), [B_gA], "c0")
    dma("sp", gainM[:], gains_d[1], (), [B_gM], "c0")
    dma("sp", oidx[:], oidx_d, (), [B_oidx], "c0")
    cp("dve", tri_b[:], tri_f, [B_cst], [B_trib])
    memset("pool", nTm[:], 0.0, [B_nTm])
    for h in range(2):
        ts("dve", ppd[:, h:h + 1], pp[:, 5 + 10 * h:6 + 10 * h], -1.0, ALU.mult, [B_pp], [B_ppd])
        ts("dve", ppd[:, 2 + h:3 + h], pp[:, 8 + 10 * h:9 + 10 * h], -1.0, ALU.mult, [B_pp], [B_ppd], s2=1.0, op1=ALU.add)
    for i in range(2):
        tt("dve", lamt[:, 0:64], lamv[:, 128 * i:128 * i + 64], lamv[:, 128 * i + 64:128 * i + 128], ALU.mult, [B_lam], [B_lamt])
        P.add("dve", (lambda i: lambda e: e.reduce_sum(out=lamt[:, 64 + i:65 + i], in_=lamt[:, 0:64], axis=mybir.AxisListType.X))(i), [B_lamt], [B_lamt])
    act(lamt[:, 66:68], lamt[:, 64:66], AF.Exp, [B_lamt], [B_lamt])
    tt("dve", lamt[:, 68:69], lamt[:, 67:68], lamt[:, 66:67], ALU.subtract, [B_lamt], [B_lamt])
    ts("dve", lamt[:, 70:71], lamt[:, 68:69], -0.2, ALU.add, [B_lamt], [B_lamt])
    neglam = lamt[:, 70:71]
    ts("dve", subln[:], subln[:], 0.8, ALU.mult, [B_subln], [B_subln])

    AR_BYTES = 151552 + 5120
    arena = sb("arena", [128, AR_BYTES // 4])

    class Arena:
        def __init__(self):
            self.off = 0

        def alloc(self, parts, free, dt):
            esz = 4 if dt in (F32, I32) else 2
            n = int(np.prod(free))
            nb = (n * esz + 3) // 4 * 4
            assert self.off + nb <= AR_BYTES, (self.off, nb)
            a = arena[:, self.off // 4:(self.off + nb) // 4]
            self.off += nb
            if dt != F32:
                a = a.bitcast(dt)
            a = a[0:parts, 0:n]
            if len(free) == 2:
                a = a.rearrange("p (a b) -> p a b", b=free[1])
            elif len(free) == 3:
                a = a.rearrange("p (a b c) -> p a b c", b=free[1], c=free[2])
            return a

    A13 = Arena()
    WG = A13.alloc(128, [8, FF], BF16)
    WU = A13.alloc(128, [8, FF], BF16)
    WD = A13.alloc(128, [NFF, D], BF16)
    WO = A13.alloc(128, [8, D], BF16)
    B_WG, B_WU, B_WD, B_WO = Buf("WG"), Buf("WU"), Buf("WD"), Buf("WO")

    def load_ffn_weights(i):
        for k in range(8):
            P.add("pool", (lambda k: lambda e: e.dma_start(out=WG[:, k, :].rearrange("p (a b) -> p a b", b=704),
                                                            in_=wg_d[i][k * 128:(k + 1) * 128, :].rearrange("p (a b) -> p a b", b=704)))(k),
                  (), [B_WG], dma="wg")
        for k in range(8):
            P.add("pool", (lambda k: lambda e: e.dma_start(out=WU[:, k, :].rearrange("p (a b) -> p a b", b=704),
                                                            in_=wu_d[i][k * 128:(k + 1) * 128, :].rearrange("p (a b) -> p a b", b=704)))(k),
                  (), [B_WU], dma="wu")
        for f in range(NFF):
            P.add("pool", (lambda f: lambda e: e.dma_start(out=WD[:, f, :].rearrange("p (a b) -> p a b", b=512),
                                                            in_=wd_d[i][f * 128:(f + 1) * 128, :].rearrange("p (a b) -> p a b", b=512)))(f),
                  (), [B_WD], dma="wd")

    load_ffn_weights(0)

    XT = Rot([(sb("xt%d" % i, [128, D]), Buf("xt%d" % i)) for i in range(3)])
    HT = Rot([(sb("ht%d" % i, [128, D]), Buf("ht%d" % i)) for i in range(2)])
    NF = Rot([(sb("nf%d" % i, [128, D]), Buf("nf%d" % i)) for i in range(1)])
    NT = Rot([(sb("nt%d" % i, [128, 8, 256], BF16), Buf("nt%d" % i)) for i in range(1)])
    NT2 = Rot([(sb("nt2_%d" % i, [128, 8, 256], BF16), Buf("nt2_%d" % i)) for i in range(1)])
    SG = Rot([(sb("sg%d" % i, [128, 256]), Buf("sg%d" % i)) for i in range(2)])
    ACTT = Rot([(sb("actt%d" % i, [128, 256], BF16), Buf("actt%d" % i)) for i in range(2)])

    def norm_T(h, hB, npart, gain, gB, dst, dB, c0):
        nf, nfB = NF.next()
        ss, ssB = statR.next()
        sd, sdB = statR.next()
        memset("pool", ss[0:npart], 0.0, [ssB])
        act(nf[0:npart, :], h, AF.Square, [hB], [nfB, ssB], accum=ss[0:npart])
        act(sd[0:npart], ss[0:npart], AF.Sqrt, [ssB], [sdB], bias=EPS, scale=1.0 / D)
        recip(sd[0:npart], sd[0:npart], [sdB], [sdB])
        stt("dve", nf[0:npart, :], h, sd[0:npart], gain[0:npart, :], ALU.mult, ALU.mult, [hB, sdB, gB], [nfB])
        for b in range(2):
            bank = 6 + b
            for j in range(4):
                k = 4 * b + j
                tr(PS[bank][:, j * 128:j * 128 + npart], nf[0:npart, k * 128:(k + 1) * 128], ident[0:npart, 0:npart], [nfB, B_cst], [B_PS[bank]])
            src = PS[bank][:, :].rearrange("p (a b) -> p a b", b=128)[:, :, 0:npart]
            cp("act" if b == 0 else "dve", dst[:, 4 * b:4 * b + 4, c0:c0 + npart], src, [B_PS[bank]], [dB])

    def ffn_group(tiles, gain, gB, outs):
        nT, nTB = NT.next()
        offs = []
        N = 0
        for (h, hB, npart) in tiles:
            offs.append(N)
            norm_T(h, hB, npart, gain, gB, nT, nTB, N)
            N += npart

        def gu(f):
            bank = 4 + f % 2
            for k in range(8):
                mm(PS[bank][:, 0:N], WG[:, k, f * 128:(f + 1) * 128], nT[:, k, 0:N], [B_WG, nTB], [B_PS[bank]], start=(k == 0), stop=(k == 7))
            for k in range(8):
                mm(PS[bank][:, 256:256 + N], WU[:, k, f * 128:(f + 1) * 128], nT[:, k, 0:N], [B_WU, nTB], [B_PS[bank]], start=(k == 0), stop=(k == 7))

        gu(0)
        for f in range(NFF):
            if f + 1 < NFF:
                gu(f + 1)
            bank = 4 + f % 2
            sg, sgB = SG.next()
            at, atB = ACTT.next()
            act(sg[:, 0:N], PS[bank][:, 0:N], AF.Silu, [B_PS[bank]], [sgB])
            tt("dve", at[:, 0:N], sg[:, 0:N], PS[bank][:, 256:256 + N], ALU.mult, [sgB, B_PS[bank]], [atB])
            for i, (h, hB, npart) in enumerate(tiles):
                for half in range(2):
                    yb = 2 * i + half
                    mm(PS[yb][0:npart, :], at[:, offs[i]:offs[i] + npart], WD[:, f, half * 512:(half + 1) * 512], [atB, B_WD], [B_PS[yb]],
                       start=(f == 0), stop=(f == NFF - 1))
        for i, (h, hB, npart) in enumerate(tiles):
            o, oB = outs[i]
            for half in range(2):
                yb = 2 * i + half
                stt("dve", o[0:npart, half * 512:(half + 1) * 512], PS[yb][0:npart, :], 0.5, h[:, half * 512:(half + 1) * 512], ALU.mult, ALU.add,
                    [B_PS[yb], hB], [oB])

    xm, xmB = XT.next()
    dma("sp", xm[0:16, :], meta_d, (), [xmB], "x", per=xmB)
    hm, hmB = HT.next()
    ffn_group([(xm[0:16, :], xmB, 16)], gainA, B_gA, [(hm, hmB)])
    norm_T(hm[0:16, :], hmB, 16, gainM, B_gM, nTm, B_nTm, 48)
    cp("pool", nTmA[:, :, :], nTm[:, :, 48:64], [B_nTm], [B_nTmA])
    for g in range(NG1):
        tiles = []
        outs = []
        for i in range(2):
            xt, xB = XT.next()
            r0 = g * 256 + i * 128
            dma("sp", xt[:], x_d[r0:r0 + 128, :], (), [xB], "x", per=xB)
            tiles.append((xt[:], xB, 128))
            outs.append(HT.next())
        ffn_group(tiles, gainA, B_gA, outs)
        n2, n2B = NT2.next()
        for i in range(2):
            ht, hB = outs[i]
            r0 = g * 256 + i * 128
            dma("sp", h1s.ap()[r0:r0 + 128, :], ht[:], [hB], [B_h1s[2 * g + i]], "h1w", per=hB)
            norm_T(ht[:], hB, 128, gainM, B_gM, n2, n2B, i * 128)
        dma("sp", nTl[g].ap().rearrange("(k p) t -> p k t", p=128), n2[:], [n2B], [B_nTl[g]], "nTw")
        if stage != 1:
            cc1 = allgather(nTl[g], nTa[g], B_nTl[g], B_nTa[g], "cc1")
    if stage == 1:
        lastw = [o for o in P.dma_ops if o.dkey == "nTw" or o.dkey.startswith("h1w")]
        P.emit(final_wait_ops=[lastw[-1]] + [o for o in lastw if o.dkey.startswith("h1w")][-2:])
        st.close()
        return nc
    if stage == 2:
        P.emit(final_wait_ops=[cc1])
        st.close()
        return nc

    A2 = Arena()
    tenants = []

    def ten(parts, free, dt, name):
        b = Buf(name)
        tenants.append(b)
        return A2.alloc(parts, free, dt), b

    WIN, B_WIN = ten(128, [8, 1056], BF16, "win")
    KT, _ = ten(128, [LP], BF16, "KT")
    B_KT = [Buf("kt%d" % j) for j in range(NB + 1)]
    VX, _ = ten(128, [NB, 129], BF16, "VX")
    B_VX = [Buf("vx%d" % j) for j in range(NB)]
    VM, B_VM = ten(16, [129], BF16, "VM")
    tenants += B_KT + B_VX
    NTG = Rot([ten(128, [8, 256], BF16, "ntg%d" % i) for i in range(2)])
    RC = Rot([ten(128, [256], F32, "rc%d" % i) for i in range(1)])
    RS = Rot([ten(128, [256], F32, "rs%d" % i) for i in range(1)])
    QTB = Rot([ten(128, [256], BF16, "qtb%d" % i) for i in range(2)])
    PTB = Rot([ten(128, [512], BF16, "ptb%d" % i) for i in range(3)])
    F128 = Rot([ten(128, [256], F32, "f128_%d" % i) for i in range(6)])
    OTD = Rot([ten(128, [256], BF16, "otd%d" % i) for i in range(2)])
    ATT = Rot([ten(128, [128], F32, "att%d" % i) for i in range(4)])
    RECS, B_RECS = ten(128, [8], F32, "recs")
    RAW = {}
    for nm, parts in (("r0", 64), ("k0", 64), ("v0", 64), ("r1", 64), ("k1", 64), ("v1", 64), ("wlo", 64), ("alo", 64), ("ga", 128), ("gb", 32)):
        RAW[nm] = ten(parts, [257], F32, "raw_" + nm) + (parts,)
    F64 = Rot([ten(64, [256], F32, "f64_%d" % i) for i in range(37)])
    ALOS = ten(64, [256], F32, "alos")
    WLOS = ten(64, [256], F32, "wlos")
    THB = ten(64, [256], F32, "thb")
    SGA, B_SGA = ten(128, [256], F32, "sga")
    SGB, B_SGB = ten(32, [256], F32, "sgb")
    ARb = Rot([ten(64, [512], F32, "ar%d" % i) for i in range(2)])
    OTR = Rot([ten(64, [256], BF16, "otr%d" % i) for i in range(2)])
    TMP = [ten(64, [3, 64], F32, "tm%d" % i) for i in range(4)]
    M12P = [ten(64, [256], F32, "m12_%d" % i) for i in range(4)]
    XXP = [[ten(64, [128], F32, "xx%d_%d" % (i, j)) for j in range(2)] for i in range(4)]
    YYP = [[ten(64, [128], F32, "yy%d_%d" % (i, j)) for j in range(2)] for i in range(4)]
    SMP = [[ten(64, [64], F32, "sm%d_%d" % (i, j)) for j in range(3)] for i in range(4)]
    HS = [[ten(64, [64], F32, "hs%d_%d" % (h, i)) for i in range(2)] for h in range(2)]

    print("arena phase2 bytes", A2.off, "of", AR_BYTES)
    P.fence(tenants)
    SKIP = int(os.environ.get("SKIP", "0"))
    if not SKIP & 1:
        for k in range(8):
            P.add("pool", (lambda k: lambda e: e.dma_start(out=WIN[:, k, :].rearrange("p (a b) -> p a b", b=528),
                                                            in_=win_d[k * 128:(k + 1) * 128, :].rearrange("p (a b) -> p a b", b=528)))(k),
                  (), [B_WIN], dma="win")
    if not SKIP & 2:
        for h in range(2):
            memset("pool", HS[h][0][0], 0.0, [HS[h][0][1]])
        for nm in RAW:
            memset("pool", RAW[nm][0][:, 0:1], 0.0, [RAW[nm][1]])
    if not SKIP & 4:
        memset("pool", VX[:, :, 128:129], 1.0, B_VX)
    if not SKIP & 8:
        memset("pool", VM[:, 128:129], 1.0, [B_VM])

    PJ = Rot([(PS[4][:, 0:256], B_PS[4]), (PS[5][:, 0:256], B_PS[5]), (PS[6][:, 0:256], B_PS[6]), (PS[7][:, 0:256], B_PS[7])])
    RW = PJ
    STS = [[(PS[c][:, 0:256], B_PS[c]), (PS[c][:, 0:256], B_PS[c])] for c in range(2)]
    hcur = [0, 0]
    blk_count = [0]

    def proj(col0, M, nT, nTB, ntok):
        pj, pjB = PJ.next()
        for k in range(8):
            mm(pj[0:M, 0:ntok], WIN[:, k, col0:col0 + M], nT[:, k, 0:ntok], [B_WIN, nTB], [pjB], start=(k == 0), stop=(k == 7))
        return pj, pjB

    def tokshift(nm, psrc, psB, ntok, mixcol, dst=None):
        raw, rawB, parts = RAW[nm]
        cp("act", raw[:, 1:ntok + 1], psrc[0:parts, 0:ntok], [psB], [rawB])
        d, dB = (F128.next() if parts > 64 else F64.next())
        o, oB = dst if dst is not None else (F128.next() if parts > 64 else F64.next())
        tt("dve", d[0:parts, 0:ntok], raw[:, 0:ntok], raw[:, 1:ntok + 1], ALU.subtract, [rawB], [dB])
        stt("pool", o[0:parts, 0:ntok], d[0:parts, 0:ntok], pp[0:parts, mixcol:mixcol + 1], raw[:, 1:ntok + 1], ALU.mult, ALU.add, [dB, rawB, B_pp], [oB])
        cp("pool", raw[:, 0:1], raw[:, ntok:ntok + 1], [rawB, dB, oB], [rawB])
        return o, oB

    class StopBuild(Exception):
        pass
    ckc = [0]
    CUTN = int(os.environ.get("CUTN", "0"))

    CUTTAG = os.environ.get("CUTTAG", "")

    def ck(tag=None):
        if tag is not None:
            if tag == CUTTAG:
                raise StopBuild()
            return
        ckc[0] += 1
        if ckc[0] == CUTN:
            raise StopBuild()

    def phase2_group(gi):
        is_meta = gi < 0
        if CUT == 5:
            return
        ntok = 64 if is_meta else 256
        nch = ntok // 64
        if is_meta:
            nT, nTB = nTm, B_nTm
            tcol = 0
        else:
            nT, nTB = NTG.next()
            q, gl = gi // NG1, gi % NG1
            src = nTa[gl].ap().rearrange("(q k p) t -> q p k t", q=4, k=8, p=128)[q]
            dma("sp", nT[:], src, [B_nTa[gl]], [nTB], "ntg", per=nTB)
            tcol = 64 + gi * 256
        rc, rcB = RC.next()
        rs, rsB = RS.next()
        dma("sp", rc[:, 0:ntok], ropec_d[:, tcol:tcol + ntok], (), [rcB], "rope", per=rcB)
        dma("sp", rs[:, 0:ntok], ropes_d[:, tcol:tcol + ntok], (), [rsB], "rope", per=rsB)

        qtb = None
        for which in (["k"] if is_meta else ["q", "k"]):
            col0 = 0 if which == "q" else 128
            gcol = 0 if which == "q" else 1
            pj, pjB = proj(col0, 128, nT, nTB, ntok)
            ck()
            sq, sqB = F128.next()
            act(sq[:, 0:ntok], pj[:, 0:ntok], AF.Square, [pjB], [sqB])
            ck()
            p2, p2B = PJ.next()
            mm(p2[:, 0:ntok], onesblk, sq[:, 0:ntok], [sqB, B_cst], [p2B])
            ck()
            rn, rnB = F128.next()
            act(rn[:, 0:ntok], p2[:, 0:ntok], AF.Sqrt, [p2B], [rnB], bias=EPS, scale=1.0 / 64)
            recip(rn[:, 0:ntok], rn[:, 0:ntok], [rnB], [rnB])
            ck()
            qn, qnB = F128.next()
            stt("dve", qn[:, 0:ntok], pj[:, 0:ntok], pp[:, gcol:gcol + 1], rn[:, 0:ntok], ALU.mult, ALU.mult, [pjB, rnB, B_pp], [qnB])
            ck()
            p3, p3B = PJ.next()
            mm(p3[:, 0:ntok], rblk, qn[:, 0:ntok], [qnB, B_cst], [p3B])
            ck()
            t1, t1B = F128.next()
            tt("pool", t1[:, 0:ntok], qn[:, 0:ntok], rc[:, 0:ntok], ALU.mult, [qnB, rcB], [t1B])
            t2, t2B = F128.next()
            tt("dve", t2[:, 0:ntok], p3[:, 0:ntok], rs[:, 0:ntok], ALU.mult, [p3B, rsB], [t2B])
            ck()
            if which == "q":
                qtb, qtbB = QTB.next()
                tt("pool", qtb[:, 0:ntok], t1[:, 0:ntok], t2[:, 0:ntok], ALU.add, [t1B, t2B], [qtbB])
            elif is_meta:
                tt("pool", KT[:, 0:16], t1[:, 48:64], t2[:, 48:64], ALU.add, [t1B, t2B], [B_KT[0]])
            else:
                for i in range(2):
                    p0 = 16 + gi * 256 + i * 128
                    tt("pool", KT[:, p0:p0 + 128], t1[:, i * 128:(i + 1) * 128], t2[:, i * 128:(i + 1) * 128], ALU.add, [t1B, t2B], [B_KT[1 + 2 * gi + i]])
        ck()
        if is_meta:
            if os.environ.get("VARS"):
                PJ.next()
            pj, pjB = PJ.next()
            for k in range(8):
                if os.environ.get("VARM") == "rhs0":
                    mm(pj[0:64, 0:128], nT[:, k, 0:64], WIN[:, k, 0:128], [B_WIN, nTB], [pjB], start=(k == 0), stop=(k == 7))
                elif os.environ.get("VARM") == "swap":
                    mm(pj[0:128, 0:64], WIN[:, k, 256:384], nT[:, k, 0:64], [B_WIN, nTB], [pjB], start=(k == 0), stop=(k == 7))
                elif os.environ.get("VARM") == "64":
                    mm(pj[0:64, 0:128], nT[:, k, 0:64], WIN[:, k, 256:384], [B_WIN, nTB], [pjB], start=(k == 0), stop=(k == 7))
                else:
                    mm(pj[0:16, 0:128], nTmA[:, k, :], WIN[:, k, 256:384], [B_WIN, B_nTmA], [pjB], start=(k == 0), stop=(k == 7))
            ck()
            cp("act", VM[:, 0:128], pj[0:16, 0:128], [pjB], [B_VM])
            ck()
        else:
            for i in range(2):
                pj, pjB = PJ.next()
                for k in range(8):
                    mm(pj[:, 0:128], nT[:, k, i * 128:(i + 1) * 128], WIN[:, k, 256:384], [B_WIN, nTB], [pjB], start=(k == 0), stop=(k == 7))
                cp("act", VX[:, 2 * gi + i, 0:128], pj[:, 0:128], [pjB], [B_VX[2 * gi + i]])

        if not is_meta and CUT != 3:
            blocks = [(-1, 16)] + [(j, 128) for j in range(2 * gi + 2)]
            first = [True, True]
            lastj = [2 * gi, 2 * gi + 1]
            for (j, kb) in blocks:
                sl = blk_count[0] % 2
                blk_count[0] += 1
                kB = B_KT[0] if j < 0 else B_KT[1 + j]
                k0 = 0 if j < 0 else 16 + j * 128
                vap = VM[:, :] if j < 0 else VX[:, j, :]
                vB = B_VM if j < 0 else B_VX[j]
                pt, ptB = PTB.next()
                for c in range(2):
                    s_ap, sB = STS[c][sl]
                    mm(s_ap[0:kb, :], KT[c * 64:(c + 1) * 64, k0:k0 + kb], qtb[c * 64:(c + 1) * 64, :], [kB, qtbB], [sB])
                    act(pt[0:kb, c * 256:(c + 1) * 256], s_ap[0:kb, :], AF.Exp, [sB], [ptB], scale=0.125)
                for it in range(2):
                    if j == lastj[it]:
                        for c in range(2):
                            a = pt[:, c * 256 + it * 128:c * 256 + (it + 1) * 128]
                            tt("pool", a, a, tri_b[:], ALU.mult, [ptB, B_trib], [ptB])
                for c in range(2):
                    for it in range(2):
                        if j > lastj[it]:
                            continue
                        mm(PS[2 + c][:, it * 129:(it + 1) * 129], pt[0:kb, c * 256 + it * 128:c * 256 + (it + 1) * 128], vap[0:kb, :],
                           [ptB, vB], [B_PS[2 + c]], start=first[c], stop=(j == lastj[it]), skip=True)
                        first[c] = False
            otd, otdB = OTD.next()
            for it in range(2):
                for c in range(2):
                    recip(RECS[:, 2 * it + c:2 * it + c + 1], PS[2 + c][:, it * 129 + 128:it * 129 + 129], [B_PS[2 + c]], [B_RECS])
                o1, o1B = ATT.next()
                t2, t2B = ATT.next()
                ts("dve", o1[:], PS[2][:, it * 129:it * 129 + 128], RECS[:, 2 * it:2 * it + 1], ALU.mult, [B_PS[2], B_RECS], [o1B])
                ts("dve", t2[:], PS[3][:, it * 129:it * 129 + 128], RECS[:, 2 * it + 1:2 * it + 2], ALU.mult, [B_PS[3], B_RECS, B_lamt], [t2B],
                   s2=neglam, op1=ALU.mult)
                od, odB = ATT.next()
                tt("pool", od[:], o1[:], t2[:], ALU.add, [o1B, t2B], [odB])
                ss, ssB = statR.next()
                sd, sdB = statR.next()
                memset("pool", ss, 0.0, [ssB])
                act(o1[:], od[:], AF.Square, [odB], [o1B, ssB], accum=ss)
                act(sd, ss, AF.Sqrt, [ssB], [sdB], bias=EPS, scale=1.0 / 128)
                recip(sd, sd, [sdB], [sdB])
                on, onB = ATT.next()
                stt("dve", on[:], od[:], sd, subln[:], ALU.mult, ALU.mult, [odB, sdB, B_subln], [onB])
                if stage == 3 and gi == 0:
                    dma("sp", dbg_d[:, 4 * it + 0, :], o1[:], [o1B], (), "dbg")
                    dma("sp", dbg_d[:, 4 * it + 1, :], t2[:], [t2B], (), "dbg")
                    dma("sp", dbg_d[:, 4 * it + 2, :], od[:], [odB], (), "dbg")
                    dma("sp", dbg_d[:, 4 * it + 3, :], on[:], [onB], (), "dbg")
                    if it == 1:
                        dma("sp", dbg_s[:, 0:8], RECS[:, :], [B_RECS], (), "dbg")
                        dma("sp", dbg_s[:, 8:80], lamt[:, :], [B_lamt], (), "dbg")
                rw, rwB = RW.next()
                tr(rw[:, 0:128], on[:], ident, [onB, B_cst], [rwB])
                cp("act", otd[:, it * 128:(it + 1) * 128], rw[:, 0:128], [rwB], [otdB])
            q, gl = gi // NG1, gi % NG1
            r0 = q * 256
            dma("sp", oTl[gl].ap()[r0:r0 + 128, :], otd[:], [otdB], [B_oTl[gl]], "oTw", per=otdB)

        if CUT == 1 or (CUT == 2 and not is_meta):
            return
        pj, pjB = proj(768, 64, nT, nTB, ntok)
        wlo, wloB = tokshift("wlo", pj, pjB, ntok, 22, dst=WLOS)
        pj, pjB = proj(832, 64, nT, nTB, ntok)
        alo, aloB = tokshift("alo", pj, pjB, ntok, 23, dst=ALOS)
        pj, pjB = proj(896, 128, nT, nTB, ntok)
        ga, gaB = tokshift("ga", pj, pjB, ntok, 24)
        pj, pjB = proj(1024, 32, nT, nTB, ntok)
        gb, gbB = tokshift("gb", pj, pjB, ntok, 25)
        if not is_meta:
            ck("A")
        th, thB = THB
        act(th[:, 0:ntok], wlo[0:64, 0:ntok], AF.Tanh, [wloB], [thB])
        act(SGA[:, 0:ntok], ga[:, 0:ntok], AF.Sigmoid, [gaB], [B_SGA])
        act(SGB[:, 0:ntok], gb[0:32, 0:ntok], AF.Sigmoid, [gbB], [B_SGB])
        for h in range(2):
            pb = 2 + 10 * h
            pj, pjB = proj(384 + 64 * h, 64, nT, nTB, ntok)
            r_s, rB = tokshift("r%d" % h, pj, pjB, ntok, pb + 0)
            pj, pjB = proj(512 + 64 * h, 64, nT, nTB, ntok)
            k_s, kB_ = tokshift("k%d" % h, pj, pjB, ntok, pb + 1)
            pj, pjB = proj(640 + 64 * h, 64, nT, nTB, ntok)
            v_s, vB_ = tokshift("v%d" % h, pj, pjB, ntok, pb + 2)
            N_ = slice(0, ntok)
            pj, pjB = PJ.next()
            mm(pj[0:64, N_], w2s[:, 64 * h:64 * h + 64], th[:, N_], [B_lw, thB], [pjB])
            e1, e1B = F64.next()
            act(e1[:, N_], pj[0:64, N_], AF.Exp, [pjB, B_ppd], [e1B], bias=ppd[0:64, h:h + 1], scale=-1.0)
            act(e1[:, N_], e1[:, N_], AF.Ln, [e1B], [e1B], bias=1.0)
            e2, e2B = F64.next()
            act(e2[:, N_], e1[:, N_], AF.Exp, [e1B], [e2B], bias=-0.5, scale=-1.0)
            pj, pjB = PJ.next()
            mm(pj[0:64, N_], a2s[:, 64 * h:64 * h + 64], alo[0:64, N_], [B_lw, aloB], [pjB])
            lr, lrB = F64.next()
            act(lr[:, N_], pj[0:64, N_], AF.Sigmoid, [pjB, B_pp], [lrB], bias=pp[0:64, pb + 4:pb + 5])
            pj, pjB = PJ.next()
            mm(pj[0:64, N_], g2a[:, 64 * h:64 * h + 64], SGA[:, N_], [B_lw, B_SGA], [pjB], start=True, stop=False)
            mm(pj[0:64, N_], g2b[:, 64 * h:64 * h + 64], SGB[:, N_], [B_lw, B_SGB], [pjB], start=False, stop=True)
            gT, gTB = F64.next()
            cp("act", gT[:, N_], pj[0:64, N_], [pjB], [gTB])
            kk, kkB = F64.next()
            ts("dve", kk[:, N_], k_s[0:64, N_], pp[0:64, pb + 5:pb + 6], ALU.mult, [kB_, B_pp], [kkB])
            ksq, ksqB = F64.next()
            act(ksq[:, N_], kk[:, N_], AF.Square, [kkB], [ksqB])
            pj, pjB = PJ.next()
            mm(pj[0:64, N_], ones64, ksq[:, N_], [ksqB, B_cst], [pjB])
            rn, rnB = F64.next()
            act(rn[:, N_], pj[0:64, N_], AF.Sqrt, [pjB], [rnB])
            ts("dve", rn[:, N_], rn[:, N_], 1e-12, ALU.max, [rnB], [rnB])
            recip(rn[:, N_], rn[:, N_], [rnB], [rnB])
            kkn, kknB = F64.next()
            tt("dve", kkn[:, N_], kk[:, N_], rn[:, N_], ALU.mult, [kkB, rnB], [kknB])
            t1, t1B = F64.next()
            ts("dve", t1[:, N_], lr[:, N_], pp[0:64, pb + 6:pb + 7], ALU.mult, [lrB, B_pp, B_ppd], [t1B], s2=ppd[0:64, 2 + h:3 + h], op1=ALU.add)
            kmod, kmB = F64.next()
            tt("dve", kmod[:, N_], k_s[0:64, N_], t1[:, N_], ALU.mult, [kB_, t1B], [kmB])
            bv, bvB = F64.next()
            tt("dve", bv[:, N_], kkn[:, N_], lr[:, N_], ALU.mult, [kknB, lrB], [bvB])
            rk, rkB = F64.next()
            stt("dve", rk[:, N_], r_s[0:64, N_], pp[0:64, pb + 7:pb + 8], kmod[:, N_], ALU.mult, ALU.mult, [rB, kmB, B_pp], [rkB])
            pj, pjB = PJ.next()
            mm(pj[0:64, N_], ones64, rk[:, N_], [rkB, B_cst], [pjB])
            bon, bonB = F64.next()
            tt("dve", bon[:, N_], pj[0:64, N_], v_s[0:64, N_], ALU.mult, [pjB, vB_], [bonB])
            gneg, gnB = F64.next()
            P.add("dve", (lambda o, d0, d1: lambda e: e.tensor_tensor_scan(out=o, data0=d0, data1=d1, initial=0.0, op0=ALU.mult, op1=ALU.add))(
                gneg[:, N_], scanmask[:, N_], e2[:, N_]), [e2B, B_cst], [gnB])
            Ep, EpB = F64.next()
            Em, EmB = F64.next()
            Ea, EaB = F64.next()
            act(Ep[:, N_], gneg[:, N_], AF.Exp, [gnB], [EpB], scale=-1.0)
            act(Em[:, N_], gneg[:, N_], AF.Exp, [gnB], [EmB])
            tt("dve", Ea[:, N_], e2[:, N_], gneg[:, N_], ALU.subtract, [e2B, gnB], [EaB])
            act(Ea[:, N_], Ea[:, N_], AF.Exp, [EaB], [EaB])
            AR, ARB = ARb.next()
            ARv = AR[:, 0:2 * ntok].rearrange("p (c t) -> p c t", t=128)
            stt("dve", ARv[:, :, 0:64], kkn[:, N_].rearrange("p (c t) -> p c t", t=64), -1.0, Ea[:, N_].rearrange("p (c t) -> p c t", t=64),
                ALU.mult, ALU.mult, [kknB, EaB], [ARB])
            tt("dve", ARv[:, :, 64:128], r_s[0:64, N_].rearrange("p (c t) -> p c t", t=64), Ep[:, N_].rearrange("p (c t) -> p c t", t=64), ALU.mult,
               [rB, EpB], [ARB])
            BT, BTB = F64.next()
            KTl, KTlB = F64.next()
            tt("dve", BT[:, N_], bv[:, N_], Em[:, N_], ALU.mult, [bvB, EmB], [BTB])
            tt("dve", KTl[:, N_], kmod[:, N_], Em[:, N_], ALU.mult, [kmB, EmB], [KTlB])
            BH, BHB = F64.next()
            KH, KHB = F64.next()
            for c in range(nch):
                cc = slice(c * 64, (c + 1) * 64)
                gc = Ep[:, c * 64 + 63:c * 64 + 64]
                ts("dve", BH[:, cc], BT[:, cc], gc, ALU.mult, [BTB, EpB], [BHB])
                ts("dve", KH[:, cc], KTl[:, cc], gc, ALU.mult, [KTlB, EpB], [KHB])
            yT, yTB = F64.next()
            if not is_meta:
                ck("B%d" % h)
            CH = range(nch)
            cc = [slice(c * 64, (c + 1) * 64) for c in CH]
            at_c = [AR[:, c * 128:c * 128 + 64] for c in CH]
            rt_c = [AR[:, c * 128 + 64:c * 128 + 128] for c in CH]
            ar_c = [AR[:, c * 128:(c + 1) * 128] for c in CH]
            gcs = [Ep[:, c * 64 + 63:c * 64 + 64] for c in CH]
            yy = [YYP[c][0] for c in CH]
            yyB = [YYP[c][1] for c in CH]
            for c in CH:
                rw, rwB = RW.next()
                tr(rw[0:64, 0:64], BH[:, cc[c]], ident64, [BHB, B_cst], [rwB])
                tr(rw[0:64, 64:128], KH[:, cc[c]], ident64, [KHB, B_cst], [rwB])
                tr(rw[0:64, 128:192], v_s[0:64, cc[c]], ident64, [vB_, B_cst], [rwB])
                tr(rw[0:64, 192:256], at_c[c], ident64, [ARB, B_cst], [rwB])
                tm, tmB = TMP[c]
                cp("act", tm[:, :, :], rw[0:64, 0:192].rearrange("p (a b) -> p a b", b=64), [rwB], [tmB])
                cp("dve", yy[c][0][:, 0:64], rw[0:64, 192:256], [rwB], [yy[c][1]])
            for c in CH:
                rw, rwB = RW.next()
                mm(rw[0:64, 0:128], BT[:, cc[c]], ar_c[c], [BTB, ARB], [rwB])
                mm(rw[0:64, 128:256], KTl[:, cc[c]], ar_c[c], [KTlB, ARB], [rwB])
                m12, m12B = M12P[c]
                tt("dve", m12[:, :], rw[0:64, 0:256], suiu2, ALU.mult, [rwB, B_cst], [m12B])
            Xs, XTs, XBs = [None] * nch, [None] * nch, [None] * nch
            for c in CH:
                tm, tmB = TMP[c]
                m12, m12B = M12P[c]
                rw, rwB = RW.next()
                mm(rw[0:64, 0:64], at_c[c], BT[:, cc[c]], [ARB, BTB], [rwB])
                mm(rw[0:64, 64:128], m12[:, 128:192], tm[:, 2, :], [m12B, tmB], [rwB])
                xx, xxB = XXP[c][0]
                tt("dve", xx[:, 0:64], rw[0:64, 0:64], slm, ALU.mult, [rwB, B_cst], [xxB])
                cp("act", yy[c][0][:, 64:128], rw[0:64, 64:128], [rwB], [yy[c][1]])
                Xs[c], XTs[c], XBs[c] = xx[:, 0:64], m12[:, 0:64], [xxB, m12B]
            cur = [0] * nch
            for m in range(6):
                for c in CH:
                    ya, yaB = YYP[c][cur[c]]
                    yb, ybB = YYP[c][1 - cur[c]]
                    rw, rwB = RW.next()
                    mm(rw[0:64, 0:128], XTs[c], ya[:, :], XBs[c] + [yaB], [rwB])
                    tt("dve", yb[:, :], rw[0:64, 0:128], ya[:, :], ALU.add, [rwB, yaB], [ybB])
                    cur[c] = 1 - cur[c]
                if m < 5:
                    for c in CH:
                        rw, rwB = RW.next()
                        mm(rw[0:64, 0:64], XTs[c], Xs[c], XBs[c], [rwB])
                        mm(rw[0:64, 64:128], Xs[c], XTs[c], XBs[c], [rwB])
                        xn, xnB = XXP[c][(m + 1) % 2]
                        cp("act", xn[:, :], rw[0:64, 0:128], [rwB], [xnB])
                        Xs[c], XTs[c], XBs[c] = xn[:, 0:64], xn[:, 64:128], [xnB]
            for c in CH:
                yf, yfB = YYP[c][cur[c]]
                tm, tmB = TMP[c]
                m12, m12B = M12P[c]
                mts, mtsB = SMP[c][0]
                rhs_, rhsB = SMP[c][1]
                rw, rwB = RW.next()
                mm(rw[0:64, 0:64], yf[:, 0:64], tm[:, 0, :], [yfB, tmB], [rwB])
                mm(rw[0:64, 64:128], yf[:, 0:64], m12[:, 64:128], [yfB, m12B], [rwB])
                stt("dve", mts[:, :], ident64, gcs[c], rw[0:64, 0:64], ALU.mult, ALU.add, [B_cst, EpB, rwB], [mtsB])
                tt("dve", rhs_[:, :], rw[0:64, 64:128], rt_c[c], ALU.add, [rwB, ARB], [rhsB])
            for c in CH:
                yf, yfB = YYP[c][cur[c]]
                tm, tmB = TMP[c]
                gs, gsB = SMP[c][2]
                rw, rwB = RW.next()
                mm(rw[0:64, 0:64], tm[:, 0, :], yf[:, 64:128], [tmB, yfB], [rwB], start=True, stop=False)
                mm(rw[0:64, 0:64], tm[:, 1, :], tm[:, 2, :], [tmB], [rwB], start=False, stop=True)
                cp("act", gs[:, :], rw[0:64, 0:64], [rwB], [gsB])
            for c in CH:
                yf, yfB = YYP[c][cur[c]]
                tm, tmB = TMP[c]
                m12, m12B = M12P[c]
                mts, mtsB = SMP[c][0]
                rhs_, rhsB = SMP[c][1]
                gs, gsB = SMP[c][2]
                hc, hcB = HS[h][hcur[h]]
                hn, hnB = HS[h][1 - hcur[h]]
                rw, rwB = RW.next()
                mm(rw[0:64, 0:64], yf[:, 64:128], m12[:, 64:128], [yfB, m12B], [rwB], start=True, stop=False)
                mm(rw[0:64, 0:64], tm[:, 2, :], m12[:, 192:256], [tmB, m12B], [rwB], start=False, stop=False)
                mm(rw[0:64, 0:64], hc[:, :], rhs_[:, :], [hcB, rhsB], [rwB], start=False, stop=True)
                rw2, rw2B = RW.next()
                mm(rw2[0:64, 0:64], mts[:, :], hc[:, :], [mtsB, hcB], [rw2B])
                tt("dve", hn[:, :], rw2[0:64, 0:64], gs[:, :], ALU.add, [rw2B, gsB], [hnB])
                cp("act", yT[:, cc[c]], rw[0:64, 0:64], [rwB], [yTB])
                hcur[h] = 1 - hcur[h]
            if not is_meta:
                ck("C%d" % h)
            if not is_meta:
                pj, pjB = PJ.next()
                mm(pj[0:64, N_], ones64, yT[:, N_], [yTB, B_cst], [pjB])
                dd, ddB = F64.next()
                stt("dve", dd[:, N_], pj[0:64, N_], -1.0 / 64, yT[:, N_], ALU.mult, ALU.add, [pjB, yTB], [ddB])
                dq, dqB = F64.next()
                act(dq[:, N_], dd[:, N_], AF.Square, [ddB], [dqB])
                pj, pjB = PJ.next()
                mm(pj[0:64, N_], ones64, dq[:, N_], [dqB, B_cst], [pjB])
                act(dq[:, N_], pj[0:64, N_], AF.Sqrt, [pjB], [dqB], bias=GN_EPS, scale=1.0 / 64)
                recip(dq[:, N_], dq[:, N_], [dqB], [dqB])
                tt("pool", dd[:, N_], dd[:, N_], dq[:, N_], ALU.mult, [ddB, dqB], [ddB])
                act(dd[:, N_], dd[:, N_], AF.Identity, [ddB, B_pp], [ddB], bias=pp[0:64, pb + 9:pb + 10], scale=pp[0:64, pb + 8:pb + 9])
                tt("pool", dd[:, N_], dd[:, N_], bon[:, N_], ALU.add, [ddB, bonB], [ddB])
                ot, otB = OTR.next()
                tt("pool", ot[:, N_], dd[:, N_], gT[:, N_], ALU.mult, [ddB, gTB], [otB])
                ck("D%d" % h)
                q, gl = gi // NG1, gi % NG1
                r0 = q * 256 + 128 + 64 * h
                dma("sp", oTl[gl].ap()[r0:r0 + 64, :], ot[:, :], [otB], [B_oTl[gl]], "oTw", per=otB)

    try:
        phase2_group(-1)
        for gi in range(NG2 if stage != 3 else (0 if CUT in (1, 4, 5) else 1)):
            phase2_group(gi)
    except StopBuild:
        P.emit(final_wait_ops=[P.ops[e][-1] for e in ("pe", "act", "dve", "pool")] + [o for o in P.dma_ops if o.dkey in ("win", "rope", "ntg")][-3:])
        st.close()
        return nc
    if stage in (3, 4):
        lastw = [o for o in P.dma_ops if o.dkey.startswith("oTw")]
        lastd = [o for o in P.dma_ops if o.dkey == "dbg"]
        P.emit(final_wait_ops=[lastw[-1]] + lastd[-1:] if lastw else [P.ops[e][-1] for e in ("pe", "act", "dve", "pool")])
        st.close()
        return nc
    for gl in range(NG1):
        allgather(oTl[gl], oTa[gl], B_oTl[gl], B_oTa[gl], "cc2")

    P.fence([B_WG, B_WU, B_WD, B_WO])
    for k in range(8):
        P.add("pool", (lambda k: lambda e: e.dma_start(out=WO[:, k, :].rearrange("p (a b) -> p a b", b=512),
                                                        in_=wout_d[k * 128:(k + 1) * 128, :].rearrange("p (a b) -> p a b", b=512)))(k),
              (), [B_WO], dma="wo")
    load_ffn_weights(1)
    dma("sp", gainA[:], gains_d[2], (), [B_gA], "c1")
    last_out = None
    for g in range(NG1):
        oT, oTB = NT2.next()
        for j in range(8):
            P.add("pool", (lambda j, g, oT: lambda e: e.indirect_dma_start(out=oT[:, j, :], out_offset=None, in_=oTa[g].ap()[:, :],
                                                                           in_offset=bass.IndirectOffsetOnAxis(ap=oidx[:, j:j + 1], axis=0)))(j, g, oT),
                  [B_oTa[g], B_oidx], [oTB], dma="og")
        tiles = []
        outs = []
        xts = []
        for i in range(2):
            xt, xB = XT.next()
            r0 = g * 256 + i * 128
            dma("sp", xt[:], h1s.ap()[r0:r0 + 128, :], [B_h1s[2 * g + i]], [xB], "x", per=xB)
            xts.append((xt, xB))
        for i in range(2):
            xt, xB = xts[i]
            for half in range(2):
                yb = 2 * i + half
                for k in range(8):
                    mm(PS[yb][:, :], oT[:, k, i * 128:(i + 1) * 128], WO[:, k, half * 512:(half + 1) * 512], [oTB, B_WO], [B_PS[yb]], start=(k == 0), stop=(k == 7))
            ht, hB = HT.next()
            for half in range(2):
                yb = 2 * i + half
                tt("dve", ht[:, half * 512:(half + 1) * 512], PS[yb][:, :], xt[:, half * 512:(half + 1) * 512], ALU.add, [B_PS[yb], xB], [hB])
            tiles.append((ht[:], hB, 128))
            outs.append((xt, xB))
        ffn_group(tiles, gainA, B_gA, outs)
        for i in range(2):
            xt, xB = outs[i]
            r0 = g * 256 + i * 128
            last_out = dma("sp", out_d[r0:r0 + 128, :], xt[:], [xB], (), "out", per=xB)
    lastk = {}
    for o in P.dma_ops:
        if o.dkey.startswith("out"):
            lastk[o.dkey] = o
    P.emit(final_wait_ops=list(lastk.values()))
    st.close()
    return nc


def _consts():
    c = np.zeros((128, NCST), np.float32)
    c[:, 0:128] = np.eye(128)
    c[0:64, 128:192] = 1.0
    c[64:128, 192:256] = 1.0
    rb = np.zeros((128, 128), np.float32)
    for b in (0, 64):
        for m in range(8):
            rb[b + m + 8, b + m] = -1.0
            rb[b + m, b + m + 8] = 1.0
    c[:, 256:384] = rb
    kq = np.arange(128)
    c[:, 384:512] = (kq[:, None] <= kq[None, :]).astype(np.float32)
    j = np.arange(64)
    su = (j[:, None] < j[None, :]).astype(np.float32)
    iu = (j[:, None] <= j[None, :]).astype(np.float32)
    c[0:64, 512:576] = su
    c[0:64, 576:640] = iu
    c[0:64, 640:704] = su
    c[0:64, 704:768] = iu
    c[0:64, 768:832] = su.T
    sm = np.ones(256, np.float32)
    sm[::64] = 0.0
    c[:, 832:1088] = sm[None, :]
    return c


def _rope(LP):
    pos = np.arange(LP, dtype=np.float32)
    inv = (np.float32(500000.0) ** (-np.arange(0, 16, 2, dtype=np.float32) / np.float32(16))).astype(np.float32)
    ang = (pos[:, None] * inv[None, :]).astype(np.float32)
    cos, sin = np.cos(ang).astype(np.float32), np.sin(ang).astype(np.float32)
    C = np.ones((128, 48 + LP), np.float32)
    S = np.zeros((128, 48 + LP), np.float32)
    for b in (0, 64):
        C[b:b + 8, 48:] = cos.T
        C[b + 8:b + 16, 48:] = cos.T
        S[b:b + 8, 48:] = sin.T
        S[b + 8:b + 16, 48:] = sin.T
    return C, S


_CACHE = {}
import os
CUT = int(os.environ.get('CUT', '0'))


def _inmaps(inp):
    f = lambda a: np.ascontiguousarray(np.asarray(a, dtype=np.float32))
    x = f(inp["x"])
    B, T, _ = x.shape
    TQ = T // 4
    NG1 = TQ // 256
    LP = 16 + T
    g = lambda k: f(inp[k])[0]
    cst = _consts()
    ropec, ropes = _rope(LP)
    gains = np.stack([np.broadcast_to(g(k)[None, :], (128, D)) for k in ("ffn1_norm", "mix_norm", "ffn2_norm")]).astype(np.float32).copy()
    w_in = g("w_in")
    w_out = g("w_out")
    mix = g("rw_shift_mix")
    lamv = np.concatenate([np.broadcast_to(g(k)[None, :], (128, 64)) for k in ("da_lambda_q1", "da_lambda_k1", "da_lambda_q2", "da_lambda_k2")], 1).astype(np.float32).copy()
    subln = np.broadcast_to(g("da_subln")[None, :], (128, 128)).astype(np.float32).copy()
    worows = np.concatenate([np.concatenate([np.arange(hd * 128, hd * 128 + 128), 512 + np.arange(hd * 128, hd * 128 + 128)]) for hd in range(4)])
    wout_p = np.ascontiguousarray(w_out[worows, :])
    RWO = 1536
    in_maps = []
    for c in range(8):
        b, q = c // 4, c % 4
        hd = q
        cols = np.concatenate([
            np.arange(hd * 128, hd * 128 + 128), 512 + np.arange(hd * 128, hd * 128 + 128), 1024 + np.arange(hd * 128, hd * 128 + 128),
            RWO + np.arange(hd * 128, hd * 128 + 128), RWO + 512 + np.arange(hd * 128, hd * 128 + 128), RWO + 1024 + np.arange(hd * 128, hd * 128 + 128),
            RWO + 1536 + np.arange(288)])
        win_c = np.ascontiguousarray(w_in[:, cols])
        pp = np.zeros((128, NPP), np.float32)
        pp[:, 0] = np.tile(g("da_q_norm"), 2)
        pp[:, 1] = np.tile(g("da_k_norm"), 2)
        for h in range(2):
            ch = slice(hd * 128 + h * 64, hd * 128 + h * 64 + 64)
            pb = 2 + 10 * h
            pp[0:64, pb + 0] = mix[0:512][ch]
            pp[0:64, pb + 1] = mix[512:1024][ch]
            pp[0:64, pb + 2] = mix[1024:1536][ch]
            pp[0:64, pb + 3] = g("rw_w0")[ch]
            pp[0:64, pb + 4] = g("rw_a0")[ch]
            pp[0:64, pb + 5] = g("rw_k_k")[ch]
            pp[0:64, pb + 6] = g("rw_k_a")[ch]
            pp[0:64, pb + 7] = g("rw_r_k").reshape(-1)[ch]
            pp[0:64, pb + 8] = g("rw_ln_w")[ch]
            pp[0:64, pb + 9] = g("rw_ln_b")[ch]
        pp[0:64, 22] = mix[1536:1600]
        pp[0:64, 23] = mix[1600:1664]
        pp[0:128, 24] = mix[1664:1792]
        pp[0:32, 25] = mix[1792:1824]
        chs = slice(hd * 128, hd * 128 + 128)
        oidx = np.zeros((128, 8), np.int32)
        for hh in range(4):
            for fc in range(2):
                oidx[:, hh * 2 + fc] = hh * 1024 + q * 256 + fc * 128 + np.arange(128)
        in_maps.append({
            "x": np.ascontiguousarray(x[b, q * TQ:(q + 1) * TQ, :]), "meta": f(inp["meta_tokens"]), "gains": gains,
            "wg1": g("ffn1_w_gate"), "wu1": g("ffn1_w_up"), "wd1": g("ffn1_w_down"),
            "wg2": g("ffn2_w_gate"), "wu2": g("ffn2_w_up"), "wd2": g("ffn2_w_down"),
            "win": win_c, "wout": wout_p, "pp": pp,
            "w2": np.ascontiguousarray(g("rw_w2")[:, chs]), "a2": np.ascontiguousarray(g("rw_a2")[:, chs]), "g2w": np.ascontiguousarray(g("rw_g2")[:, chs]),
            "lamv": lamv, "subln": subln, "ropec": ropec, "ropes": ropes, "cst": cst, "oidx": oidx,
        })
    return in_maps, B, T, TQ


def kernel(**inp):
    in_maps, B, T, TQ = _inmaps(inp)
    if T not in _CACHE:
        _CACHE[T] = build(T)
    nc = _CACHE[T]
    res = run_bass_kernel_spmd(nc, in_maps, core_ids=list(range(8)))
    out = np.zeros((B, T, D), np.float32)
    for c in range(8):
        b, q = c // 4, c % 4
        out[b, q * TQ:(q + 1) * TQ, :] = res.results[c]["out"]
    return out
```
